# Optimizing a Trainium2 kernel written in Bass

```python
import math
import jax, jax.numpy as jnp
from jax import lax
import numpy as np

D_MODEL = 1024
BATCH = 8
SEQ = 4096
DEPTH = 1

N_HEADS = 8
HEAD_DIM = 64
N_KV = 2
GQA = N_HEADS // N_KV
CMP_BLOCK = 32
CMP_STRIDE = 16
SLC_BLOCK = 64
N_SLC = 16
WINDOW = 512
Q_BLOCK = 128
SLC_Q_BLOCK = 64
FORCE_SCORE = 1e4
SCALE = 1.0 / math.sqrt(HEAD_DIM)
CONV_WIDTH = D_MODEL
CONV_K = 3
D_FF = 4 * D_MODEL
EPS = 1e-6
NEG_INF = -1e30

QW = N_HEADS * HEAD_DIM
KVW = N_KV * HEAD_DIM
SPLITS = (QW, KVW, KVW, KVW, KVW, KVW, KVW, 3 * N_HEADS,
          CONV_WIDTH, CONV_WIDTH, CONV_WIDTH, D_MODEL, D_MODEL)
PROJ_WIDTH = sum(SPLITS)
SPLIT_POINTS = tuple(int(v) for v in np.cumsum(SPLITS)[:-1])

kernel_name = "hybrid_nsa_shortconv_gated_block"


def rms_norm(x, g):
    xf = x.astype(jnp.float32)
    y = xf * lax.rsqrt(jnp.mean(xf * xf, axis=-1, keepdims=True) + EPS)
    return (y * g.astype(jnp.float32)).astype(x.dtype)


def alibi_slopes():
    s = 2.0 ** (-8.0 * jnp.arange(1, N_HEADS + 1, dtype=jnp.float32) / N_HEADS)
    return s.reshape(N_KV, GQA)


def masked_softmax(s, mask):
    s = jnp.where(mask, s.astype(jnp.float32), NEG_INF)
    m = jnp.max(s, axis=-1, keepdims=True)
    e = jnp.where(mask, jnp.exp(s - m), 0.0)
    return e / jnp.maximum(jnp.sum(e, axis=-1, keepdims=True), 1e-30)


def compress(k, pos_emb, w):
    B_, S_ = k.shape[0], k.shape[1]
    r = CMP_BLOCK // CMP_STRIDE
    c = k.reshape(B_, S_ // CMP_STRIDE, CMP_STRIDE, N_KV, HEAD_DIM)
    nc = S_ // CMP_STRIDE - r + 1
    blocks = jnp.concatenate([c[:, i:i + nc] for i in range(r)], axis=2)
    blocks = blocks + pos_emb[None, None, :, None, :]
    blocks = blocks.transpose(0, 1, 3, 2, 4).reshape(B_, nc, N_KV, CMP_BLOCK * HEAD_DIM)
    return blocks @ w


def hybrid_layer(x, norm1_g, w_in, q_norm_g, k_norm_g, cmp_pos_k, cmp_pos_v,
                 w_cmp_k, w_cmp_v, conv_w, w_branch_a, w_branch_b, w_out,
                 norm2_g, w_up, w_down):
    B_, S_, _ = x.shape
    xn = rms_norm(x, norm1_g)
    proj = xn @ w_in
    (q, k_c, v_c, k_s, v_s, k_w, v_w, g_nsa, conv_b, conv_c, conv_x,
     gate_a, gate_b) = jnp.split(proj, SPLIT_POINTS, axis=-1)

    q = rms_norm(q.reshape(B_, S_, N_HEADS, HEAD_DIM), q_norm_g)
    q = q.reshape(B_, S_, N_KV, GQA, HEAD_DIM)
    kv_shape = (B_, S_, N_KV, HEAD_DIM)
    k_s = rms_norm(k_s.reshape(kv_shape), k_norm_g[1])
    k_w = rms_norm(k_w.reshape(kv_shape), k_norm_g[2])
    v_s = v_s.reshape(kv_shape)
    v_w = v_w.reshape(kv_shape)
    kc = rms_norm(compress(k_c.reshape(kv_shape), cmp_pos_k, w_cmp_k), k_norm_g[0])
    vc = compress(v_c.reshape(kv_shape), cmp_pos_v, w_cmp_v)

    slopes = alibi_slopes()
    t = jnp.arange(S_)

    nc = kc.shape[1]
    c_start = jnp.arange(nc) * CMP_STRIDE
    c_end = c_start + CMP_BLOCK - 1
    dist_c = (t[:, None] - c_end[None, :]).astype(jnp.float32)
    s_c = (jnp.einsum('bsgrd,bcgd->bgrsc', q, kc).astype(jnp.float32) * SCALE
           - slopes[:, :, None, None] * dist_c)
    p_c = masked_softmax(s_c, dist_c >= 0)
    o_cmp = jnp.einsum('bgrsc,bcgd->bsgrd', p_c.astype(vc.dtype), vc)

    ns = S_ // SLC_BLOCK
    s_start = jnp.arange(ns) * SLC_BLOCK
    overlap = jnp.clip(jnp.minimum(c_start[:, None] + CMP_BLOCK, s_start[None, :] + SLC_BLOCK)
                       - jnp.maximum(c_start[:, None], s_start[None, :]), 0, None)
    overlap = overlap.astype(jnp.float32) / CMP_BLOCK
    imp = jnp.einsum('bgrsc,cj->bgsj', p_c, overlap)
    cur = (t // SLC_BLOCK)[:, None]
    j = jnp.arange(ns)[None, :]
    forced = (j == 0) | (j == cur) | (j == cur - 1)
    score = jnp.where(forced, FORCE_SCORE, jnp.where(j <= cur, imp, -1.0))
    n_sel = min(N_SLC, ns)
    _, sel_idx = lax.top_k(score, n_sel)

    kb = k_s.reshape(B_, ns, SLC_BLOCK, N_KV, HEAD_DIM).transpose(0, 3, 1, 2, 4)
    vb = v_s.reshape(B_, ns, SLC_BLOCK, N_KV, HEAD_DIM).transpose(0, 3, 1, 2, 4)
    gather = jax.vmap(jax.vmap(lambda blk, ids: blk[ids]))
    m_sel = n_sel * SLC_BLOCK

    def sel_chunk(i):
        start = i * SLC_Q_BLOCK
        q_i = lax.dynamic_slice_in_dim(q, start, SLC_Q_BLOCK, axis=1)
        idx_i = lax.dynamic_slice_in_dim(sel_idx, start, SLC_Q_BLOCK, axis=2)
        t_i = start + jnp.arange(SLC_Q_BLOCK)
        ks = gather(kb, idx_i).reshape(B_, N_KV, SLC_Q_BLOCK, m_sel, HEAD_DIM)
        vs = gather(vb, idx_i).reshape(B_, N_KV, SLC_Q_BLOCK, m_sel, HEAD_DIM)
        pos = (idx_i[..., None] * SLC_BLOCK + jnp.arange(SLC_BLOCK)).reshape(B_, N_KV, SLC_Q_BLOCK, m_sel)
        dist = (t_i[None, None, :, None] - pos).astype(jnp.float32)
        s = (jnp.einsum('bqgrd,bgqmd->bgrqm', q_i, ks).astype(jnp.float32) * SCALE
             - slopes[None, :, :, None, None] * dist[:, :, None])
        p = masked_softmax(s, (dist >= 0)[:, :, None])
        return jnp.einsum('bgrqm,bgqmd->bqgrd', p.astype(vs.dtype), vs)

    o_slc = lax.map(sel_chunk, jnp.arange(S_ // SLC_Q_BLOCK))
    o_slc = o_slc.transpose(1, 0, 2, 3, 4, 5).reshape(B_, S_, N_KV, GQA, HEAD_DIM)

    kp = jnp.pad(k_w, ((0, 0), (WINDOW, 0), (0, 0), (0, 0)))
    vp = jnp.pad(v_w, ((0, 0), (WINDOW, 0), (0, 0), (0, 0)))
    span = WINDOW + Q_BLOCK

    def win_chunk(i):
        start = i * Q_BLOCK
        q_i = lax.dynamic_slice_in_dim(q, start, Q_BLOCK, axis=1)
        k_i = lax.dynamic_slice_in_dim(kp, start, span, axis=1)
        v_i = lax.dynamic_slice_in_dim(vp, start, span, axis=1)
        t_i = start + jnp.arange(Q_BLOCK)
        s_pos = start - WINDOW + jnp.arange(span)
        dist = t_i[:, None] - s_pos[None, :]
        mask = (dist >= 0) & (dist < WINDOW) & (s_pos[None, :] >= 0)
        s = (jnp.einsum('bqgrd,bkgd->bgrqk', q_i, k_i).astype(jnp.float32) * SCALE
             - slopes[:, :, None, None] * dist.astype(jnp.float32))
        p = masked_softmax(s, mask)
        return jnp.einsum('bgrqk,bkgd->bqgrd', p.astype(v_i.dtype), v_i)

    o_win = lax.map(win_chunk, jnp.arange(S_ // Q_BLOCK))
    o_win = o_win.transpose(1, 0, 2, 3, 4, 5).reshape(B_, S_, N_KV, GQA, HEAD_DIM)

    g = jax.nn.sigmoid(g_nsa.astype(jnp.float32)).astype(x.dtype).reshape(B_, S_, 3, N_KV, GQA, 1)
    o_nsa = (g[:, :, 0] * o_cmp + g[:, :, 1] * o_slc + g[:, :, 2] * o_win).reshape(B_, S_, QW)

    u = conv_c * conv_x
    y = lax.conv_general_dilated(u, conv_w[:, None, :], window_strides=(1,),
                                 padding=[(CONV_K - 1, 0)],
                                 dimension_numbers=('NWC', 'WIO', 'NWC'),
                                 feature_group_count=CONV_WIDTH)
    z = conv_b * y

    mixed = (jax.nn.sigmoid(gate_a) * (o_nsa @ w_branch_a)
             + jax.nn.sigmoid(gate_b) * (z @ w_branch_b))
    x = x + mixed @ w_out

    h = rms_norm(x, norm2_g)
    return x + jnp.square(jax.nn.relu(h @ w_up)) @ w_down


def setup_inputs(seed: int = 0) -> dict:
    key = jax.random.key(seed)
    ks = jax.random.split(key, 16)
    nrm = jax.random.normal
    L = DEPTH
    return {
        "x": nrm(ks[0], (BATCH, SEQ, D_MODEL), jnp.float32),
        "norm1_g": 1.0 + 0.1 * nrm(ks[1], (L, D_MODEL), jnp.float32),
        "w_in": nrm(ks[2], (L, D_MODEL, PROJ_WIDTH), jnp.float32) * D_MODEL ** -0.5,
        "q_norm_g": 1.0 + 0.1 * nrm(ks[3], (L, HEAD_DIM), jnp.float32),
        "k_norm_g": 1.0 + 0.1 * nrm(ks[4], (L, 3, HEAD_DIM), jnp.float32),
        "cmp_pos_k": 0.1 * nrm(ks[5], (L, CMP_BLOCK, HEAD_DIM), jnp.float32),
        "cmp_pos_v": 0.1 * nrm(ks[6], (L, CMP_BLOCK, HEAD_DIM), jnp.float32),
        "w_cmp_k": nrm(ks[7], (L, CMP_BLOCK * HEAD_DIM, HEAD_DIM), jnp.float32) * (CMP_BLOCK * HEAD_DIM) ** -0.5,
        "w_cmp_v": nrm(ks[8], (L, CMP_BLOCK * HEAD_DIM, HEAD_DIM), jnp.float32) * (CMP_BLOCK * HEAD_DIM) ** -0.5,
        "conv_w": nrm(ks[9], (L, CONV_K, CONV_WIDTH), jnp.float32) * CONV_K ** -0.5,
        "w_branch_a": nrm(ks[10], (L, QW, D_MODEL), jnp.float32) * QW ** -0.5,
        "w_branch_b": nrm(ks[11], (L, CONV_WIDTH, D_MODEL), jnp.float32) * CONV_WIDTH ** -0.5,
        "w_out": nrm(ks[12], (L, D_MODEL, D_MODEL), jnp.float32) * D_MODEL ** -0.5,
        "norm2_g": 1.0 + 0.1 * nrm(ks[13], (L, D_MODEL), jnp.float32),
        "w_up": nrm(ks[14], (L, D_MODEL, D_FF), jnp.float32) * D_MODEL ** -0.5,
        "w_down": nrm(ks[15], (L, D_FF, D_MODEL), jnp.float32) * D_FF ** -0.5,
    }


def reference(x, norm1_g, w_in, q_norm_g, k_norm_g, cmp_pos_k, cmp_pos_v,
              w_cmp_k, w_cmp_v, conv_w, w_branch_a, w_branch_b, w_out,
              norm2_g, w_up, w_down):
    for l in range(DEPTH):
        x = hybrid_layer(x, norm1_g[l], w_in[l], q_norm_g[l], k_norm_g[l],
                         cmp_pos_k[l], cmp_pos_v[l], w_cmp_k[l], w_cmp_v[l],
                         conv_w[l], w_branch_a[l], w_branch_b[l], w_out[l],
                         norm2_g[l], w_up[l], w_down[l])
    return x
```

```python
import numpy as np
import concourse.bass as bass
import concourse.mybir as mybir
from concourse.bass_utils import run_bass_kernel_spmd

F32 = mybir.dt.float32
BF16 = mybir.dt.bfloat16
ALU = mybir.AluOpType
ACTF = mybir.ActivationFunctionType
AX = mybir.AxisListType

D = 1024
KC = 8
PW = 6424
DFF = 4096
EPS = 1e-6
NEG = -30000.0
(C_Q, C_KC, C_VC, C_KS, C_VS, C_KW, C_VW, C_GN, C_CB, C_CC, C_CX, C_GA, C_GB) = (
    0, 512, 640, 768, 896, 1024, 1152, 1280, 1304, 2328, 3352, 4376, 5400)
SLOT = 4096
NSLOTS = 3
import os as _os
CASTE = int(_os.environ.get("CASTE", str(1 << 17)))


class Tracker:
    def __init__(self, nc):
        self.nc = nc
        self.eng = {"pe": nc.tensor, "act": nc.scalar, "dve": nc.vector,
                    "pool": nc.gpsimd, "sp": nc.sync}
        self.sem = {}
        self.cnt = {}
        for k in self.eng:
            self.sem[k] = nc.alloc_semaphore("s_" + k)
            self.cnt[k] = 0
        self.seen = {k: {} for k in self.eng}
        self.lastw = {}
        self.readers = {}
        self.ninstr = {k: 0 for k in self.eng}

    def _sem(self, key):
        if key not in self.sem:
            self.sem[key] = self.nc.alloc_semaphore("d%d" % len(self.sem))
            self.cnt[key] = 0
        return self.sem[key]

    def _deps(self, ek, reads, writes):
        deps = {}

        def add(h, same_ok):
            if h is None:
                return
            sk, v = h
            if sk == ek and ek == "pe":
                return
            if deps.get(sk, 0) < v:
                deps[sk] = v

        for r in reads:
            add(self.lastw.get(r), True)
            if isinstance(r, tuple) and r[0] in ("G", "S", "O", "T"):
                for sk, v in self.readers.get(r, {}).items():
                    if sk != ek:
                        add((sk, v), False)
        for w in writes:
            add(self.lastw.get(w), False)
            for sk, v in self.readers.get(w, {}).items():
                add((sk, v), False)
        return deps

    def _emit_waits(self, ek, deps):
        e = self.eng[ek]
        seen = self.seen[ek]
        for sk, v in deps.items():
            if seen.get(sk, 0) >= v:
                continue
            e.wait_ge(self.sem[sk], v)
            seen[sk] = v

    def _record(self, h, reads, writes):
        sk, v = h
        for r in reads:
            d = self.readers.setdefault(r, {})
            if d.get(sk, 0) < v:
                d[sk] = v
        for w in writes:
            self.lastw[w] = h
            self.readers[w] = {}

    def op(self, ek, fn, reads=(), writes=()):
        deps = self._deps(ek, reads, writes)
        self._emit_waits(ek, deps)
        ins = fn(self.eng[ek])
        self.cnt[ek] += 1
        self.ninstr[ek] += 1
        ins.then_inc(self.sem[ek], 1)
        h = (ek, self.cnt[ek])
        self._record(h, reads, writes)
        return h

    def mm(self, fn, reads=(), writes=(), inc=True):
        deps = self._deps("pe", reads, writes)
        self._emit_waits("pe", deps)
        ins = fn(self.eng["pe"])
        self.ninstr["pe"] += 1
        if inc:
            self.cnt["pe"] += 1
            ins.then_inc(self.sem["pe"], 1)
            h = ("pe", self.cnt["pe"])
        else:
            h = ("pe", self.cnt["pe"] + 1)
        self._record(h, reads, writes)
        return h

    def dma(self, qk, out, in_, reads=(), writes=(), semkey=None, **kw):
        deps = self._deps(qk, reads, writes)
        self._emit_waits(qk, deps)
        s = self._sem(semkey)
        ins = self.eng[qk].dma_start(out=out, in_=in_, **kw)
        self.cnt[semkey] += 16
        ins.then_inc(s, 16)
        h = (semkey, self.cnt[semkey])
        self._record(h, reads, writes)
        return h


def make_consts(S):
    NCB = S // 16 - 1
    NCT = (NCB + 127) // 128
    c = {}
    c["c_ident"] = np.eye(128, dtype=np.float32)
    k = np.arange(S)
    c["c_kaug"] = np.stack([(k % 128) - 64, k // 128, np.ones(S)]).astype(np.float32)
    cc = np.arange(NCT * 128)
    pos = 16 * cc + 31
    c["c_kcaug"] = np.stack([(pos % 128) - 64, pos // 128, np.ones_like(pos)]).astype(np.float32)
    NT_ = S // 128
    qa = np.zeros((3, 2, NT_, 4, 128), np.float32)
    for g in range(2):
        for h in range(4):
            sl = 2.0 ** (-(g * 4 + h + 1))
            qa[0, g, :, h, :] = sl
            qa[1, g, :, h, :] = 128.0 * sl
            qa[2, g, :, h, :] = -128.0 * sl * np.arange(NT_)[:, None]
    c["c_qaug"] = qa.reshape(3, 2, NT_ * 512)
    E = np.zeros((128, S), np.float32)
    for j in range(S // 64):
        E[j, 64 * j:64 * j + 64] = 1.0
    c["c_E"] = E
    u = np.arange(17 * 128)
    ccl = np.arange(128)
    c["c_wm"] = np.where(16 * ccl[:, None] + 31 > u[None, :], NEG, 0.0).astype(np.float32)
    kk = np.arange(128)[:, None]
    tt = np.arange(128)[None, :]
    caus = np.zeros((128, 2, 128), np.float32)
    caus[:, 0, :] = np.where(kk > tt, NEG, 0.0)
    caus[:, 1, :] = np.where(kk <= tt, NEG, 0.0)
    c["c_caus"] = caus
    TA = np.zeros((128, 128), np.float32)
    TB = np.zeros((128, 128), np.float32)
    for ttv in range(128):
        cr = 1 if ttv >= 64 else 0
        for mp in range(128):
            mr = mp - 64
            if mr <= cr - 2:
                TA[ttv, mp] = 1.0
            if mr == cr or mr == cr - 1:
                TB[ttv, mp] = 1e4
            elif mr > cr:
                TB[ttv, mp] = -1.0
    c["c_TA"] = TA
    c["c_TB"] = TB
    ov = np.zeros((128, NCT, 64), np.float32)
    for cb in range(NCB):
        for j in range(S // 64):
            o = min(16 * cb + 32, 64 * j + 64) - max(16 * cb, 64 * j)
            if o > 0:
                ov[cb % 128, cb // 128, j] = o / 32.0
    c["c_ov"] = ov
    return c


class _Stop(Exception):
    pass


def build(S, stop=None, dump=()):
    assert S % 512 == 0
    NT = S // 128
    NST = S // 512
    NSB = S // 64
    NSEL = min(16, NSB)
    assert NSEL in (8, 16)
    NCB = S // 16 - 1
    NCT = (NCB + 127) // 128
    NB16 = S // 16

    nc = bass.Bass("TRN2", target_bir_lowering=False)
    T = Tracker(nc)

    def din(name, shape):
        return nc.dram_tensor(name, list(shape), F32, kind="ExternalInput").ap()

    x_d = din("x", [S, D])
    y_d = nc.dram_tensor("y", [S, D], F32, kind="ExternalOutput").ap()
    wck_d = din("w_cmp_k", [2048, 64])
    wcv_d = din("w_cmp_v", [2048, 64])
    g1c_d = din("g1c", [128, 8])
    g2c_d = din("g2c", [128, 8])
    gq_d = din("gq", [64, 1])
    gk_d = din("gk", [64, 3])
    posk_d = din("posk", [128, 16])
    posv_d = din("posv", [128, 16])
    convw_d = din("convw", [128, 8, 3])
    cst = {}
    for name, shape in [("c_ident", [128, 128]), ("c_kaug", [3, S]), ("c_kcaug", [3, NCT * 128]),
                        ("c_qaug", [3, 2, S * 4]), ("c_E", [128, S]), ("c_wm", [128, 17 * 128]),
                        ("c_caus", [128, 2, 128]), ("c_TA", [128, 128]), ("c_TB", [128, 128]),
                        ("c_ov", [128, NCT, 64])]:
        cst[name] = din(name, shape)


    def sb(name, shape, dt):
        return nc.alloc_sbuf_tensor(name, list(shape), dt)

    ksT = sb("ksT", [67, 2, S], BF16)
    kwT = sb("kwT", [67, 2, S], BF16)
    vs = sb("vs", [128, NT, 2, 65], BF16)
    vw = sb("vw", [128, NT, 2, 65], BF16)
    kcr = sb("kcr", [128, 16, 33], BF16)
    vcr = sb("vcr", [128, 16, 33], BF16)
    kcT = sb("kcT", [67, 2, NCT * 128], BF16)
    vca = sb("vca", [128, NCT, 2, 128], BF16)
    cE = sb("cE", [128, S], BF16)
    wm = sb("wm", [128, 17 * 128], BF16)
    caus = sb("caus", [128, 2, 128], BF16)
    ident = sb("ident", [128, 128], BF16)
    TA = sb("TA", [128, 128], F32)
    TB = sb("TB", [128, 128], F32)
    posk = sb("posk_s", [128, 16], BF16)
    posv = sb("posv_s", [128, 16], BF16)
    biasrow = sb("biasrow", [1, 2, 2, 64], BF16)
    onesrow = sb("onesrow", [1, 128], BF16)
    g1c = sb("g1c_s", [128, 8], F32)
    g2c = sb("g2c_s", [128, 8], F32)
    gq = sb("gq_s", [64, 1], F32)
    gk = sb("gk_s", [64, 3], F32)
    convw = sb("convw_s", [128, 8, 3], F32)
    halo = sb("halo", [128, 8, 2], F32)
    neghalf = sb("neghalf", [128, 16], F32)
    selneg = sb("selneg", [128, 2, 128], BF16)
    xt = sb("xt", [128, 4, D], F32)
    xnT = sb("xnT", [128, KC, 512], BF16)
    xn_tok = sb("xn_tok", [128, 2, D], BF16)
    ss = sb("ss", [128, 8], F32)
    rstd = sb("rstd", [128, 8], F32)
    sq = sb("sq", [128, 768], F32)
    ssq = sb("ssq", [128, 16], F32)
    rq = sb("rq", [128, 16], F32)
    qk_tok2 = [sb("qk_tok%d" % i, [128, 768], BF16) for i in range(3)]
    tg = sb("tg", [128, 4, 24], F32)
    arena = sb("arena", [128, 32, 512], BF16)
    actT = arena
    zT = arena[:, 0:8, :]
    mixedT = arena[:, 8:16, :]
    qT = arena[0:67, 16:24, :].rearrange("p c t -> p (c t)").rearrange("p (g j h t) -> p g j h t", g=2, j=4, h=4)
    o_nsaT = arena[:, 24:28, :]
    vc_tok2 = [sb("vc_tok%d" % i, [128, 128], BF16) for i in range(3)]
    vtmp = sb("vtmp", [32, 2, 64], BF16)

    def alias(fc):
        if fc < 8:
            return [("zT", fc)]
        if fc < 16:
            return [("mixedT", fc - 8)]
        if fc < 20:
            return [("qT", 0)]
        if fc < 24:
            return [("qT", 1)]
        if fc < 28:
            return [("o_nsaT", jj) for jj in range(4)]
        return []
    Pt = [sb("P%d" % i, [128, 512], BF16) for i in range(4)]
    Oc2 = [sb("Oc%d" % i, [128, 8, 128], F32) for i in range(2)]
    Ow = sb("Ow", [128, 8, 65], F32)
    Os = sb("Os", [128, 8, 65], F32)
    sm2 = [sb("sm%d" % i, [128, 64], F32) for i in range(2)]
    imp2 = [sb("imp%d" % i, [128, 2, 64], F32) for i in range(2)]
    score2 = [sb("score%d" % i, [128, 2, 64], F32) for i in range(2)]
    work2 = [sb("work%d" % i, [128, 2, 64], F32) for i in range(2)]
    m82 = [sb("m8_%d" % i, [128, 2, 16], F32) for i in range(2)]
    selb2 = [sb("selb%d" % i, [128, 2, 64], BF16) for i in range(2)]
    o_tok = sb("o_tok", [128, 512], BF16)
    kc_tok2 = [sb("kc_tok%d" % i, [128, 128], BF16) for i in range(3)]
    kc_tok = kc_tok2[0]
    cxt = sb("cxt", [128, 512], F32)
    ubuf = sb("ubuf", [128, 514], F32)
    ybuf = sb("ybuf", [128, 512], F32)
    ta = sb("ta", [128, 2, 512], F32)
    otmp = ta
    mt = sb("mt", [128, 2, 512], F32)
    rl = sb("rl", [128, 2, 512], BF16)
    obuf = sb("obuf", [128, 4, 512], F32)
    wslots = [sb("wslot%d" % i, [128, SLOT], BF16) for i in range(NSLOTS)]

    NG = 6
    Gb = [nc.alloc_psum_tensor("G%d" % i, [128, 512], F32) for i in range(NG)]
    Sb = Gb[0:3]
    Ob = Gb[3:5]
    Tb = [nc.alloc_psum_tensor("T%d" % i, [128, 1024], BF16) for i in range(2)]
    rot = {"G": 0, "S": 0, "O": 0, "T": 0, "P": 0, "A": 0, "OB": 0}
    sel_done = {}
    selT_done = {}
    defer_pe = []

    def nxt(kind, n):
        v = rot[kind]
        rot[kind] = (v + 1) % n
        return v

    CK = "const"

    import os
    DBG = os.environ.get("KDBG", "").split(",")
    cl_n = [0]

    def cload(dst, src, key, after=()):
        cl_n[0] += 1
        T.dma("pool", dst, src, reads=list(after), writes=[("cl", cl_n[0])], semkey=CK)

    cload(ident[:, :], cst["c_ident"], "ident")
    cload(g1c[:, :], g1c_d, "g1c")
    cload(g2c[:, :], g2c_d, "g2c")
    cload(gq[:, :], gq_d, "gq")
    cload(gk[:, :], gk_d, "gk")
    cload(convw[:, :, :], convw_d, "convw")
    cload(TA[:, :], cst["c_TA"], "TA")
    cload(TB[:, :], cst["c_TB"], "TB")
    cload(cE[:, :], cst["c_E"], "cE")
    cload(wm[:, :], cst["c_wm"], "wm")
    cload(caus[:, :, :], cst["c_caus"], "caus")
    cload(posk[:, :], posk_d, "posk")
    cload(posv[:, :], posv_d, "posv")
    wflk = wslots[NSLOTS - 1][:, 0:1024].rearrange("p (c e) -> p c e", c=16)
    wflv = wslots[NSLOTS - 1][:, 1024:2048].rearrange("p (c e) -> p c e", c=16)
    WFK = ("ws", NSLOTS - 1)
    T.dma("pool", wflk, wck_d.rearrange("(c p) e -> p c e", p=128), writes=[WFK], semkey=("wsq", NSLOTS - 1))
    T.dma("pool", wflv, wcv_d.rearrange("(c p) e -> p c e", p=128), writes=[WFK], semkey=("wsq", NSLOTS - 1))
    T.op("dve", lambda e: e.memset(kcT[:, :, :], 0.0), writes=["kcT_ms"])
    T.op("dve", lambda e: e.memset(vca[:, :, :, :], 0.0), writes=["vca_ms"])
    T.op("dve", lambda e: e.memset(vs[:, :, :, 64:65], 1.0), writes=["vs1"])
    T.op("dve", lambda e: e.memset(vw[:, :, :, 64:65], 1.0), writes=["vw1"])
    T.op("dve", lambda e: e.memset(selneg[:, :, :], 0.0), writes=["selneg0", "selneg1"])
    T.op("dve", lambda e: e.memset(halo[:, :, :], 0.0), writes=["halo"])
    T.op("dve", lambda e: e.memset(neghalf[:, :], -0.5), writes=["neghalf"])
    T.op("dve", lambda e: e.memset(onesrow[:, :], 1.0), writes=["onesrow"])
    wck_v = wck_d.rearrange("(l d) e -> d l e", d=64)
    wcv_v = wcv_d.rearrange("(l d) e -> d l e", d=64)
    for g in range(2):
        cload(ksT[64:67, g, :], cst["c_kaug"], "ksTaug")
        cload(kwT[64:67, g, :], cst["c_kaug"], "kwTaug")
        cload(kcT[64:67, g, :], cst["c_kcaug"], "kcT", after=["kcT_ms"])
    cload(vca[:, :, 0, 64:128], cst["c_ov"], "vca", after=["vca_ms"])
    cload(vca[:, :, 1, 64:128], cst["c_ov"], "vca", after=["vca_ms"])
    ctot = ("const", T.cnt[CK])
    for key in ["ident", "g1c", "g2c", "gq", "gk", "convw", "TA", "TB", "cE", "wm", "caus", "posk", "posv",
                "ksTaug", "kwTaug", "kcT", "vca"]:
        T.lastw[key] = ctot

    def load_x(st, hh, q="pool"):
        for j in range(4):
            i = 4 * st + j
            T.dma(q, xt[:, j, hh * 512:(hh + 1) * 512], x_d[i * 128:(i + 1) * 128, hh * 512:(hh + 1) * 512],
                  writes=[("xt", j, hh)], semkey=("xt" if q == "pool" else "xts", j))

    load_x(0, 0)
    load_x(0, 1)

    JPS = 39
    NJOBS = JPS * NST
    wjobs_d = din("wjobs", [JPS, 128, SLOT])
    scr_t = nc.dram_tensor("scr_slots", [JPS, 128, SLOT], BF16).ap()
    job_used = [4096, 4096, 2240, 4096, 4096] + [3072] * 8 + [3584] * 8 + [4096] * 18
    assert len(job_used) == JPS
    wstate = {"issued": 0, "next": 0}

    def issue_job(m):
        sidx = m % NSLOTS
        s = wslots[sidx]
        wk_ = ("ws", sidx)
        jidx = m % JPS
        nu = job_used[jidx]
        if m >= JPS:
            T.dma("sp", s[0:64, 0:nu], scr_t[jidx][0:64, 0:nu], reads=[("scr", jidx)], writes=[wk_], semkey=wk_)
            T.dma("act", s[64:128, 0:nu], scr_t[jidx][64:128, 0:nu], reads=[("scr", jidx)], writes=[wk_], semkey=wk_)
            return
        for c0 in range(0, nu, 1024):
            c1 = min(nu, c0 + 1024)
            T.dma("pool", s[:, c0:c1], wjobs_d[jidx][:, c0:c1], writes=[wk_], semkey=("wsq", sidx))
        T.dma("sp", scr_t[jidx][:, 0:nu], s[:, 0:nu], reads=[wk_], writes=[("scr", jidx)], semkey=("scrs", jidx % 4))

    def wnext(hold=0):
        n = wstate["next"]
        wstate["next"] += 1
        while wstate["issued"] < min(NJOBS, n + NSLOTS - hold):
            issue_job(wstate["issued"])
            wstate["issued"] += 1
        return wslots[n % NSLOTS], ("ws", n % NSLOTS)

    T.op("dve", lambda e: e.tensor_scalar(out=gq[:, :], in0=gq[:, :], scalar1=0.125, scalar2=None, op0=ALU.mult),
         reads=["gq"], writes=["gq"])
    for kv, (pp, wf, pk, wk) in enumerate(((posk, wflk, "posk", WFK), (posv, wflv, "posv", WFK))):
        gb = nxt("G", NG)
        for c in range(16):
            T.mm(lambda e, c=c, pp=pp, wf=wf, gb=gb: e.matmul(Gb[gb][0:1, 0:64], lhsT=pp[:, c:c + 1], rhs=wf[:, c, :],
                                                              start=(c == 0), stop=(c == 15)),
                 reads=[pk, wk], writes=[("G", gb)], inc=(c == 15))
        for g in range(2):
            T.op("dve", lambda e, kv=kv, g=g, gb=gb: e.tensor_copy(out=biasrow[0:1, kv, g, :], in_=Gb[gb][0:1, 0:64]),
                 reads=[("G", gb)], writes=["biasrow"])

    def rms_chain(j, stat_col):
        a = nxt("A", 2)
        src_keys = [("xt", j, 0), ("xt", j, 1)]
        T.op("act", lambda e: e.activation(out=rl[:, :, :].rearrange("p a b -> p (a b)"), in_=xt[:, j, :], func=ACTF.Square,
                                           accum_out=ss[:, stat_col:stat_col + 1]),
             reads=src_keys, writes=[("rl", 0), ("rl", 1), ("ss", stat_col)])
        T.op("dve", lambda e: e.tensor_scalar(out=rstd[:, stat_col:stat_col + 1], in0=ss[:, stat_col:stat_col + 1],
                                              scalar1=1.0 / D, scalar2=EPS, op0=ALU.mult, op1=ALU.add),
             reads=[("ss", stat_col)], writes=[("rstd", stat_col)])
        T.op("pool", lambda e: e.tensor_tensor(out=rstd[:, stat_col:stat_col + 1], in0=rstd[:, stat_col:stat_col + 1],
                                               in1=neghalf[:, 0:1], op=ALU.pow),
             reads=[("rstd", stat_col), "neghalf"], writes=[("rstd", stat_col)])
        T.op("dve", lambda e: e.tensor_scalar(out=xn_tok[:, a, :], in0=xt[:, j, :], scalar1=rstd[:, stat_col:stat_col + 1],
                                              scalar2=None, op0=ALU.mult),
             reads=src_keys + [("rstd", stat_col)], writes=[("xn_tok", a)])
        return a

    def rms_tr(a, j, gcol):
        jsl = slice(j * 128, (j + 1) * 128)
        tb = nxt("T", 2)
        for k in range(KC):
            T.mm(lambda e, k=k: e.transpose(Tb[tb][:, k * 128:(k + 1) * 128], xn_tok[:, a, k * 128:(k + 1) * 128], ident[:, :]),
                 reads=[("xn_tok", a), "ident"], writes=[("T", tb)], inc=(k == KC - 1))
        T.op("dve", lambda e: e.tensor_tensor(out=xnT[:, :, jsl],
                                              in0=Tb[tb][:, :].rearrange("p (k t) -> p k t", k=KC),
                                              in1=gcol[:, :].unsqueeze(2).to_broadcast([128, KC, 128]), op=ALU.mult),
             reads=[("T", tb), "g1c", "g2c"], writes=[("xnT", j)])

    def rms_phase(gcol, col0):
        slots = {}
        for j in range(4):
            slots[j] = rms_chain(j, col0 + j)
            if j >= 1:
                rms_tr(slots[j - 1], j - 1, gcol)
        rms_tr(slots[3], 3, gcol)

    def qbc(g, jsl):
        return qT[0:67, g, jsl.start // 128, :, :]

    def ps3(bank):
        return bank[:, :].rearrange("p (h t) -> p h t", h=4)

    def bc4(ap2d):
        return ap2d.unsqueeze(1).to_broadcast([ap2d.shape[0], 4, ap2d.shape[1]])

    pend = []
    LAG = 2

    def emit_pv(blk):
        (p, v_ap, vkeys, ob, first, last, ncols, done) = blk
        for h in range(4):
            T.mm(lambda e, h=h: e.matmul(Ob[ob][:, h * ncols:(h + 1) * ncols], lhsT=Pt[p][:, h * 128:(h + 1) * 128], rhs=v_ap,
                                         start=(first and h == 0), stop=(last and h == 3), skip_group_check=True),
                 reads=[("P", p)] + vkeys, writes=[("G", 3 + ob)], inc=(h == 3))
        if done is not None:
            done()

    def flush_pv():
        while pend:
            emit_pv(pend.pop(0))

    def attn_block(g, jsl, kT_ap, kkeys, masks, v_ap, vkeys, ob, first, last, ncols, done=None):
        sbk = nxt("S", 3)
        nm = len(masks)
        T.mm(lambda e: e.matmul(ps3(Sb[sbk]), lhsT=kT_ap, rhs=qbc(g, jsl), start=True, stop=(nm == 0)),
             reads=kkeys + [("qT", g)], writes=[("G", sbk)], inc=(nm == 0))
        for mi, (ml, mr, mk) in enumerate(masks):
            T.mm(lambda e, ml=ml, mr=mr, mi=mi: e.matmul(ps3(Sb[sbk]), lhsT=ml, rhs=mr, start=False, stop=(mi == nm - 1)),
                 reads=mk, writes=[("G", sbk)], inc=(mi == nm - 1))
        p = nxt("P", 4)
        T.op("act", lambda e: e.activation(out=Pt[p][:, :], in_=Sb[sbk][:, :], func=ACTF.Exp),
             reads=[("G", sbk)], writes=[("P", p)])
        pend.append((p, v_ap, vkeys, ob, first, last, ncols, done))
        while len(pend) > LAG:
            emit_pv(pend.pop(0))

    build.marks = []

    def ckpt(name):
        build.marks.append((name, T.ninstr["pe"]))
        if stop == name:
            raise _Stop()

    def _main_loop():
        for st in range(NST):
            for g in range(2):
                T.dma("pool", arena[64:67, 16 + 4 * g:20 + 4 * g, :], cst["c_qaug"][:, g, st * 2048:(st + 1) * 2048].rearrange("r (c t) -> r c t", c=4),
                      writes=[("qT", g)], semkey=("qaug", g), )

            rms_phase(g1c, 0)
            xnT_keys = [("xnT", j) for j in range(4)]

            ckpt("A")
            def headnorm(bank_ap, bkey, nh, sqc, stc, dstc, par):
                w_ = nh * 64
                T.op("act", lambda e: e.activation(out=sq[:, sqc:sqc + w_], in_=bank_ap, func=ACTF.Square),
                     reads=[bkey], writes=[("sq", sqc)])
                T.op("dve", lambda e: e.tensor_reduce(out=ssq[:, stc:stc + nh], in_=sq[:, sqc:sqc + w_].rearrange("p (h d) -> p h d", d=64),
                                                      axis=AX.X, op=ALU.add),
                     reads=[("sq", sqc)], writes=[("ssq", stc)])
                T.op("dve", lambda e: e.tensor_scalar(out=rq[:, stc:stc + nh], in0=ssq[:, stc:stc + nh], scalar1=1.0 / 64, scalar2=EPS,
                                                      op0=ALU.mult, op1=ALU.add),
                     reads=[("ssq", stc)], writes=[("rq", stc)])
                T.op("pool", lambda e: e.tensor_tensor(out=rq[:, stc:stc + nh], in0=rq[:, stc:stc + nh], in1=neghalf[:, 0:nh], op=ALU.pow),
                     reads=[("rq", stc), "neghalf"], writes=[("rq", stc)])
                T.op("dve", lambda e: e.tensor_tensor(out=qk_tok2[par][:, dstc:dstc + w_].rearrange("p (h d) -> p h d", d=64),
                                                      in0=bank_ap.rearrange("p (h d) -> p h d", d=64),
                                                      in1=rq[:, stc:stc + nh].unsqueeze(2).to_broadcast([128, nh, 64]), op=ALU.mult),
                     reads=[bkey, ("rq", stc)], writes=[("qk_tok", par, dstc)])

            pipeB = []
            PLAG = 2

            def pipe2(p1, p2):
                for j in range(4):
                    k_ = pipeB_n[0]
                    pipeB_n[0] += 1
                    p1(j, k_ % 3)
                    pipeB.append((p2, j, k_ % 3))
                    while len(pipeB) > PLAG:
                        f_, j_, par_ = pipeB.pop(0)
                        f_(j_, par_)
            pipeB_n = [0]

            sA, kA = wnext()
            wA = sA[:, 0:4096].rearrange("p (k c) -> p k c", k=8)

            def a1(j, par):
                jsl = slice(j * 128, (j + 1) * 128)
                gb = nxt("G", NG)
                for k in range(KC):
                    T.mm(lambda e, k=k: e.matmul(Gb[gb][:, :], lhsT=xnT[:, k, jsl], rhs=wA[:, k, :], start=(k == 0), stop=(k == KC - 1)),
                         reads=[("xnT", j), kA], writes=[("G", gb)], inc=(k == KC - 1))
                headnorm(Gb[gb][:, :], ("G", gb), 8, 0, 0, 0, par)

            def a2(j, par):
                jsl = slice(j * 128, (j + 1) * 128)
                tb = nxt("T", 2)
                for h in range(8):
                    T.mm(lambda e, h=h: e.transpose(Tb[tb][0:64, h * 128:(h + 1) * 128], qk_tok2[par][:, h * 64:(h + 1) * 64], ident[:, :]),
                         reads=[("qk_tok", par, 0), "ident"], writes=[("T", tb)], inc=(h == 7))
                T.op("act", lambda e: e.activation(out=qT[0:64, :, j, :, :],
                                                   in_=Tb[tb][0:64, :].rearrange("p (g h t) -> p g h t", g=2, h=4),
                                                   func=ACTF.Copy, scale=gq[:, 0:1]),
                     reads=[("T", tb), "gq"], writes=[("qT", 0), ("qT", 1)])
            pipe2(a1, a2)

            sB, kB = wnext()
            wB = sB[:, 0:4096].rearrange("p (k c) -> p k c", k=8)

            def b1(j, par):
                i = 4 * st + j
                jsl = slice(j * 128, (j + 1) * 128)
                gb = nxt("G", NG)
                for k in range(KC):
                    T.mm(lambda e, k=k: e.matmul(Gb[gb][:, :], lhsT=xnT[:, k, jsl], rhs=wB[:, k, :], start=(k == 0), stop=(k == KC - 1)),
                         reads=[("xnT", j), kB], writes=[("G", gb)], inc=(k == KC - 1))
                T.op("act", lambda e: e.activation(out=kc_tok2[par][:, :], in_=Gb[gb][:, 0:128], func=ACTF.Copy),
                     reads=[("G", gb)], writes=[("kc_tok", par)])
                T.op("act", lambda e: e.activation(out=vc_tok2[par][:, :], in_=Gb[gb][:, 128:256], func=ACTF.Copy),
                     reads=[("G", gb)], writes=[("vc_tok", par)])
                T.op("act", lambda e: e.activation(out=vs[:, i, :, 0:64], in_=Gb[gb][:, 384:512].rearrange("p (g d) -> p g d", g=2),
                                                   func=ACTF.Copy),
                     reads=[("G", gb)], writes=[("vs", i)])
                headnorm(Gb[gb][:, 256:384], ("G", gb), 2, 512, 8, 512, par)

            def b2(j, par):
                i = 4 * st + j
                isl = slice(i * 128, (i + 1) * 128)
                tb2 = nxt("T", 2)
                for h in range(2):
                    T.mm(lambda e, h=h: e.transpose(Tb[tb2][0:64, h * 128:(h + 1) * 128], qk_tok2[par][:, 512 + h * 64:512 + (h + 1) * 64], ident[:, :]),
                         reads=[("qk_tok", par, 512), "ident"], writes=[("T", tb2)], inc=False)
                T.mm(lambda e: e.transpose(Tb[tb2][:, 512:640], kc_tok2[par][:, :], ident[:, :]),
                     reads=[("kc_tok", par), "ident"], writes=[("T", tb2)], inc=False)
                T.mm(lambda e: e.transpose(Tb[tb2][:, 640:768], vc_tok2[par][:, :], ident[:, :]),
                     reads=[("vc_tok", par), "ident"], writes=[("T", tb2)], inc=True)
                T.op("act", lambda e: e.activation(out=ksT[0:64, :, isl], in_=Tb[tb2][0:64, 0:256].rearrange("p (g t) -> p g t", g=2),
                                                   func=ACTF.Copy, scale=gk[:, 1:2]),
                     reads=[("T", tb2), "gk"], writes=[("ksT", i)])
                T.op("dve", lambda e: e.tensor_copy(out=kcr[:, :, 8 * j + 1:8 * j + 9].rearrange("p r b -> p b r"),
                                                    in_=Tb[tb2][:, 512:640].rearrange("p (b r) -> p b r", r=16)),
                     reads=[("T", tb2)], writes=["kcr"])
                T.op("dve", lambda e: e.tensor_copy(out=vcr[:, :, 8 * j + 1:8 * j + 9].rearrange("p r b -> p b r"),
                                                    in_=Tb[tb2][:, 640:768].rearrange("p (b r) -> p b r", r=16)),
                     reads=[("T", tb2)], writes=["vcr"])
            pipe2(b1, b2)

            sC, kC_ = wnext()
            wC = sC[:, 0:8 * 280].rearrange("p (k c) -> p k c", k=8)

            def c1(j, par):
                i = 4 * st + j
                jsl = slice(j * 128, (j + 1) * 128)
                gb = nxt("G", NG)
                for k in range(KC):
                    T.mm(lambda e, k=k: e.matmul(Gb[gb][:, 0:280], lhsT=xnT[:, k, jsl], rhs=wC[:, k, :], start=(k == 0), stop=(k == KC - 1)),
                         reads=[("xnT", j), kC_], writes=[("G", gb)], inc=(k == KC - 1))
                T.op("act", lambda e: e.activation(out=vw[:, i, :, 0:64], in_=Gb[gb][:, 128:256].rearrange("p (g d) -> p g d", g=2),
                                                   func=ACTF.Copy),
                     reads=[("G", gb)], writes=[("vw", i)])
                T.op("act", lambda e: e.activation(out=tg[:, j, :], in_=Gb[gb][:, 256:280], func=ACTF.Tanh, scale=0.5),
                     reads=[("G", gb)], writes=[("tg", j)])
                headnorm(Gb[gb][:, 0:128], ("G", gb), 2, 640, 10, 640, par)

            def c2(j, par):
                i = 4 * st + j
                isl = slice(i * 128, (i + 1) * 128)
                tb3 = nxt("T", 2)
                for h in range(2):
                    T.mm(lambda e, h=h: e.transpose(Tb[tb3][0:64, h * 128:(h + 1) * 128], qk_tok2[par][:, 640 + h * 64:640 + (h + 1) * 64], ident[:, :]),
                         reads=[("qk_tok", par, 640), "ident"], writes=[("T", tb3)], inc=(h == 1))
                T.op("act", lambda e: e.activation(out=kwT[0:64, :, isl], in_=Tb[tb3][0:64, 0:256].rearrange("p (g t) -> p g t", g=2),
                                                   func=ACTF.Copy, scale=gk[:, 2:3]),
                     reads=[("T", tb3), "gk"], writes=[("kwT", i)])
            pipe2(c1, c2)
            while pipeB:
                f_, j_, par_ = pipeB.pop(0)
                f_(j_, par_)

            ckpt("B")
            c0 = max(0, 32 * st - 1)
            c1 = 32 * st + 30
            n = c1 - c0 + 1
            bl0 = c0 - 32 * st + 1
            for kv, (src, srck) in enumerate(((kcr, "kcr"), (vcr, "vcr"))):
                s_, sk = wnext()
                wbd = s_[:, 0:4096].rearrange("p (l e) -> p l e", l=32)
                gb = nxt("G", NG)
                for l in range(32):
                    T.mm(lambda e, l=l: e.matmul(Gb[gb][0:n, 0:128], lhsT=src[:, l % 16, bl0 + l // 16:bl0 + l // 16 + n],
                                                 rhs=wbd[:, l, :], start=(l == 0), stop=False),
                         reads=[srck, sk], writes=[("G", gb)], inc=False)
                T.mm(lambda e: e.matmul(Gb[gb][0:n, 0:128], lhsT=onesrow[0:1, 0:n],
                                        rhs=biasrow[0:1, kv, :, :].rearrange("p g e -> p (g e)"), start=False, stop=True),
                     reads=["onesrow", "biasrow"], writes=[("G", gb)], inc=True)
                if kv == 1:
                    T.op("act", lambda e: e.activation(out=vtmp[0:n, :, :], in_=Gb[gb][0:n, 0:128].rearrange("p (g d) -> p g d", g=2),
                                                       func=ACTF.Copy),
                         reads=[("G", gb)], writes=["vtmp"])
                    for ct in range(NCT):
                        lo = max(c0, 128 * ct)
                        hi = min(c1, 128 * ct + 127)
                        if lo > hi:
                            continue
                        T.dma("pool", vca[lo - 128 * ct:hi - 128 * ct + 1, ct, :, 0:64], vtmp[lo - c0:hi - c0 + 1, :, :],
                              reads=["vtmp"], writes=["vca"], semkey="vcadma")
                else:
                    T.op("act", lambda e: e.activation(out=sq[0:n, 0:128], in_=Gb[gb][0:n, 0:128], func=ACTF.Square),
                         reads=[("G", gb)], writes=["sq_q"])
                    T.op("dve", lambda e: e.tensor_reduce(out=ssq[0:n, 12:14], in_=sq[0:n, 0:128].rearrange("p (h d) -> p h d", d=64),
                                                          axis=AX.X, op=ALU.add),
                         reads=["sq_q"], writes=["ssqc"])
                    T.op("dve", lambda e: e.tensor_scalar(out=rq[0:n, 12:14], in0=ssq[0:n, 12:14], scalar1=1.0 / 64, scalar2=EPS,
                                                          op0=ALU.mult, op1=ALU.add),
                         reads=["ssqc"], writes=["rqc"])
                    T.op("pool", lambda e: e.tensor_tensor(out=rq[0:n, 12:14], in0=rq[0:n, 12:14], in1=neghalf[0:n, 0:2], op=ALU.pow),
                         reads=["rqc", "neghalf"], writes=["rqc"])
                    T.op("dve", lambda e: e.tensor_tensor(out=kc_tok[0:n, :].rearrange("p (h d) -> p h d", d=64),
                                                          in0=Gb[gb][0:n, 0:128].rearrange("p (h d) -> p h d", d=64),
                                                          in1=rq[0:n, 12:14].unsqueeze(2).to_broadcast([n, 2, 64]), op=ALU.mult),
                         reads=[("G", gb), "rqc"], writes=[("kc_tok", 0)])
                    tb = nxt("T", 2)
                    for g in range(2):
                        T.mm(lambda e, g=g: e.transpose(Tb[tb][0:64, g * 128:g * 128 + n], kc_tok[0:n, g * 64:(g + 1) * 64], ident[0:n, 0:n]),
                             reads=[("kc_tok", 0), "ident"], writes=[("T", tb)], inc=(g == 1))
                    T.op("act", lambda e: e.activation(out=kcT[0:64, :, c0:c0 + n],
                                                       in_=Tb[tb][0:64, 0:256].rearrange("p (g t) -> p g t", g=2)[:, :, 0:n],
                                                       func=ACTF.Copy, scale=gk[:, 0:1]),
                         reads=[("T", tb), "gk"], writes=["kcT"])
            T.op("dve", lambda e: e.tensor_copy(out=kcr[:, :, 0:1], in_=kcr[:, :, 32:33]), reads=["kcr"], writes=["kcr"])
            T.op("dve", lambda e: e.tensor_copy(out=vcr[:, :, 0:1], in_=vcr[:, :, 32:33]), reads=["vcr"], writes=["vcr"])

            ckpt("C")
            def cmp_done(j, g, ob):
                i = 4 * st + j
                pr = j % 2
                Oc, sm, imp, score, work, m8, selb = Oc2[pr], sm2[pr], imp2[pr], score2[pr], work2[pr], m82[pr], selb2[pr]
                tsl = slice(64 - 2 * i, 128 - 2 * i)
                T.op("act", lambda e: e.activation(out=Oc[:, g * 4:(g + 1) * 4, :], in_=Ob[ob][:, :].rearrange("p (h c) -> p h c", h=4),
                                                   func=ACTF.Copy),
                     reads=[("G", 3 + ob)], writes=[("Oc", pr, g)])
                T.op("dve", lambda e: e.tensor_reduce(out=sm[:, g * 4:(g + 1) * 4], in_=Oc[:, g * 4:(g + 1) * 4, 64:128],
                                                      axis=AX.X, op=ALU.add),
                     reads=[("Oc", pr, g)], writes=[("denc", pr, g)])
                T.op("dve", lambda e: e.tensor_scalar(out=sm[:, g * 4:(g + 1) * 4], in0=sm[:, g * 4:(g + 1) * 4], scalar1=1e-30,
                                                      scalar2=None, op0=ALU.max),
                     reads=[("denc", pr, g)], writes=[("denc", pr, g)])
                T.op("dve", lambda e: e.reciprocal(out=sm[:, 8 + g * 4:8 + (g + 1) * 4], in_=sm[:, g * 4:(g + 1) * 4]),
                     reads=[("denc", pr, g)], writes=[("rc", pr, g)])
                for h in range(4):
                    hh = g * 4 + h
                    if h == 0:
                        T.op("dve", lambda e: e.tensor_scalar(out=imp[:, g, :], in0=Oc[:, hh, 64:128], scalar1=sm[:, 8 + hh:9 + hh],
                                                              scalar2=None, op0=ALU.mult),
                             reads=[("Oc", pr, g), ("rc", pr, g)], writes=[("imp", pr, g)])
                    else:
                        T.op("dve", lambda e: e.scalar_tensor_tensor(out=imp[:, g, :], in0=Oc[:, hh, 64:128], scalar=sm[:, 8 + hh:9 + hh],
                                                                     in1=imp[:, g, :], op0=ALU.mult, op1=ALU.add),
                             reads=[("Oc", pr, g), ("rc", pr, g), ("imp", pr, g)], writes=[("imp", pr, g)])
                T.op("dve", lambda e: e.tensor_tensor(out=score[:, g, :], in0=imp[:, g, :], in1=TA[:, tsl], op=ALU.mult),
                     reads=[("imp", pr, g), "TA"], writes=[("score", pr, g)])
                T.op("dve", lambda e: e.tensor_tensor(out=score[:, g, :], in0=score[:, g, :], in1=TB[:, tsl], op=ALU.add),
                     reads=[("score", pr, g), "TB"], writes=[("score", pr, g)])
                T.op("dve", lambda e: e.memset(score[:, g, 0:1], 1e4), reads=[("score", pr, g)], writes=[("score", pr, g)])
                T.op("dve", lambda e: e.max(out=m8[:, g, 0:8], in_=score[:, g, :]), reads=[("score", pr, g)], writes=[("m8", pr, g)])
                if NSEL == 16:
                    T.op("dve", lambda e: e.match_replace(out=work[:, g, :], in_to_replace=m8[:, g, 0:8], in_values=score[:, g, :],
                                                          imm_value=-2.0),
                         reads=[("score", pr, g), ("m8", pr, g)], writes=[("work", pr, g)])
                    T.op("dve", lambda e: e.max(out=m8[:, g, 8:16], in_=work[:, g, :]), reads=[("work", pr, g)], writes=[("m8b", pr, g)])
                    thr_ap, thrk = m8[:, g, 15:16], "m8b"
                else:
                    thr_ap, thrk = m8[:, g, 7:8], "m8"
                T.op("dve", lambda e: e.tensor_scalar(out=selb[:, g, :], in0=score[:, g, :], scalar1=thr_ap, scalar2=None, op0=ALU.is_ge),
                     reads=[("score", pr, g), (thrk, pr, g)], writes=[("selb", pr, g)])
                sel_done[(i, g)] = True

            def sel_transpose(j, g):
                pr = j % 2
                tb = nxt("T", 2)
                T.mm(lambda e: e.transpose(Tb[tb][0:64, 0:128], selb2[pr][:, g, :], ident[:, :]),
                     reads=[("selb", pr, g), "ident"], writes=[("T", tb)])
                T.op("dve", lambda e: e.tensor_scalar(out=selneg[0:64, g, :], in0=Tb[tb][0:64, 0:128], scalar1=-NEG, scalar2=NEG,
                                                      op0=ALU.mult, op1=ALU.add),
                     reads=[("T", tb)], writes=["selneg%d" % g])

            def win_done(j, g, ob):
                T.op("act", lambda e: e.activation(out=Ow[:, g * 4:(g + 1) * 4, :],
                                                   in_=Ob[ob][:, 0:260].rearrange("p (h c) -> p h c", h=4), func=ACTF.Copy),
                     reads=[("G", 3 + ob)], writes=[("Ow", g)])

            def slc_done(j, g, ob):
                pr = j % 2
                Oc, sm = Oc2[pr], sm2[pr]
                jsl = slice(j * 128, (j + 1) * 128)
                T.op("act", lambda e: e.activation(out=Os[:, g * 4:(g + 1) * 4, :],
                                                   in_=Ob[ob][:, 0:260].rearrange("p (h c) -> p h c", h=4), func=ACTF.Copy),
                     reads=[("G", 3 + ob)], writes=[("Os", g)])
                if g == 0:
                    return
                T.op("dve", lambda e: e.tensor_scalar(out=sm[:, 16:40], in0=tg[:, j, :], scalar1=1.0, scalar2=0.5, op0=ALU.add, op1=ALU.mult),
                     reads=[("tg", j)], writes=["sg"])
                T.op("dve", lambda e: e.reciprocal(out=sm[:, 40:48], in_=Os[:, :, 64:65].rearrange("p h o -> p (h o)")),
                     reads=[("Os", 0), ("Os", 1)], writes=["rs"])
                T.op("dve", lambda e: e.reciprocal(out=sm[:, 48:56], in_=Ow[:, :, 64:65].rearrange("p h o -> p (h o)")),
                     reads=[("Ow", 0), ("Ow", 1)], writes=["rw"])
                T.op("dve", lambda e: e.tensor_tensor(out=sm[:, 16:24], in0=sm[:, 16:24], in1=sm[:, 8:16], op=ALU.mult),
                     reads=["sg", ("rc", pr, 0), ("rc", pr, 1)], writes=["fc"])
                T.op("dve", lambda e: e.tensor_tensor(out=sm[:, 24:40], in0=sm[:, 24:40], in1=sm[:, 40:56], op=ALU.mult),
                     reads=["sg", "rs", "rw"], writes=["fsw"])
                T.op("dve", lambda e: e.tensor_tensor(out=otmp[:, 0, :].rearrange("p (h d) -> p h d", d=64), in0=Oc[:, :, 0:64],
                                                      in1=sm[:, 16:24].unsqueeze(2).to_broadcast([128, 8, 64]), op=ALU.mult),
                     reads=[("Oc", pr, 0), ("Oc", pr, 1), "fc"], writes=[("ta", 0)])
                T.op("dve", lambda e: e.tensor_tensor(out=otmp[:, 1, :].rearrange("p (h d) -> p h d", d=64), in0=Os[:, :, 0:64],
                                                      in1=sm[:, 24:32].unsqueeze(2).to_broadcast([128, 8, 64]), op=ALU.mult),
                     reads=[("Os", 0), ("Os", 1), "fsw"], writes=[("ta", 1)])
                T.op("dve", lambda e: e.tensor_tensor(out=otmp[:, 0, :], in0=otmp[:, 0, :], in1=otmp[:, 1, :], op=ALU.add),
                     reads=[("ta", 0), ("ta", 1)], writes=[("ta", 0)])
                T.op("dve", lambda e: e.tensor_tensor(out=otmp[:, 1, :].rearrange("p (h d) -> p h d", d=64), in0=Ow[:, :, 0:64],
                                                      in1=sm[:, 32:40].unsqueeze(2).to_broadcast([128, 8, 64]), op=ALU.mult),
                     reads=[("Ow", 0), ("Ow", 1), "fsw"], writes=[("ta", 1)])
                T.op("dve", lambda e: e.tensor_tensor(out=o_tok[:, :], in0=otmp[:, 0, :], in1=otmp[:, 1, :], op=ALU.add),
                     reads=[("ta", 0), ("ta", 1)], writes=["o_tok"])

                def fin():
                    tb = nxt("T", 2)
                    for k in range(4):
                        T.mm(lambda e, k=k: e.transpose(Tb[tb][:, k * 128:(k + 1) * 128], o_tok[:, k * 128:(k + 1) * 128], ident[:, :]),
                             reads=["o_tok", "ident"], writes=[("T", tb)], inc=(k == 3))
                    T.op("act", lambda e: e.activation(out=o_nsaT[:, :, jsl], in_=Tb[tb][:, 0:512].rearrange("p (k t) -> p k t", k=4),
                                                       func=ACTF.Copy),
                         reads=[("T", tb)], writes=[("o_nsaT", j)])
                defer_pe.append(fin)

            def C_blocks(j):
                i = 4 * st + j
                jsl = slice(j * 128, (j + 1) * 128)
                nct_vis = min(NCT, (8 * i + 6) // 128 + 1)
                for g in range(2):
                    ob = nxt("O", 2)
                    for ct in range(nct_vis):
                        masks = []
                        ut = i - 16 * ct
                        if ut < 17:
                            masks.append((ident[:, :], bc4(wm[:, ut * 128:(ut + 1) * 128]), ["ident", "wm"]))
                        lastb = (ct == nct_vis - 1)
                        attn_block(g, jsl, kcT[0:67, g, ct * 128:(ct + 1) * 128], ["kcT"], masks,
                                   vca[:, ct, g, :], ["vca"], ob, ct == 0, lastb, 128,
                                   done=(lambda j=j, g=g, ob=ob: cmp_done(j, g, ob)) if lastb else None)

            def W_blocks(j):
                i = 4 * st + j
                jsl = slice(j * 128, (j + 1) * 128)
                for g in range(2):
                    ob = nxt("O", 2)
                    kts = list(range(max(0, i - 4), i + 1))
                    for kt in kts:
                        masks = []
                        if kt == i:
                            masks.append((ident[:, :], bc4(caus[:, 0, :]), ["ident", "caus"]))
                        if kt == i - 4:
                            masks.append((ident[:, :], bc4(caus[:, 1, :]), ["ident", "caus"]))
                        lastb = (kt == kts[-1])
                        attn_block(g, jsl, kwT[0:67, g, kt * 128:(kt + 1) * 128], [("kwT", kt), "kwTaug"], masks,
                                   vw[:, kt, g, :], [("vw", kt), "vw1"], ob, kt == kts[0], lastb, 65,
                                   done=(lambda j=j, g=g, ob=ob: win_done(j, g, ob)) if lastb else None)
                    if g == 0:
                        for g2 in range(2):
                            if (i, g2) in sel_done:
                                sel_transpose(j, g2)
                                selT_done[(i, g2)] = True

            def S_blocks(j):
                i = 4 * st + j
                jsl = slice(j * 128, (j + 1) * 128)
                for g in range(2):
                    if (i, g) not in selT_done:
                        if (i, g) not in sel_done:
                            flush_pv()
                        sel_transpose(j, g)
                    if g == 1:
                        while defer_pe:
                            defer_pe.pop(0)()
                    ob = nxt("O", 2)
                    for kt in range(i + 1):
                        if kt == i:
                            masks = [(ident[:, :], bc4(caus[:, 0, :]), ["ident", "caus"])]
                        else:
                            masks = [(cE[:, kt * 128:(kt + 1) * 128], bc4(selneg[:, g, :]), ["cE", "selneg%d" % g])]
                        lastb = (kt == i)
                        attn_block(g, jsl, ksT[0:67, g, kt * 128:(kt + 1) * 128], [("ksT", kt), "ksTaug"], masks,
                                   vs[:, kt, g, :], [("vs", kt), "vs1"], ob, kt == 0, lastb, 65,
                                   done=(lambda j=j, g=g, ob=ob: slc_done(j, g, ob)) if lastb else None)

            W_blocks(0)
            C_blocks(0)
            for j in range(4):
                if j > 0:
                    W_blocks(j)
                if j < 3:
                    C_blocks(j + 1)
                S_blocks(j)
            flush_pv()
            onsa_keys = [("o_nsaT", j) for j in range(4)]

            ckpt("D")
            for cc in range(8):
                s, sk = wnext()
                wv = s[:, 0:3072].rearrange("p (k r c) -> p k r c", k=8, r=3)
                gbs = []
                for r in (1, 2, 0):
                    gb = nxt("G", NG)
                    for k in range(KC):
                        T.mm(lambda e, k=k, r=r, gb=gb: e.matmul(Gb[gb][:, :], lhsT=wv[:, k, r, :], rhs=xnT[:, k, :], start=(k == 0), stop=(k == KC - 1)),
                             reads=xnT_keys + [sk], writes=[("G", gb)], inc=(k == KC - 1))
                    if r == 1:
                        T.op("act", lambda e, gb=gb: e.activation(out=cxt[:, :], in_=Gb[gb][:, :], func=ACTF.Copy),
                             reads=[("G", gb)], writes=["cxt"])
                    elif r == 2:
                        T.op("dve", lambda e: e.tensor_copy(out=ubuf[:, 0:2], in_=halo[:, cc, :]), reads=["halo"], writes=["ubufh"])
                        T.op("dve", lambda e, gb=gb: e.tensor_tensor(out=ubuf[:, 2:514], in0=Gb[gb][:, :], in1=cxt[:, :], op=ALU.mult),
                             reads=[("G", gb), "cxt"], writes=["ubuf"])
                        T.op("dve", lambda e: e.tensor_copy(out=halo[:, cc, :], in_=ubuf[:, 512:514]), reads=["ubuf"], writes=["halo"])
                        T.op("dve", lambda e: e.tensor_scalar(out=ybuf[:, :], in0=ubuf[:, 2:514], scalar1=convw[:, cc, 2:3], scalar2=None, op0=ALU.mult),
                             reads=["ubuf", "convw"], writes=["ybuf"])
                        T.op("dve", lambda e: e.scalar_tensor_tensor(out=ybuf[:, :], in0=ubuf[:, 1:513], scalar=convw[:, cc, 1:2], in1=ybuf[:, :],
                                                                     op0=ALU.mult, op1=ALU.add),
                             reads=["ubuf", "ubufh", "convw", "ybuf"], writes=["ybuf"])
                        T.op("dve", lambda e: e.scalar_tensor_tensor(out=ybuf[:, :], in0=ubuf[:, 0:512], scalar=convw[:, cc, 0:1], in1=ybuf[:, :],
                                                                     op0=ALU.mult, op1=ALU.add),
                             reads=["ubuf", "ubufh", "convw", "ybuf"], writes=["ybuf"])
                    else:
                        T.op("dve", lambda e, gb=gb: e.tensor_tensor(out=zT[:, cc, :], in0=Gb[gb][:, :], in1=ybuf[:, :], op=ALU.mult),
                             reads=[("G", gb), "ybuf"], writes=[("zT", cc)])
                if cc == 1:
                    while defer_pe:
                        defer_pe.pop(0)()
            zT_keys = [("zT", cc) for cc in range(8)]

            ckpt("E")
            for m in range(8):
                s, sk = wnext()
                wg = s[:, 0:2048].rearrange("p (k r c) -> p k r c", k=8, r=2)
                wa = s[:, 2048:2560].rearrange("p (k c) -> p k c", k=4)
                wb = s[:, 2560:3584].rearrange("p (k c) -> p k c", k=8)
                for r in range(2):
                    gb = nxt("G", NG)
                    for k in range(KC):
                        T.mm(lambda e, k=k, r=r, gb=gb: e.matmul(Gb[gb][:, :], lhsT=wg[:, k, r, :], rhs=xnT[:, k, :], start=(k == 0), stop=(k == KC - 1)),
                             reads=xnT_keys + [sk], writes=[("G", gb)], inc=(k == KC - 1))
                    T.op("act", lambda e, r=r, gb=gb: e.activation(out=ta[:, r, :], in_=Gb[gb][:, :], func=ACTF.Tanh, scale=0.5),
                         reads=[("G", gb)], writes=[("ta", r)])
                    gb2 = nxt("G", NG)
                    if r == 0:
                        for k in range(4):
                            T.mm(lambda e, k=k, gb2=gb2: e.matmul(Gb[gb2][:, :], lhsT=wa[:, k, :], rhs=o_nsaT[:, k, :], start=(k == 0), stop=(k == 3)),
                                 reads=onsa_keys + [sk], writes=[("G", gb2)], inc=(k == 3))
                    else:
                        for k in range(KC):
                            T.mm(lambda e, k=k, gb2=gb2: e.matmul(Gb[gb2][:, :], lhsT=wb[:, k, :], rhs=zT[:, k, :], start=(k == 0), stop=(k == KC - 1)),
                                 reads=zT_keys + [sk], writes=[("G", gb2)], inc=(k == KC - 1))
                    T.op("dve", lambda e, r=r, gb2=gb2: e.scalar_tensor_tensor(out=mt[:, r, :], in0=ta[:, r, :], scalar=1.0, in1=Gb[gb2][:, :],
                                                                               op0=ALU.add, op1=ALU.mult),
                         reads=[("ta", r), ("G", gb2)], writes=[("mt", r)])
                T.op("dve", lambda e: e.tensor_tensor(out=mixedT[:, m, :], in0=mt[:, 0, :], in1=mt[:, 1, :], op=ALU.add),
                     reads=[("mt", 0), ("mt", 1)], writes=[("mixedT", m)])
            mixed_keys = [("mixedT", m) for m in range(8)]

            ckpt("F")
            s0, sk0 = wnext()
            s1, sk1 = wnext(hold=1)
            wos = [(s0[:, 0:4096].rearrange("p (k c) -> p k c", k=8), sk0), (s1[:, 0:4096].rearrange("p (k c) -> p k c", k=8), sk1)]
            slots_ = {}
            for j in range(4):
                jsl = slice(j * 128, (j + 1) * 128)
                for hh in range(2):
                    wo, sk = wos[hh]
                    gb = nxt("G", NG)
                    for k in range(KC):
                        T.mm(lambda e, k=k, gb=gb: e.matmul(Gb[gb][:, :], lhsT=mixedT[:, k, jsl], rhs=wo[:, k, :], start=(k == 0), stop=(k == KC - 1)),
                             reads=mixed_keys + [sk], writes=[("G", gb)], inc=(k == KC - 1))
                    T.op("dve", lambda e, gb=gb, j=j: e.scalar_tensor_tensor(out=xt[:, j, hh * 512:(hh + 1) * 512], in0=Gb[gb][:, :], scalar=0.5,
                                                                             in1=xt[:, j, hh * 512:(hh + 1) * 512], op0=ALU.mult, op1=ALU.add),
                         reads=[("G", gb), ("xt", j, hh)], writes=[("xt", j, hh)])
                slots_[j] = rms_chain(j, 4 + j)
                if j >= 1:
                    rms_tr(slots_[j - 1], j - 1, g2c)
            rms_tr(slots_[3], 3, g2c)

            ckpt("G")
            ckpt("H")
            for fs in range(8):
                s, sk = wnext()
                wu = s[:, 0:4096].rearrange("p (k c) -> p k c", k=8)
                for fl in range(4):
                    fc = fs * 4 + fl
                    gb = nxt("G", NG)
                    for k in range(KC):
                        T.mm(lambda e, k=k, gb=gb: e.matmul(Gb[gb][:, :], lhsT=wu[:, k, fl * 128:(fl + 1) * 128], rhs=xnT[:, k, :],
                                                            start=(k == 0), stop=(k == KC - 1)),
                             reads=xnT_keys + [sk], writes=[("G", gb)], inc=(k == KC - 1))
                    r = fc % 2
                    T.op("act", lambda e, gb=gb, r=r: e.activation(out=rl[:, r, :], in_=Gb[gb][:, :], func=ACTF.Relu),
                         reads=[("G", gb)], writes=[("rl", r)])
                    T.op("dve", lambda e, r=r, fc=fc: e.tensor_tensor(out=actT[:, fc, :], in0=rl[:, r, :], in1=rl[:, r, :], op=ALU.mult),
                         reads=[("rl", r)], writes=[("actT", fc)] + alias(fc))
            act_keys = [("actT", fc) for fc in range(32)]

            ckpt("I")
            accs = [(Gb[q_], ("G", q_)) for q_ in range(4)]
            for hh in range(2):
                for fq in range(4):
                    s, sk = wnext()
                    wd = s[:, 0:4096].rearrange("p (k c) -> p k c", k=8)
                    for fl in range(8):
                        fc = fq * 8 + fl
                        for j in range(4):
                            jsl = slice(j * 128, (j + 1) * 128)
                            T.mm(lambda e, fl=fl, fc=fc, j=j: e.matmul(accs[j][0][:, :], lhsT=actT[:, fc, jsl], rhs=wd[:, fl, :],
                                                                       start=(fc == 0), stop=(fc == 31)),
                                 reads=[("actT", fc), sk] + alias(fc), writes=[accs[j][1]], inc=(fl == 7 and j == 3) or fc == 31)
                for j in range(4):
                    i = 4 * st + j
                    o_ = nxt("OB", 4)
                    T.op("dve", lambda e, j=j: e.tensor_tensor(out=obuf[:, o_, :], in0=accs[j][0][:, :],
                                                               in1=xt[:, j, hh * 512:(hh + 1) * 512], op=ALU.add),
                         reads=[accs[j][1], ("xt", j, hh)], writes=[("ob", o_)])
                    T.dma("pool", y_d[i * 128:(i + 1) * 128, hh * 512:(hh + 1) * 512], obuf[:, o_, :], reads=[("ob", o_)],
                          semkey=("ob", o_))
                if st + 1 < NST:
                    load_x(st + 1, hh, "sp")

            ckpt("J")

    try:
        ckpt("setup")
        _main_loop()
    except _Stop:
        for sk_, v_ in T.cnt.items():
            if v_ > 0:
                nc.gpsimd.wait_ge(T.sem[sk_], v_)
        dumpable = {"xnT": xnT[:, :, :], "qT": qT, "ksT": ksT[:, :, :], "kwT": kwT[:, :, :], "vs": vs[:, :, :, :],
                    "vw": vw[:, :, :, :], "kcT": kcT[:, :, :], "vca": vca[:, :, :, :], "o_nsaT": o_nsaT, "tg": tg[:, :, :],
                    "zT": zT, "mixedT": mixedT, "Oc": Oc2[0][:, :, :], "Ow": Ow[:, :, :], "Os": Os[:, :, :], "imp": imp2[0][:, :, :],
                    "selb": selb2[0][:, :, :], "score": score2[0][:, :, :], "selneg": selneg[:, :, :], "o_tok": o_tok[:, :],
                    "actT": arena[:, :, :], "kcr": kcr[:, :, :], "sm": sm2[0][:, :], "m8": m82[0][:, :, :]}
        for name in dump:
            ap_ = dumpable[name]
            od = nc.dram_tensor("dbg_" + name, list(ap_.shape), F32, kind="ExternalOutput").ap()
            T.dma("pool", od, ap_, semkey="dbg")
        if dump:
            nc.gpsimd.wait_ge(T.sem["dbg"], T.cnt["dbg"])
        for j in range(4):
            T.dma("pool", y_d[j * 128:(j + 1) * 128, :], xt[:, j, :], reads=[("xt", j, 0), ("xt", j, 1)], semkey=("ob", j % 3))

    for o_ in range(4):
        if ("ob", o_) in T.sem:
            nc.gpsimd.wait_ge(T.sem[("ob", o_)], T.cnt[("ob", o_)])
    build.stats = dict(T.ninstr)
    return nc


def _host_inputs(S, x_b, p):
    d = {"x": np.ascontiguousarray(x_b, dtype=np.float32)}
    d.update(p)
    return d


def _pack_jobs(w_in, w_ba, w_bb, w_out, w_up, w_down, wck, wcv):
    J = np.zeros((39, 128, SLOT), np.float32)

    def kp(a):
        k = a.shape[0] // 128
        return a.reshape(k, 128, a.shape[1]).transpose(1, 0, 2)

    J[0, :, 0:4096] = kp(w_in[:, 0:512]).reshape(128, -1)
    J[1, :, 0:4096] = kp(w_in[:, 512:1024]).reshape(128, -1)
    J[2, :, 0:2240] = kp(w_in[:, 1024:1304]).reshape(128, -1)
    for kv, w in enumerate((wck, wcv)):
        w3 = w.reshape(32, 64, 64)
        blk = np.zeros((128, 32, 128), np.float32)
        for g in range(2):
            blk[g * 64:(g + 1) * 64, :, g * 64:(g + 1) * 64] = w3.transpose(1, 0, 2)
        J[3 + kv, :, 0:4096] = blk.reshape(128, -1)
    for cc in range(8):
        t = np.zeros((128, 8, 3, 128), np.float32)
        for r, c0 in enumerate((C_CB, C_CC, C_CX)):
            t[:, :, r, :] = kp(w_in[:, c0 + cc * 128:c0 + (cc + 1) * 128])
        J[5 + cc, :, 0:3072] = t.reshape(128, -1)
    for m in range(8):
        t = np.zeros((128, 8, 2, 128), np.float32)
        for r, c0 in enumerate((C_GA, C_GB)):
            t[:, :, r, :] = kp(w_in[:, c0 + m * 128:c0 + (m + 1) * 128])
        J[13 + m, :, 0:2048] = t.reshape(128, -1)
        J[13 + m, :, 2048:2560] = kp(w_ba[:, m * 128:(m + 1) * 128]).reshape(128, -1)
        J[13 + m, :, 2560:3584] = kp(w_bb[:, m * 128:(m + 1) * 128]).reshape(128, -1)
    for hh in range(2):
        J[21 + hh, :, 0:4096] = kp(w_out[:, hh * 512:(hh + 1) * 512]).reshape(128, -1)
    for fs in range(8):
        J[23 + fs, :, 0:4096] = kp(w_up[:, fs * 512:(fs + 1) * 512]).reshape(128, -1)
    for hh in range(2):
        for fq in range(4):
            J[31 + hh * 4 + fq, :, 0:4096] = kp(w_down[fq * 1024:(fq + 1) * 1024, hh * 512:(hh + 1) * 512]).reshape(128, -1)
    return J


def _param_layout(norm1_g, w_in, q_norm_g, k_norm_g, cmp_pos_k, cmp_pos_v, w_cmp_k, w_cmp_v, conv_w,
                  w_branch_a, w_branch_b, w_out, norm2_g, w_up, w_down, l=0):
    f = lambda a: np.ascontiguousarray(np.asarray(a), dtype=np.float32)
    p = {
        "wjobs": _pack_jobs(f(w_in[l]), f(w_branch_a[l]), f(w_branch_b[l]), f(w_out[l]), f(w_up[l]), f(w_down[l]),
                            f(w_cmp_k[l]), f(w_cmp_v[l])),
        "w_cmp_k": f(w_cmp_k[l]), "w_cmp_v": f(w_cmp_v[l]),
        "g1c": f(np.asarray(norm1_g[l]).reshape(8, 128).T), "g2c": f(np.asarray(norm2_g[l]).reshape(8, 128).T),
        "gq": f(np.asarray(q_norm_g[l]).reshape(64, 1)), "gk": f(np.asarray(k_norm_g[l]).T),
        "posk": f(np.asarray(cmp_pos_k[l]).reshape(16, 128).T), "posv": f(np.asarray(cmp_pos_v[l]).reshape(16, 128).T),
        "convw": f(np.asarray(conv_w[l]).reshape(3, 8, 128).transpose(2, 1, 0)),
    }
    return p


def kernel(x, norm1_g, w_in, q_norm_g, k_norm_g, cmp_pos_k, cmp_pos_v, w_cmp_k, w_cmp_v, conv_w,
           w_branch_a, w_branch_b, w_out, norm2_g, w_up, w_down):
    x = np.asarray(x)
    B, S, _ = x.shape
    p = _param_layout(norm1_g, w_in, q_norm_g, k_norm_g, cmp_pos_k, cmp_pos_v, w_cmp_k, w_cmp_v, conv_w,
                      w_branch_a, w_branch_b, w_out, norm2_g, w_up, w_down)
    p.update(make_consts(S))
    nc = build(S)
    in_maps = [_host_inputs(S, x[b], p) for b in range(B)]
    res = run_bass_kernel_spmd(nc, in_maps, core_ids=list(range(B)))
    return np.stack([np.asarray(r["y"], dtype=np.float32) for r in res.results], axis=0)
```

```python
import numpy as np
import concourse.bass as bass
import concourse.mybir as mybir
from concourse.bass_utils import run_bass_kernel_spmd

F32 = mybir.dt.float32
BF16 = mybir.dt.bfloat16
ALU = mybir.AluOpType
ACTF = mybir.ActivationFunctionType
AX = mybir.AxisListType

D = 1024
KC = 8
PW = 6424
DFF = 4096
EPS = 1e-6
NEG = -30000.0
(C_Q, C_KC, C_VC, C_KS, C_VS, C_KW, C_VW, C_GN, C_CB, C_CC, C_CX, C_GA, C_GB) = (
    0, 512, 640, 768, 896, 1024, 1152, 1280, 1304, 2328, 3352, 4376, 5400)
SLOT = 4096
NSLOTS = 3
import os as _os
CASTE = int(_os.environ.get("CASTE", str(1 << 17)))


class Tracker:
    def __init__(self, nc):
        self.nc = nc
        self.eng = {"pe": nc.tensor, "act": nc.scalar, "dve": nc.vector,
                    "pool": nc.gpsimd, "sp": nc.sync}
        self.sem = {}
        self.cnt = {}
        for k in self.eng:
            self.sem[k] = nc.alloc_semaphore("s_" + k)
            self.cnt[k] = 0
        self.seen = {k: {} for k in self.eng}
        self.lastw = {}
        self.readers = {}
        self.ninstr = {k: 0 for k in self.eng}

    def _sem(self, key):
        if key not in self.sem:
            self.sem[key] = self.nc.alloc_semaphore("d%d" % len(self.sem))
            self.cnt[key] = 0
        return self.sem[key]

    def _deps(self, ek, reads, writes):
        deps = {}

        def add(h, same_ok):
            if h is None:
                return
            sk, v = h
            if sk == ek and ek == "pe":
                return
            if deps.get(sk, 0) < v:
                deps[sk] = v

        for r in reads:
            add(self.lastw.get(r), True)
            if isinstance(r, tuple) and r[0] in ("G", "S", "O", "T"):
                for sk, v in self.readers.get(r, {}).items():
                    if sk != ek:
                        add((sk, v), False)
        for w in writes:
            add(self.lastw.get(w), False)
            for sk, v in self.readers.get(w, {}).items():
                add((sk, v), False)
        return deps

    def _emit_waits(self, ek, deps):
        e = self.eng[ek]
        seen = self.seen[ek]
        for sk, v in deps.items():
            if seen.get(sk, 0) >= v:
                continue
            e.wait_ge(self.sem[sk], v)
            seen[sk] = v

    def _record(self, h, reads, writes):
        sk, v = h
        for r in reads:
            d = self.readers.setdefault(r, {})
            if d.get(sk, 0) < v:
                d[sk] = v
        for w in writes:
            self.lastw[w] = h
            self.readers[w] = {}

    def op(self, ek, fn, reads=(), writes=()):
        deps = self._deps(ek, reads, writes)
        self._emit_waits(ek, deps)
        ins = fn(self.eng[ek])
        self.cnt[ek] += 1
        self.ninstr[ek] += 1
        ins.then_inc(self.sem[ek], 1)
        h = (ek, self.cnt[ek])
        self._record(h, reads, writes)
        return h

    def mm(self, fn, reads=(), writes=(), inc=True):
        deps = self._deps("pe", reads, writes)
        self._emit_waits("pe", deps)
        ins = fn(self.eng["pe"])
        self.ninstr["pe"] += 1
        if inc:
            self.cnt["pe"] += 1
            ins.then_inc(self.sem["pe"], 1)
            h = ("pe", self.cnt["pe"])
        else:
            h = ("pe", self.cnt["pe"] + 1)
        self._record(h, reads, writes)
        return h

    def dma(self, qk, out, in_, reads=(), writes=(), semkey=None, **kw):
        deps = self._deps(qk, reads, writes)
        self._emit_waits(qk, deps)
        s = self._sem(semkey)
        ins = self.eng[qk].dma_start(out=out, in_=in_, **kw)
        self.cnt[semkey] += 16
        ins.then_inc(s, 16)
        h = (semkey, self.cnt[semkey])
        self._record(h, reads, writes)
        return h


def make_consts(S):
    NCB = S // 16 - 1
    NCT = (NCB + 127) // 128
    c = {}
    c["c_ident"] = np.eye(128, dtype=np.float32)
    k = np.arange(S)
    c["c_kaug"] = np.stack([(k % 128) - 64, k // 128, np.ones(S)]).astype(np.float32)
    cc = np.arange(NCT * 128)
    pos = 16 * cc + 31
    c["c_kcaug"] = np.stack([(pos % 128) - 64, pos // 128, np.ones_like(pos)]).astype(np.float32)
    NT_ = S // 128
    qa = np.zeros((3, 2, NT_, 4, 128), np.float32)
    for g in range(2):
        for h in range(4):
            sl = 2.0 ** (-(g * 4 + h + 1))
            qa[0, g, :, h, :] = sl
            qa[1, g, :, h, :] = 128.0 * sl
            qa[2, g, :, h, :] = -128.0 * sl * np.arange(NT_)[:, None]
    c["c_qaug"] = qa.reshape(3, 2, NT_ * 512)
    E = np.zeros((128, S), np.float32)
    for j in range(S // 64):
        E[j, 64 * j:64 * j + 64] = 1.0
    c["c_E"] = E
    u = np.arange(17 * 128)
    ccl = np.arange(128)
    c["c_wm"] = np.where(16 * ccl[:, None] + 31 > u[None, :], NEG, 0.0).astype(np.float32)
    kk = np.arange(128)[:, None]
    tt = np.arange(128)[None, :]
    caus = np.zeros((128, 2, 128), np.float32)
    caus[:, 0, :] = np.where(kk > tt, NEG, 0.0)
    caus[:, 1, :] = np.where(kk <= tt, NEG, 0.0)
    c["c_caus"] = caus
    TA = np.zeros((128, 128), np.float32)
    TB = np.zeros((128, 128), np.float32)
    for ttv in range(128):
        cr = 1 if ttv >= 64 else 0
        for mp in range(128):
            mr = mp - 64
            if mr <= cr - 2:
                TA[ttv, mp] = 1.0
            if mr == cr or mr == cr - 1:
                TB[ttv, mp] = 1e4
            elif mr > cr:
                TB[ttv, mp] = -1.0
    c["c_TA"] = TA
    c["c_TB"] = TB
    ov = np.zeros((128, NCT, 64), np.float32)
    for cb in range(NCB):
        for j in range(S // 64):
            o = min(16 * cb + 32, 64 * j + 64) - max(16 * cb, 64 * j)
            if o > 0:
                ov[cb % 128, cb // 128, j] = o / 32.0
    c["c_ov"] = ov
    return c


class _Stop(Exception):
    pass


def build(S, stop=None, dump=()):
    assert S % 512 == 0
    NT = S // 128
    NST = S // 512
    NSB = S // 64
    NSEL = min(16, NSB)
    assert NSEL in (8, 16)
    NCB = S // 16 - 1
    NCT = (NCB + 127) // 128
    NB16 = S // 16

    nc = bass.Bass("TRN2", target_bir_lowering=False)
    T = Tracker(nc)

    def din(name, shape):
        return nc.dram_tensor(name, list(shape), F32, kind="ExternalInput").ap()

    x_d = din("x", [S, D])
    y_d = nc.dram_tensor("y", [S, D], F32, kind="ExternalOutput").ap()
    wck_d = din("w_cmp_k", [2048, 64])
    wcv_d = din("w_cmp_v", [2048, 64])
    g1c_d = din("g1c", [128, 8])
    g2c_d = din("g2c", [128, 8])
    gq_d = din("gq", [64, 1])
    gk_d = din("gk", [64, 3])
    posk_d = din("posk", [128, 16])
    posv_d = din("posv", [128, 16])
    convw_d = din("convw", [128, 8, 3])
    cst = {}
    for name, shape in [("c_ident", [128, 128]), ("c_kaug", [3, S]), ("c_kcaug", [3, NCT * 128]),
                        ("c_qaug", [3, 2, S * 4]), ("c_E", [128, S]), ("c_wm", [128, 17 * 128]),
                        ("c_caus", [128, 2, 128]), ("c_TA", [128, 128]), ("c_TB", [128, 128]),
                        ("c_ov", [128, NCT, 64])]:
        cst[name] = din(name, shape)


    def sb(name, shape, dt):
        return nc.alloc_sbuf_tensor(name, list(shape), dt)

    ksT = sb("ksT", [67, 2, S], BF16)
    kwT = sb("kwT", [67, 2, S], BF16)
    vs = sb("vs", [128, NT, 2, 65], BF16)
    vw = sb("vw", [128, NT, 2, 65], BF16)
    kcr = sb("kcr", [128, 16, 33], BF16)
    vcr = sb("vcr", [128, 16, 33], BF16)
    kcT = sb("kcT", [67, 2, NCT * 128], BF16)
    vca = sb("vca", [128, NCT, 2, 128], BF16)
    cE = sb("cE", [128, S], BF16)
    wm = sb("wm", [128, 17 * 128], BF16)
    caus = sb("caus", [128, 2, 128], BF16)
    ident = sb("ident", [128, 128], BF16)
    TA = sb("TA", [128, 128], F32)
    TB = sb("TB", [128, 128], F32)
    posk = sb("posk_s", [128, 16], BF16)
    posv = sb("posv_s", [128, 16], BF16)
    biasrow = sb("biasrow", [1, 2, 2, 64], BF16)
    onesrow = sb("onesrow", [1, 128], BF16)
    g1c = sb("g1c_s", [128, 8], F32)
    g2c = sb("g2c_s", [128, 8], F32)
    gq = sb("gq_s", [64, 1], F32)
    gk = sb("gk_s", [64, 3], F32)
    convw = sb("convw_s", [128, 8, 3], F32)
    halo = sb("halo", [128, 8, 2], F32)
    neghalf = sb("neghalf", [128, 16], F32)
    selneg = sb("selneg", [128, 2, 128], BF16)
    xt = sb("xt", [128, 4, D], F32)
    xnT = sb("xnT", [128, KC, 512], BF16)
    xn_tok = sb("xn_tok", [128, 2, D], BF16)
    ss = sb("ss", [128, 8], F32)
    rstd = sb("rstd", [128, 8], F32)
    sq = sb("sq", [128, 768], F32)
    ssq = sb("ssq", [128, 16], F32)
    rq = sb("rq", [128, 16], F32)
    qk_tok2 = [sb("qk_tok%d" % i, [128, 768], BF16) for i in range(3)]
    tg = sb("tg", [128, 4, 24], F32)
    arena = sb("arena", [128, 32, 512], BF16)
    actT = arena
    zT = arena[:, 0:8, :]
    mixedT = arena[:, 8:16, :]
    qT = arena[0:67, 16:24, :].rearrange("p c t -> p (c t)").rearrange("p (g j h t) -> p g j h t", g=2, j=4, h=4)
    o_nsaT = arena[:, 24:28, :]
    vc_tok2 = [sb("vc_tok%d" % i, [128, 128], BF16) for i in range(3)]
    vtmp = sb("vtmp", [32, 2, 64], BF16)

    def alias(fc):
        if fc < 8:
            return [("zT", fc)]
        if fc < 16:
            return [("mixedT", fc - 8)]
        if fc < 20:
            return [("qT", 0)]
        if fc < 24:
            return [("qT", 1)]
        if fc < 28:
            return [("o_nsaT", jj) for jj in range(4)]
        return []
    Pt = [sb("P%d" % i, [128, 512], BF16) for i in range(4)]
    Oc2 = [sb("Oc%d" % i, [128, 8, 128], F32) for i in range(2)]
    Ow = sb("Ow", [128, 8, 65], F32)
    Os = sb("Os", [128, 8, 65], F32)
    sm2 = [sb("sm%d" % i, [128, 64], F32) for i in range(2)]
    imp2 = [sb("imp%d" % i, [128, 2, 64], F32) for i in range(2)]
    score2 = [sb("score%d" % i, [128, 2, 64], F32) for i in range(2)]
    work2 = [sb("work%d" % i, [128, 2, 64], F32) for i in range(2)]
    m82 = [sb("m8_%d" % i, [128, 2, 16], F32) for i in range(2)]
    selb2 = [sb("selb%d" % i, [128, 2, 64], BF16) for i in range(2)]
    o_tok = sb("o_tok", [128, 512], BF16)
    kc_tok2 = [sb("kc_tok%d" % i, [128, 128], BF16) for i in range(3)]
    kc_tok = kc_tok2[0]
    cxt = sb("cxt", [128, 512], F32)
    ubuf = sb("ubuf", [128, 514], F32)
    ybuf = sb("ybuf", [128, 512], F32)
    ta = sb("ta", [128, 2, 512], F32)
    otmp = ta
    mt = sb("mt", [128, 2, 512], F32)
    rl = sb("rl", [128, 2, 512], BF16)
    obuf = sb("obuf", [128, 4, 512], F32)
    wslots = [sb("wslot%d" % i, [128, SLOT], BF16) for i in range(NSLOTS)]

    NG = 6
    Gb = [nc.alloc_psum_tensor("G%d" % i, [128, 512], F32) for i in range(NG)]
    Sb = Gb[0:3]
    Ob = Gb[3:5]
    Tb = [nc.alloc_psum_tensor("T%d" % i, [128, 1024], BF16) for i in range(2)]
    rot = {"G": 0, "S": 0, "O": 0, "T": 0, "P": 0, "A": 0, "OB": 0}
    sel_done = {}
    selT_done = {}
    defer_pe = []

    def nxt(kind, n):
        v = rot[kind]
        rot[kind] = (v + 1) % n
        return v

    CK = "const"

    import os
    DBG = os.environ.get("KDBG", "").split(",")
    cl_n = [0]

    def cload(dst, src, key, after=()):
        cl_n[0] += 1
        T.dma("pool", dst, src, reads=list(after), writes=[("cl", cl_n[0])], semkey=CK)

    cload(ident[:, :], cst["c_ident"], "ident")
    cload(g1c[:, :], g1c_d, "g1c")
    cload(g2c[:, :], g2c_d, "g2c")
    cload(gq[:, :], gq_d, "gq")
    cload(gk[:, :], gk_d, "gk")
    cload(convw[:, :, :], convw_d, "convw")
    cload(TA[:, :], cst["c_TA"], "TA")
    cload(TB[:, :], cst["c_TB"], "TB")
    cload(cE[:, :], cst["c_E"], "cE")
    cload(wm[:, :], cst["c_wm"], "wm")
    cload(caus[:, :, :], cst["c_caus"], "caus")
    cload(posk[:, :], posk_d, "posk")
    cload(posv[:, :], posv_d, "posv")
    wflk = wslots[NSLOTS - 1][:, 0:1024].rearrange("p (c e) -> p c e", c=16)
    wflv = wslots[NSLOTS - 1][:, 1024:2048].rearrange("p (c e) -> p c e", c=16)
    WFK = ("ws", NSLOTS - 1)
    T.dma("pool", wflk, wck_d.rearrange("(c p) e -> p c e", p=128), writes=[WFK], semkey=("wsq", NSLOTS - 1))
    T.dma("pool", wflv, wcv_d.rearrange("(c p) e -> p c e", p=128), writes=[WFK], semkey=("wsq", NSLOTS - 1))
    T.op("dve", lambda e: e.memset(kcT[:, :, :], 0.0), writes=["kcT_ms"])
    T.op("dve", lambda e: e.memset(vca[:, :, :, :], 0.0), writes=["vca_ms"])
    T.op("dve", lambda e: e.memset(vs[:, :, :, 64:65], 1.0), writes=["vs1"])
    T.op("dve", lambda e: e.memset(vw[:, :, :, 64:65], 1.0), writes=["vw1"])
    T.op("dve", lambda e: e.memset(selneg[:, :, :], 0.0), writes=["selneg0", "selneg1"])
    T.op("dve", lambda e: e.memset(halo[:, :, :], 0.0), writes=["halo"])
    T.op("dve", lambda e: e.memset(neghalf[:, :], -0.5), writes=["neghalf"])
    T.op("dve", lambda e: e.memset(onesrow[:, :], 1.0), writes=["onesrow"])
    wck_v = wck_d.rearrange("(l d) e -> d l e", d=64)
    wcv_v = wcv_d.rearrange("(l d) e -> d l e", d=64)
    for g in range(2):
        cload(ksT[64:67, g, :], cst["c_kaug"], "ksTaug")
        cload(kwT[64:67, g, :], cst["c_kaug"], "kwTaug")
        cload(kcT[64:67, g, :], cst["c_kcaug"], "kcT", after=["kcT_ms"])
    cload(vca[:, :, 0, 64:128], cst["c_ov"], "vca", after=["vca_ms"])
    cload(vca[:, :, 1, 64:128], cst["c_ov"], "vca", after=["vca_ms"])
    ctot = ("const", T.cnt[CK])
    for key in ["ident", "g1c", "g2c", "gq", "gk", "convw", "TA", "TB", "cE", "wm", "caus", "posk", "posv",
                "ksTaug", "kwTaug", "kcT", "vca"]:
        T.lastw[key] = ctot

    def load_x(st, hh, q="pool"):
        for j in range(4):
            i = 4 * st + j
            T.dma(q, xt[:, j, hh * 512:(hh + 1) * 512], x_d[i * 128:(i + 1) * 128, hh * 512:(hh + 1) * 512],
                  writes=[("xt", j, hh)], semkey=("xt" if q == "pool" else "xts", j))

    load_x(0, 0)
    load_x(0, 1)

    JPS = 39
    NJOBS = JPS * NST
    wjobs_d = din("wjobs", [JPS, 128, SLOT])
    scr_t = nc.dram_tensor("scr_slots", [JPS, 128, SLOT], BF16).ap()
    job_used = [4096, 4096, 2240, 4096, 4096] + [3072] * 8 + [3584] * 8 + [4096] * 18
    assert len(job_used) == JPS
    wstate = {"issued": 0, "next": 0}

    def issue_job(m):
        sidx = m % NSLOTS
        s = wslots[sidx]
        wk_ = ("ws", sidx)
        jidx = m % JPS
        nu = job_used[jidx]
        if m >= JPS:
            T.dma("sp", s[0:64, 0:nu], scr_t[jidx][0:64, 0:nu], reads=[("scr", jidx)], writes=[wk_], semkey=wk_)
            T.dma("act", s[64:128, 0:nu], scr_t[jidx][64:128, 0:nu], reads=[("scr", jidx)], writes=[wk_], semkey=wk_)
            return
        for c0 in range(0, nu, 1024):
            c1 = min(nu, c0 + 1024)
            T.dma("pool", s[:, c0:c1], wjobs_d[jidx][:, c0:c1], writes=[wk_], semkey=("wsq", sidx))
        T.dma("sp", scr_t[jidx][:, 0:nu], s[:, 0:nu], reads=[wk_], writes=[("scr", jidx)], semkey=("scrs", jidx % 4))

    def wnext(hold=0):
        n = wstate["next"]
        wstate["next"] += 1
        while wstate["issued"] < min(NJOBS, n + NSLOTS - hold):
            issue_job(wstate["issued"])
            wstate["issued"] += 1
        return wslots[n % NSLOTS], ("ws", n % NSLOTS)

    T.op("dve", lambda e: e.tensor_scalar(out=gq[:, :], in0=gq[:, :], scalar1=0.125, scalar2=None, op0=ALU.mult),
         reads=["gq"], writes=["gq"])
    for kv, (pp, wf, pk, wk) in enumerate(((posk, wflk, "posk", WFK), (posv, wflv, "posv", WFK))):
        gb = nxt("G", NG)
        for c in range(16):
            T.mm(lambda e, c=c, pp=pp, wf=wf, gb=gb: e.matmul(Gb[gb][0:1, 0:64], lhsT=pp[:, c:c + 1], rhs=wf[:, c, :],
                                                              start=(c == 0), stop=(c == 15)),
                 reads=[pk, wk], writes=[("G", gb)], inc=(c == 15))
        for g in range(2):
            T.op("dve", lambda e, kv=kv, g=g, gb=gb: e.tensor_copy(out=biasrow[0:1, kv, g, :], in_=Gb[gb][0:1, 0:64]),
                 reads=[("G", gb)], writes=["biasrow"])

    def rms_chain(j, stat_col):
        a = nxt("A", 2)
        src_keys = [("xt", j, 0), ("xt", j, 1)]
        T.op("act", lambda e: e.activation(out=rl[:, :, :].rearrange("p a b -> p (a b)"), in_=xt[:, j, :], func=ACTF.Square,
                                           accum_out=ss[:, stat_col:stat_col + 1]),
             reads=src_keys, writes=[("rl", 0), ("rl", 1), ("ss", stat_col)])
        T.op("dve", lambda e: e.tensor_scalar(out=rstd[:, stat_col:stat_col + 1], in0=ss[:, stat_col:stat_col + 1],
                                              scalar1=1.0 / D, scalar2=EPS, op0=ALU.mult, op1=ALU.add),
             reads=[("ss", stat_col)], writes=[("rstd", stat_col)])
        T.op("pool", lambda e: e.tensor_tensor(out=rstd[:, stat_col:stat_col + 1], in0=rstd[:, stat_col:stat_col + 1],
                                               in1=neghalf[:, 0:1], op=ALU.pow),
             reads=[("rstd", stat_col), "neghalf"], writes=[("rstd", stat_col)])
        T.op("dve", lambda e: e.tensor_scalar(out=xn_tok[:, a, :], in0=xt[:, j, :], scalar1=rstd[:, stat_col:stat_col + 1],
                                              scalar2=None, op0=ALU.mult),
             reads=src_keys + [("rstd", stat_col)], writes=[("xn_tok", a)])
        return a

    def rms_tr(a, j, gcol):
        jsl = slice(j * 128, (j + 1) * 128)
        tb = nxt("T", 2)
        for k in range(KC):
            T.mm(lambda e, k=k: e.transpose(Tb[tb][:, k * 128:(k + 1) * 128], xn_tok[:, a, k * 128:(k + 1) * 128], ident[:, :]),
                 reads=[("xn_tok", a), "ident"], writes=[("T", tb)], inc=(k == KC - 1))
        T.op("dve", lambda e: e.tensor_tensor(out=xnT[:, :, jsl],
                                              in0=Tb[tb][:, :].rearrange("p (k t) -> p k t", k=KC),
                                              in1=gcol[:, :].unsqueeze(2).to_broadcast([128, KC, 128]), op=ALU.mult),
             reads=[("T", tb), "g1c", "g2c"], writes=[("xnT", j)])

    def rms_phase(gcol, col0):
        slots = {}
        for j in range(4):
            slots[j] = rms_chain(j, col0 + j)
            if j >= 1:
                rms_tr(slots[j - 1], j - 1, gcol)
        rms_tr(slots[3], 3, gcol)

    def qbc(g, jsl):
        return qT[0:67, g, jsl.start // 128, :, :]

    def ps3(bank):
        return bank[:, :].rearrange("p (h t) -> p h t", h=4)

    def bc4(ap2d):
        return ap2d.unsqueeze(1).to_broadcast([ap2d.shape[0], 4, ap2d.shape[1]])

    pend = []
    LAG = 2

    def emit_pv(blk):
        (p, v_ap, vkeys, ob, first, last, ncols, done) = blk
        for h in range(4):
            T.mm(lambda e, h=h: e.matmul(Ob[ob][:, h * ncols:(h + 1) * ncols], lhsT=Pt[p][:, h * 128:(h + 1) * 128], rhs=v_ap,
                                         start=(first and h == 0), stop=(last and h == 3), skip_group_check=True),
                 reads=[("P", p)] + vkeys, writes=[("G", 3 + ob)], inc=(h == 3))
        if done is not None:
            done()

    def flush_pv():
        while pend:
            emit_pv(pend.pop(0))

    def attn_block(g, jsl, kT_ap, kkeys, masks, v_ap, vkeys, ob, first, last, ncols, done=None):
        sbk = nxt("S", 3)
        nm = len(masks)
        T.mm(lambda e: e.matmul(ps3(Sb[sbk]), lhsT=kT_ap, rhs=qbc(g, jsl), start=True, stop=(nm == 0)),
             reads=kkeys + [("qT", g)], writes=[("G", sbk)], inc=(nm == 0))
        for mi, (ml, mr, mk) in enumerate(masks):
            T.mm(lambda e, ml=ml, mr=mr, mi=mi: e.matmul(ps3(Sb[sbk]), lhsT=ml, rhs=mr, start=False, stop=(mi == nm - 1)),
                 reads=mk, writes=[("G", sbk)], inc=(mi == nm - 1))
        p = nxt("P", 4)
        T.op("act", lambda e: e.activation(out=Pt[p][:, :], in_=Sb[sbk][:, :], func=ACTF.Exp),
             reads=[("G", sbk)], writes=[("P", p)])
        pend.append((p, v_ap, vkeys, ob, first, last, ncols, done))
        while len(pend) > LAG:
            emit_pv(pend.pop(0))

    build.marks = []

    def ckpt(name):
        build.marks.append((name, T.ninstr["pe"]))
        if stop == name:
            raise _Stop()

    def _main_loop():
        for st in range(NST):
            for g in range(2):
                T.dma("pool", arena[64:67, 16 + 4 * g:20 + 4 * g, :], cst["c_qaug"][:, g, st * 2048:(st + 1) * 2048].rearrange("r (c t) -> r c t", c=4),
                      writes=[("qT", g)], semkey=("qaug", g), )

            rms_phase(g1c, 0)
            xnT_keys = [("xnT", j) for j in range(4)]

            ckpt("A")
            def headnorm(bank_ap, bkey, nh, sqc, stc, dstc, par):
                w_ = nh * 64
                T.op("act", lambda e: e.activation(out=sq[:, sqc:sqc + w_], in_=bank_ap, func=ACTF.Square),
                     reads=[bkey], writes=[("sq", sqc)])
                T.op("dve", lambda e: e.tensor_reduce(out=ssq[:, stc:stc + nh], in_=sq[:, sqc:sqc + w_].rearrange("p (h d) -> p h d", d=64),
                                                      axis=AX.X, op=ALU.add),
                     reads=[("sq", sqc)], writes=[("ssq", stc)])
                T.op("dve", lambda e: e.tensor_scalar(out=rq[:, stc:stc + nh], in0=ssq[:, stc:stc + nh], scalar1=1.0 / 64, scalar2=EPS,
                                                      op0=ALU.mult, op1=ALU.add),
                     reads=[("ssq", stc)], writes=[("rq", stc)])
                T.op("pool", lambda e: e.tensor_tensor(out=rq[:, stc:stc + nh], in0=rq[:, stc:stc + nh], in1=neghalf[:, 0:nh], op=ALU.pow),
                     reads=[("rq", stc), "neghalf"], writes=[("rq", stc)])
                T.op("dve", lambda e: e.tensor_tensor(out=qk_tok2[par][:, dstc:dstc + w_].rearrange("p (h d) -> p h d", d=64),
                                                      in0=bank_ap.rearrange("p (h d) -> p h d", d=64),
                                                      in1=rq[:, stc:stc + nh].unsqueeze(2).to_broadcast([128, nh, 64]), op=ALU.mult),
                     reads=[bkey, ("rq", stc)], writes=[("qk_tok", par, dstc)])

            pipeB = []
            PLAG = 2

            def pipe2(p1, p2):
                for j in range(4):
                    k_ = pipeB_n[0]
                    pipeB_n[0] += 1
                    p1(j, k_ % 3)
                    pipeB.append((p2, j, k_ % 3))
                    while len(pipeB) > PLAG:
                        f_, j_, par_ = pipeB.pop(0)
                        f_(j_, par_)
            pipeB_n = [0]

            sA, kA = wnext()
            wA = sA[:, 0:4096].rearrange("p (k c) -> p k c", k=8)

            def a1(j, par):
                jsl = slice(j * 128, (j + 1) * 128)
                gb = nxt("G", NG)
                for k in range(KC):
                    T.mm(lambda e, k=k: e.matmul(Gb[gb][:, :], lhsT=xnT[:, k, jsl], rhs=wA[:, k, :], start=(k == 0), stop=(k == KC - 1)),
                         reads=[("xnT", j), kA], writes=[("G", gb)], inc=(k == KC - 1))
                headnorm(Gb[gb][:, :], ("G", gb), 8, 0, 0, 0, par)

            def a2(j, par):
                jsl = slice(j * 128, (j + 1) * 128)
                tb = nxt("T", 2)
                for h in range(8):
                    T.mm(lambda e, h=h: e.transpose(Tb[tb][0:64, h * 128:(h + 1) * 128], qk_tok2[par][:, h * 64:(h + 1) * 64], ident[:, :]),
                         reads=[("qk_tok", par, 0), "ident"], writes=[("T", tb)], inc=(h == 7))
                T.op("act", lambda e: e.activation(out=qT[0:64, :, j, :, :],
                                                   in_=Tb[tb][0:64, :].rearrange("p (g h t) -> p g h t", g=2, h=4),
                                                   func=ACTF.Copy, scale=gq[:, 0:1]),
                     reads=[("T", tb), "gq"], writes=[("qT", 0), ("qT", 1)])
            pipe2(a1, a2)

            sB, kB = wnext()
            wB = sB[:, 0:4096].rearrange("p (k c) -> p k c", k=8)

            def b1(j, par):
                i = 4 * st + j
                jsl = slice(j * 128, (j + 1) * 128)
                gb = nxt("G", NG)
                for k in range(KC):
                    T.mm(lambda e, k=k: e.matmul(Gb[gb][:, :], lhsT=xnT[:, k, jsl], rhs=wB[:, k, :], start=(k == 0), stop=(k == KC - 1)),
                         reads=[("xnT", j), kB], writes=[("G", gb)], inc=(k == KC - 1))
                T.op("act", lambda e: e.activation(out=kc_tok2[par][:, :], in_=Gb[gb][:, 0:128], func=ACTF.Copy),
                     reads=[("G", gb)], writes=[("kc_tok", par)])
                T.op("act", lambda e: e.activation(out=vc_tok2[par][:, :], in_=Gb[gb][:, 128:256], func=ACTF.Copy),
                     reads=[("G", gb)], writes=[("vc_tok", par)])
                T.op("act", lambda e: e.activation(out=vs[:, i, :, 0:64], in_=Gb[gb][:, 384:512].rearrange("p (g d) -> p g d", g=2),
                                                   func=ACTF.Copy),
                     reads=[("G", gb)], writes=[("vs", i)])
                headnorm(Gb[gb][:, 256:384], ("G", gb), 2, 512, 8, 512, par)

            def b2(j, par):
                i = 4 * st + j
                isl = slice(i * 128, (i + 1) * 128)
                tb2 = nxt("T", 2)
                for h in range(2):
                    T.mm(lambda e, h=h: e.transpose(Tb[tb2][0:64, h * 128:(h + 1) * 128], qk_tok2[par][:, 512 + h * 64:512 + (h + 1) * 64], ident[:, :]),
                         reads=[("qk_tok", par, 512), "ident"], writes=[("T", tb2)], inc=False)
                T.mm(lambda e: e.transpose(Tb[tb2][:, 512:640], kc_tok2[par][:, :], ident[:, :]),
                     reads=[("kc_tok", par), "ident"], writes=[("T", tb2)], inc=False)
                T.mm(lambda e: e.transpose(Tb[tb2][:, 640:768], vc_tok2[par][:, :], ident[:, :]),
                     reads=[("vc_tok", par), "ident"], writes=[("T", tb2)], inc=True)
                T.op("act", lambda e: e.activation(out=ksT[0:64, :, isl], in_=Tb[tb2][0:64, 0:256].rearrange("p (g t) -> p g t", g=2),
                                                   func=ACTF.Copy, scale=gk[:, 1:2]),
                     reads=[("T", tb2), "gk"], writes=[("ksT", i)])
                T.op("dve", lambda e: e.tensor_copy(out=kcr[:, :, 8 * j + 1:8 * j + 9].rearrange("p r b -> p b r"),
                                                    in_=Tb[tb2][:, 512:640].rearrange("p (b r) -> p b r", r=16)),
                     reads=[("T", tb2)], writes=["kcr"])
                T.op("dve", lambda e: e.tensor_copy(out=vcr[:, :, 8 * j + 1:8 * j + 9].rearrange("p r b -> p b r"),
                                                    in_=Tb[tb2][:, 640:768].rearrange("p (b r) -> p b r", r=16)),
                     reads=[("T", tb2)], writes=["vcr"])
            pipe2(b1, b2)

            sC, kC_ = wnext()
            wC = sC[:, 0:8 * 280].rearrange("p (k c) -> p k c", k=8)

            def c1(j, par):
                i = 4 * st + j
                jsl = slice(j * 128, (j + 1) * 128)
                gb = nxt("G", NG)
                for k in range(KC):
                    T.mm(lambda e, k=k: e.matmul(Gb[gb][:, 0:280], lhsT=xnT[:, k, jsl], rhs=wC[:, k, :], start=(k == 0), stop=(k == KC - 1)),
                         reads=[("xnT", j), kC_], writes=[("G", gb)], inc=(k == KC - 1))
                T.op("act", lambda e: e.activation(out=vw[:, i, :, 0:64], in_=Gb[gb][:, 128:256].rearrange("p (g d) -> p g d", g=2),
                                                   func=ACTF.Copy),
                     reads=[("G", gb)], writes=[("vw", i)])
                T.op("act", lambda e: e.activation(out=tg[:, j, :], in_=Gb[gb][:, 256:280], func=ACTF.Tanh, scale=0.5),
                     reads=[("G", gb)], writes=[("tg", j)])
                headnorm(Gb[gb][:, 0:128], ("G", gb), 2, 640, 10, 640, par)

            def c2(j, par):
                i = 4 * st + j
                isl = slice(i * 128, (i + 1) * 128)
                tb3 = nxt("T", 2)
                for h in range(2):
                    T.mm(lambda e, h=h: e.transpose(Tb[tb3][0:64, h * 128:(h + 1) * 128], qk_tok2[par][:, 640 + h * 64:640 + (h + 1) * 64], ident[:, :]),
                         reads=[("qk_tok", par, 640), "ident"], writes=[("T", tb3)], inc=(h == 1))
                T.op("act", lambda e: e.activation(out=kwT[0:64, :, isl], in_=Tb[tb3][0:64, 0:256].rearrange("p (g t) -> p g t", g=2),
                                                   func=ACTF.Copy, scale=gk[:, 2:3]),
                     reads=[("T", tb3), "gk"], writes=[("kwT", i)])
            pipe2(c1, c2)
            while pipeB:
                f_, j_, par_ = pipeB.pop(0)
                f_(j_, par_)

            ckpt("B")
            c0 = max(0, 32 * st - 1)
            c1 = 32 * st + 30
            n = c1 - c0 + 1
            bl0 = c0 - 32 * st + 1
            for kv, (src, srck) in enumerate(((kcr, "kcr"), (vcr, "vcr"))):
                s_, sk = wnext()
                wbd = s_[:, 0:4096].rearrange("p (l e) -> p l e", l=32)
                gb = nxt("G", NG)
                for l in range(32):
                    T.mm(lambda e, l=l: e.matmul(Gb[gb][0:n, 0:128], lhsT=src[:, l % 16, bl0 + l // 16:bl0 + l // 16 + n],
                                                 rhs=wbd[:, l, :], start=(l == 0), stop=False),
                         reads=[srck, sk], writes=[("G", gb)], inc=False)
                T.mm(lambda e: e.matmul(Gb[gb][0:n, 0:128], lhsT=onesrow[0:1, 0:n],
                                        rhs=biasrow[0:1, kv, :, :].rearrange("p g e -> p (g e)"), start=False, stop=True),
                     reads=["onesrow", "biasrow"], writes=[("G", gb)], inc=True)
                if kv == 1:
                    T.op("act", lambda e: e.activation(out=vtmp[0:n, :, :], in_=Gb[gb][0:n, 0:128].rearrange("p (g d) -> p g d", g=2),
                                                       func=ACTF.Copy),
                         reads=[("G", gb)], writes=["vtmp"])
                    for ct in range(NCT):
                        lo = max(c0, 128 * ct)
                        hi = min(c1, 128 * ct + 127)
                        if lo > hi:
                            continue
                        T.dma("pool", vca[lo - 128 * ct:hi - 128 * ct + 1, ct, :, 0:64], vtmp[lo - c0:hi - c0 + 1, :, :],
                              reads=["vtmp"], writes=["vca"], semkey="vcadma")
                else:
                    T.op("act", lambda e: e.activation(out=sq[0:n, 0:128], in_=Gb[gb][0:n, 0:128], func=ACTF.Square),
                         reads=[("G", gb)], writes=["sq_q"])
                    T.op("dve", lambda e: e.tensor_reduce(out=ssq[0:n, 12:14], in_=sq[0:n, 0:128].rearrange("p (h d) -> p h d", d=64),
                                                          axis=AX.X, op=ALU.add),
                         reads=["sq_q"], writes=["ssqc"])
                    T.op("dve", lambda e: e.tensor_scalar(out=rq[0:n, 12:14], in0=ssq[0:n, 12:14], scalar1=1.0 / 64, scalar2=EPS,
                                                          op0=ALU.mult, op1=ALU.add),
                         reads=["ssqc"], writes=["rqc"])
                    T.op("pool", lambda e: e.tensor_tensor(out=rq[0:n, 12:14], in0=rq[0:n, 12:14], in1=neghalf[0:n, 0:2], op=ALU.pow),
                         reads=["rqc", "neghalf"], writes=["rqc"])
                    T.op("dve", lambda e: e.tensor_tensor(out=kc_tok[0:n, :].rearrange("p (h d) -> p h d", d=64),
                                                          in0=Gb[gb][0:n, 0:128].rearrange("p (h d) -> p h d", d=64),
                                                          in1=rq[0:n, 12:14].unsqueeze(2).to_broadcast([n, 2, 64]), op=ALU.mult),
                         reads=[("G", gb), "rqc"], writes=[("kc_tok", 0)])
                    tb = nxt("T", 2)
                    for g in range(2):
                        T.mm(lambda e, g=g: e.transpose(Tb[tb][0:64, g * 128:g * 128 + n], kc_tok[0:n, g * 64:(g + 1) * 64], ident[0:n, 0:n]),
                             reads=[("kc_tok", 0), "ident"], writes=[("T", tb)], inc=(g == 1))
                    T.op("act", lambda e: e.activation(out=kcT[0:64, :, c0:c0 + n],
                                                       in_=Tb[tb][0:64, 0:256].rearrange("p (g t) -> p g t", g=2)[:, :, 0:n],
                                                       func=ACTF.Copy, scale=gk[:, 0:1]),
                         reads=[("T", tb), "gk"], writes=["kcT"])
            T.op("dve", lambda e: e.tensor_copy(out=kcr[:, :, 0:1], in_=kcr[:, :, 32:33]), reads=["kcr"], writes=["kcr"])
            T.op("dve", lambda e: e.tensor_copy(out=vcr[:, :, 0:1], in_=vcr[:, :, 32:33]), reads=["vcr"], writes=["vcr"])

            ckpt("C")
            def cmp_done(j, g, ob):
                i = 4 * st + j
                pr = j % 2
                Oc, sm, imp, score, work, m8, selb = Oc2[pr], sm2[pr], imp2[pr], score2[pr], work2[pr], m82[pr], selb2[pr]
                tsl = slice(64 - 2 * i, 128 - 2 * i)
                T.op("act", lambda e: e.activation(out=Oc[:, g * 4:(g + 1) * 4, :], in_=Ob[ob][:, :].rearrange("p (h c) -> p h c", h=4),
                                                   func=ACTF.Copy),
                     reads=[("G", 3 + ob)], writes=[("Oc", pr, g)])
                T.op("dve", lambda e: e.tensor_reduce(out=sm[:, g * 4:(g + 1) * 4], in_=Oc[:, g * 4:(g + 1) * 4, 64:128],
                                                      axis=AX.X, op=ALU.add),
                     reads=[("Oc", pr, g)], writes=[("denc", pr, g)])
                T.op("dve", lambda e: e.tensor_scalar(out=sm[:, g * 4:(g + 1) * 4], in0=sm[:, g * 4:(g + 1) * 4], scalar1=1e-30,
                                                      scalar2=None, op0=ALU.max),
                     reads=[("denc", pr, g)], writes=[("denc", pr, g)])
                T.op("dve", lambda e: e.reciprocal(out=sm[:, 8 + g * 4:8 + (g + 1) * 4], in_=sm[:, g * 4:(g + 1) * 4]),
                     reads=[("denc", pr, g)], writes=[("rc", pr, g)])
                for h in range(4):
                    hh = g * 4 + h
                    if h == 0:
                        T.op("dve", lambda e: e.tensor_scalar(out=imp[:, g, :], in0=Oc[:, hh, 64:128], scalar1=sm[:, 8 + hh:9 + hh],
                                                              scalar2=None, op0=ALU.mult),
                             reads=[("Oc", pr, g), ("rc", pr, g)], writes=[("imp", pr, g)])
                    else:
                        T.op("dve", lambda e: e.scalar_tensor_tensor(out=imp[:, g, :], in0=Oc[:, hh, 64:128], scalar=sm[:, 8 + hh:9 + hh],
                                                                     in1=imp[:, g, :], op0=ALU.mult, op1=ALU.add),
                             reads=[("Oc", pr, g), ("rc", pr, g), ("imp", pr, g)], writes=[("imp", pr, g)])
                T.op("dve", lambda e: e.tensor_tensor(out=score[:, g, :], in0=imp[:, g, :], in1=TA[:, tsl], op=ALU.mult),
                     reads=[("imp", pr, g), "TA"], writes=[("score", pr, g)])
                T.op("dve", lambda e: e.tensor_tensor(out=score[:, g, :], in0=score[:, g, :], in1=TB[:, tsl], op=ALU.add),
                     reads=[("score", pr, g), "TB"], writes=[("score", pr, g)])
                T.op("dve", lambda e: e.memset(score[:, g, 0:1], 1e4), reads=[("score", pr, g)], writes=[("score", pr, g)])
                T.op("dve", lambda e: e.max(out=m8[:, g, 0:8], in_=score[:, g, :]), reads=[("score", pr, g)], writes=[("m8", pr, g)])
                if NSEL == 16:
                    T.op("dve", lambda e: e.match_replace(out=work[:, g, :], in_to_replace=m8[:, g, 0:8], in_values=score[:, g, :],
                                                          imm_value=-2.0),
                         reads=[("score", pr, g), ("m8", pr, g)], writes=[("work", pr, g)])
                    T.op("dve", lambda e: e.max(out=m8[:, g, 8:16], in_=work[:, g, :]), reads=[("work", pr, g)], writes=[("m8b", pr, g)])
                    thr_ap, thrk = m8[:, g, 15:16], "m8b"
                else:
                    thr_ap, thrk = m8[:, g, 7:8], "m8"
                T.op("dve", lambda e: e.tensor_scalar(out=selb[:, g, :], in0=score[:, g, :], scalar1=thr_ap, scalar2=None, op0=ALU.is_ge),
                     reads=[("score", pr, g), (thrk, pr, g)], writes=[("selb", pr, g)])
                sel_done[(i, g)] = True

            def sel_transpose(j, g):
                pr = j % 2
                tb = nxt("T", 2)
                T.mm(lambda e: e.transpose(Tb[tb][0:64, 0:128], selb2[pr][:, g, :], ident[:, :]),
                     reads=[("selb", pr, g), "ident"], writes=[("T", tb)])
                T.op("dve", lambda e: e.tensor_scalar(out=selneg[0:64, g, :], in0=Tb[tb][0:64, 0:128], scalar1=-NEG, scalar2=NEG,
                                                      op0=ALU.mult, op1=ALU.add),
                     reads=[("T", tb)], writes=["selneg%d" % g])

            def win_done(j, g, ob):
                T.op("act", lambda e: e.activation(out=Ow[:, g * 4:(g + 1) * 4, :],
                                                   in_=Ob[ob][:, 0:260].rearrange("p (h c) -> p h c", h=4), func=ACTF.Copy),
                     reads=[("G", 3 + ob)], writes=[("Ow", g)])

            def slc_done(j, g, ob):
                pr = j % 2
                Oc, sm = Oc2[pr], sm2[pr]
                jsl = slice(j * 128, (j + 1) * 128)
                T.op("act", lambda e: e.activation(out=Os[:, g * 4:(g + 1) * 4, :],
                                                   in_=Ob[ob][:, 0:260].rearrange("p (h c) -> p h c", h=4), func=ACTF.Copy),
                     reads=[("G", 3 + ob)], writes=[("Os", g)])
                if g == 0:
                    return
                T.op("dve", lambda e: e.tensor_scalar(out=sm[:, 16:40], in0=tg[:, j, :], scalar1=1.0, scalar2=0.5, op0=ALU.add, op1=ALU.mult),
                     reads=[("tg", j)], writes=["sg"])
                T.op("dve", lambda e: e.reciprocal(out=sm[:, 40:48], in_=Os[:, :, 64:65].rearrange("p h o -> p (h o)")),
                     reads=[("Os", 0), ("Os", 1)], writes=["rs"])
                T.op("dve", lambda e: e.reciprocal(out=sm[:, 48:56], in_=Ow[:, :, 64:65].rearrange("p h o -> p (h o)")),
                     reads=[("Ow", 0), ("Ow", 1)], writes=["rw"])
                T.op("dve", lambda e: e.tensor_tensor(out=sm[:, 16:24], in0=sm[:, 16:24], in1=sm[:, 8:16], op=ALU.mult),
                     reads=["sg", ("rc", pr, 0), ("rc", pr, 1)], writes=["fc"])
                T.op("dve", lambda e: e.tensor_tensor(out=sm[:, 24:40], in0=sm[:, 24:40], in1=sm[:, 40:56], op=ALU.mult),
                     reads=["sg", "rs", "rw"], writes=["fsw"])
                T.op("dve", lambda e: e.tensor_tensor(out=otmp[:, 0, :].rearrange("p (h d) -> p h d", d=64), in0=Oc[:, :, 0:64],
                                                      in1=sm[:, 16:24].unsqueeze(2).to_broadcast([128, 8, 64]), op=ALU.mult),
                     reads=[("Oc", pr, 0), ("Oc", pr, 1), "fc"], writes=[("ta", 0)])
                T.op("dve", lambda e: e.tensor_tensor(out=otmp[:, 1, :].rearrange("p (h d) -> p h d", d=64), in0=Os[:, :, 0:64],
                                                      in1=sm[:, 24:32].unsqueeze(2).to_broadcast([128, 8, 64]), op=ALU.mult),
                     reads=[("Os", 0), ("Os", 1), "fsw"], writes=[("ta", 1)])
                T.op("dve", lambda e: e.tensor_tensor(out=otmp[:, 0, :], in0=otmp[:, 0, :], in1=otmp[:, 1, :], op=ALU.add),
                     reads=[("ta", 0), ("ta", 1)], writes=[("ta", 0)])
                T.op("dve", lambda e: e.tensor_tensor(out=otmp[:, 1, :].rearrange("p (h d) -> p h d", d=64), in0=Ow[:, :, 0:64],
                                                      in1=sm[:, 32:40].unsqueeze(2).to_broadcast([128, 8, 64]), op=ALU.mult),
                     reads=[("Ow", 0), ("Ow", 1), "fsw"], writes=[("ta", 1)])
                T.op("dve", lambda e: e.tensor_tensor(out=o_tok[:, :], in0=otmp[:, 0, :], in1=otmp[:, 1, :], op=ALU.add),
                     reads=[("ta", 0), ("ta", 1)], writes=["o_tok"])

                def fin():
                    tb = nxt("T", 2)
                    for k in range(4):
                        T.mm(lambda e, k=k: e.transpose(Tb[tb][:, k * 128:(k + 1) * 128], o_tok[:, k * 128:(k + 1) * 128], ident[:, :]),
                             reads=["o_tok", "ident"], writes=[("T", tb)], inc=(k == 3))
                    T.op("act", lambda e: e.activation(out=o_nsaT[:, :, jsl], in_=Tb[tb][:, 0:512].rearrange("p (k t) -> p k t", k=4),
                                                       func=ACTF.Copy),
                         reads=[("T", tb)], writes=[("o_nsaT", j)])
                defer_pe.append(fin)

            def C_blocks(j):
                i = 4 * st + j
                jsl = slice(j * 128, (j + 1) * 128)
                nct_vis = min(NCT, (8 * i + 6) // 128 + 1)
                for g in range(2):
                    ob = nxt("O", 2)
                    for ct in range(nct_vis):
                        masks = []
                        ut = i - 16 * ct
                        if ut < 17:
                            masks.append((ident[:, :], bc4(wm[:, ut * 128:(ut + 1) * 128]), ["ident", "wm"]))
                        lastb = (ct == nct_vis - 1)
                        attn_block(g, jsl, kcT[0:67, g, ct * 128:(ct + 1) * 128], ["kcT"], masks,
                                   vca[:, ct, g, :], ["vca"], ob, ct == 0, lastb, 128,
                                   done=(lambda j=j, g=g, ob=ob: cmp_done(j, g, ob)) if lastb else None)

            def W_blocks(j):
                i = 4 * st + j
                jsl = slice(j * 128, (j + 1) * 128)
                for g in range(2):
                    ob = nxt("O", 2)
                    kts = list(range(max(0, i - 4), i + 1))
                    for kt in kts:
                        masks = []
                        if kt == i:
                            masks.append((ident[:, :], bc4(caus[:, 0, :]), ["ident", "caus"]))
                        if kt == i - 4:
                            masks.append((ident[:, :], bc4(caus[:, 1, :]), ["ident", "caus"]))
                        lastb = (kt == kts[-1])
                        attn_block(g, jsl, kwT[0:67, g, kt * 128:(kt + 1) * 128], [("kwT", kt), "kwTaug"], masks,
                                   vw[:, kt, g, :], [("vw", kt), "vw1"], ob, kt == kts[0], lastb, 65,
                                   done=(lambda j=j, g=g, ob=ob: win_done(j, g, ob)) if lastb else None)
                    if g == 0:
                        for g2 in range(2):
                            if (i, g2) in sel_done:
                                sel_transpose(j, g2)
                                selT_done[(i, g2)] = True

            def S_blocks(j):
                i = 4 * st + j
                jsl = slice(j * 128, (j + 1) * 128)
                for g in range(2):
                    if (i, g) not in selT_done:
                        if (i, g) not in sel_done:
                            flush_pv()
                        sel_transpose(j, g)
                    if g == 1:
                        while defer_pe:
                            defer_pe.pop(0)()
                    ob = nxt("O", 2)
                    for kt in range(i + 1):
                        if kt == i:
                            masks = [(ident[:, :], bc4(caus[:, 0, :]), ["ident", "caus"])]
                        else:
                            masks = [(cE[:, kt * 128:(kt + 1) * 128], bc4(selneg[:, g, :]), ["cE", "selneg%d" % g])]
                        lastb = (kt == i)
                        attn_block(g, jsl, ksT[0:67, g, kt * 128:(kt + 1) * 128], [("ksT", kt), "ksTaug"], masks,
                                   vs[:, kt, g, :], [("vs", kt), "vs1"], ob, kt == 0, lastb, 65,
                                   done=(lambda j=j, g=g, ob=ob: slc_done(j, g, ob)) if lastb else None)

            C_blocks(0)
            for j in range(4):
                W_blocks(j)
                if j < 3:
                    C_blocks(j + 1)
                S_blocks(j)
            flush_pv()
            onsa_keys = [("o_nsaT", j) for j in range(4)]

            ckpt("D")
            for cc in range(8):
                s, sk = wnext()
                wv = s[:, 0:3072].rearrange("p (k r c) -> p k r c", k=8, r=3)
                gbs = []
                for r in (1, 2, 0):
                    gb = nxt("G", NG)
                    for k in range(KC):
                        T.mm(lambda e, k=k, r=r, gb=gb: e.matmul(Gb[gb][:, :], lhsT=wv[:, k, r, :], rhs=xnT[:, k, :], start=(k == 0), stop=(k == KC - 1)),
                             reads=xnT_keys + [sk], writes=[("G", gb)], inc=(k == KC - 1))
                    if r == 1:
                        T.op("act", lambda e, gb=gb: e.activation(out=cxt[:, :], in_=Gb[gb][:, :], func=ACTF.Copy),
                             reads=[("G", gb)], writes=["cxt"])
                    elif r == 2:
                        T.op("dve", lambda e: e.tensor_copy(out=ubuf[:, 0:2], in_=halo[:, cc, :]), reads=["halo"], writes=["ubufh"])
                        T.op("dve", lambda e, gb=gb: e.tensor_tensor(out=ubuf[:, 2:514], in0=Gb[gb][:, :], in1=cxt[:, :], op=ALU.mult),
                             reads=[("G", gb), "cxt"], writes=["ubuf"])
                        T.op("dve", lambda e: e.tensor_copy(out=halo[:, cc, :], in_=ubuf[:, 512:514]), reads=["ubuf"], writes=["halo"])
                        T.op("dve", lambda e: e.tensor_scalar(out=ybuf[:, :], in0=ubuf[:, 2:514], scalar1=convw[:, cc, 2:3], scalar2=None, op0=ALU.mult),
                             reads=["ubuf", "convw"], writes=["ybuf"])
                        T.op("dve", lambda e: e.scalar_tensor_tensor(out=ybuf[:, :], in0=ubuf[:, 1:513], scalar=convw[:, cc, 1:2], in1=ybuf[:, :],
                                                                     op0=ALU.mult, op1=ALU.add),
                             reads=["ubuf", "ubufh", "convw", "ybuf"], writes=["ybuf"])
                        T.op("dve", lambda e: e.scalar_tensor_tensor(out=ybuf[:, :], in0=ubuf[:, 0:512], scalar=convw[:, cc, 0:1], in1=ybuf[:, :],
                                                                     op0=ALU.mult, op1=ALU.add),
                             reads=["ubuf", "ubufh", "convw", "ybuf"], writes=["ybuf"])
                    else:
                        T.op("dve", lambda e, gb=gb: e.tensor_tensor(out=zT[:, cc, :], in0=Gb[gb][:, :], in1=ybuf[:, :], op=ALU.mult),
                             reads=[("G", gb), "ybuf"], writes=[("zT", cc)])
                if cc == 1:
                    while defer_pe:
                        defer_pe.pop(0)()
            zT_keys = [("zT", cc) for cc in range(8)]

            ckpt("E")
            for m in range(8):
                s, sk = wnext()
                wg = s[:, 0:2048].rearrange("p (k r c) -> p k r c", k=8, r=2)
                wa = s[:, 2048:2560].rearrange("p (k c) -> p k c", k=4)
                wb = s[:, 2560:3584].rearrange("p (k c) -> p k c", k=8)
                for r in range(2):
                    gb = nxt("G", NG)
                    for k in range(KC):
                        T.mm(lambda e, k=k, r=r, gb=gb: e.matmul(Gb[gb][:, :], lhsT=wg[:, k, r, :], rhs=xnT[:, k, :], start=(k == 0), stop=(k == KC - 1)),
                             reads=xnT_keys + [sk], writes=[("G", gb)], inc=(k == KC - 1))
                    T.op("act", lambda e, r=r, gb=gb: e.activation(out=ta[:, r, :], in_=Gb[gb][:, :], func=ACTF.Tanh, scale=0.5),
                         reads=[("G", gb)], writes=[("ta", r)])
                    gb2 = nxt("G", NG)
                    if r == 0:
                        for k in range(4):
                            T.mm(lambda e, k=k, gb2=gb2: e.matmul(Gb[gb2][:, :], lhsT=wa[:, k, :], rhs=o_nsaT[:, k, :], start=(k == 0), stop=(k == 3)),
                                 reads=onsa_keys + [sk], writes=[("G", gb2)], inc=(k == 3))
                    else:
                        for k in range(KC):
                            T.mm(lambda e, k=k, gb2=gb2: e.matmul(Gb[gb2][:, :], lhsT=wb[:, k, :], rhs=zT[:, k, :], start=(k == 0), stop=(k == KC - 1)),
                                 reads=zT_keys + [sk], writes=[("G", gb2)], inc=(k == KC - 1))
                    T.op("dve", lambda e, r=r, gb2=gb2: e.scalar_tensor_tensor(out=mt[:, r, :], in0=ta[:, r, :], scalar=1.0, in1=Gb[gb2][:, :],
                                                                               op0=ALU.add, op1=ALU.mult),
                         reads=[("ta", r), ("G", gb2)], writes=[("mt", r)])
                T.op("dve", lambda e: e.tensor_tensor(out=mixedT[:, m, :], in0=mt[:, 0, :], in1=mt[:, 1, :], op=ALU.add),
                     reads=[("mt", 0), ("mt", 1)], writes=[("mixedT", m)])
            mixed_keys = [("mixedT", m) for m in range(8)]

            ckpt("F")
            s0, sk0 = wnext()
            s1, sk1 = wnext(hold=1)
            wos = [(s0[:, 0:4096].rearrange("p (k c) -> p k c", k=8), sk0), (s1[:, 0:4096].rearrange("p (k c) -> p k c", k=8), sk1)]
            slots_ = {}
            for j in range(4):
                jsl = slice(j * 128, (j + 1) * 128)
                for hh in range(2):
                    wo, sk = wos[hh]
                    gb = nxt("G", NG)
                    for k in range(KC):
                        T.mm(lambda e, k=k, gb=gb: e.matmul(Gb[gb][:, :], lhsT=mixedT[:, k, jsl], rhs=wo[:, k, :], start=(k == 0), stop=(k == KC - 1)),
                             reads=mixed_keys + [sk], writes=[("G", gb)], inc=(k == KC - 1))
                    T.op("dve", lambda e, gb=gb, j=j: e.scalar_tensor_tensor(out=xt[:, j, hh * 512:(hh + 1) * 512], in0=Gb[gb][:, :], scalar=0.5,
                                                                             in1=xt[:, j, hh * 512:(hh + 1) * 512], op0=ALU.mult, op1=ALU.add),
                         reads=[("G", gb), ("xt", j, hh)], writes=[("xt", j, hh)])
                slots_[j] = rms_chain(j, 4 + j)
                if j >= 1:
                    rms_tr(slots_[j - 1], j - 1, g2c)
            rms_tr(slots_[3], 3, g2c)

            ckpt("G")
            ckpt("H")
            for fs in range(8):
                s, sk = wnext()
                wu = s[:, 0:4096].rearrange("p (k c) -> p k c", k=8)
                for fl in range(4):
                    fc = fs * 4 + fl
                    gb = nxt("G", NG)
                    for k in range(KC):
                        T.mm(lambda e, k=k, gb=gb: e.matmul(Gb[gb][:, :], lhsT=wu[:, k, fl * 128:(fl + 1) * 128], rhs=xnT[:, k, :],
                                                            start=(k == 0), stop=(k == KC - 1)),
                             reads=xnT_keys + [sk], writes=[("G", gb)], inc=(k == KC - 1))
                    r = fc % 2
                    T.op("act", lambda e, gb=gb, r=r: e.activation(out=rl[:, r, :], in_=Gb[gb][:, :], func=ACTF.Relu),
                         reads=[("G", gb)], writes=[("rl", r)])
                    T.op("dve", lambda e, r=r, fc=fc: e.tensor_tensor(out=actT[:, fc, :], in0=rl[:, r, :], in1=rl[:, r, :], op=ALU.mult),
                         reads=[("rl", r)], writes=[("actT", fc)] + alias(fc))
            act_keys = [("actT", fc) for fc in range(32)]

            ckpt("I")
            for hh in range(2):
                accs = [(Gb[(hh * 4 + q_) % NG], ("G", (hh * 4 + q_) % NG)) for q_ in range(4)]
                for fq in range(4):
                    s, sk = wnext()
                    wd = s[:, 0:4096].rearrange("p (k c) -> p k c", k=8)
                    for fl in range(8):
                        fc = fq * 8 + fl
                        for j in range(4):
                            jsl = slice(j * 128, (j + 1) * 128)
                            T.mm(lambda e, fl=fl, fc=fc, j=j: e.matmul(accs[j][0][:, :], lhsT=actT[:, fc, jsl], rhs=wd[:, fl, :],
                                                                       start=(fc == 0), stop=(fc == 31)),
                                 reads=[("actT", fc), sk] + alias(fc), writes=[accs[j][1]], inc=(fl == 7 and j == 3) or fc == 31)
                for j in range(4):
                    i = 4 * st + j
                    o_ = nxt("OB", 4)
                    T.op("dve", lambda e, j=j: e.tensor_tensor(out=obuf[:, o_, :], in0=accs[j][0][:, :],
                                                               in1=xt[:, j, hh * 512:(hh + 1) * 512], op=ALU.add),
                         reads=[accs[j][1], ("xt", j, hh)], writes=[("ob", o_)])
                    T.dma("pool", y_d[i * 128:(i + 1) * 128, hh * 512:(hh + 1) * 512], obuf[:, o_, :], reads=[("ob", o_)],
                          semkey=("ob", o_))
                if st + 1 < NST:
                    load_x(st + 1, hh, "sp")

            ckpt("J")

    try:
        ckpt("setup")
        _main_loop()
    except _Stop:
        for sk_, v_ in T.cnt.items():
            if v_ > 0:
                nc.gpsimd.wait_ge(T.sem[sk_], v_)
        dumpable = {"xnT": xnT[:, :, :], "qT": qT, "ksT": ksT[:, :, :], "kwT": kwT[:, :, :], "vs": vs[:, :, :, :],
                    "vw": vw[:, :, :, :], "kcT": kcT[:, :, :], "vca": vca[:, :, :, :], "o_nsaT": o_nsaT, "tg": tg[:, :, :],
                    "zT": zT, "mixedT": mixedT, "Oc": Oc2[0][:, :, :], "Ow": Ow[:, :, :], "Os": Os[:, :, :], "imp": imp2[0][:, :, :],
                    "selb": selb2[0][:, :, :], "score": score2[0][:, :, :], "selneg": selneg[:, :, :], "o_tok": o_tok[:, :],
                    "actT": arena[:, :, :], "kcr": kcr[:, :, :], "sm": sm2[0][:, :], "m8": m82[0][:, :, :]}
        for name in dump:
            ap_ = dumpable[name]
            od = nc.dram_tensor("dbg_" + name, list(ap_.shape), F32, kind="ExternalOutput").ap()
            T.dma("pool", od, ap_, semkey="dbg")
        if dump:
            nc.gpsimd.wait_ge(T.sem["dbg"], T.cnt["dbg"])
        for j in range(4):
            T.dma("pool", y_d[j * 128:(j + 1) * 128, :], xt[:, j, :], reads=[("xt", j, 0), ("xt", j, 1)], semkey=("ob", j % 3))

    for o_ in range(4):
        if ("ob", o_) in T.sem:
            nc.gpsimd.wait_ge(T.sem[("ob", o_)], T.cnt[("ob", o_)])
    build.stats = dict(T.ninstr)
    return nc


def _host_inputs(S, x_b, p):
    d = {"x": np.ascontiguousarray(x_b, dtype=np.float32)}
    d.update(p)
    return d


def _pack_jobs(w_in, w_ba, w_bb, w_out, w_up, w_down, wck, wcv):
    J = np.zeros((39, 128, SLOT), np.float32)

    def kp(a):
        k = a.shape[0] // 128
        return a.reshape(k, 128, a.shape[1]).transpose(1, 0, 2)

    J[0, :, 0:4096] = kp(w_in[:, 0:512]).reshape(128, -1)
    J[1, :, 0:4096] = kp(w_in[:, 512:1024]).reshape(128, -1)
    J[2, :, 0:2240] = kp(w_in[:, 1024:1304]).reshape(128, -1)
    for kv, w in enumerate((wck, wcv)):
        w3 = w.reshape(32, 64, 64)
        blk = np.zeros((128, 32, 128), np.float32)
        for g in range(2):
            blk[g * 64:(g + 1) * 64, :, g * 64:(g + 1) * 64] = w3.transpose(1, 0, 2)
        J[3 + kv, :, 0:4096] = blk.reshape(128, -1)
    for cc in range(8):
        t = np.zeros((128, 8, 3, 128), np.float32)
        for r, c0 in enumerate((C_CB, C_CC, C_CX)):
            t[:, :, r, :] = kp(w_in[:, c0 + cc * 128:c0 + (cc + 1) * 128])
        J[5 + cc, :, 0:3072] = t.reshape(128, -1)
    for m in range(8):
        t = np.zeros((128, 8, 2, 128), np.float32)
        for r, c0 in enumerate((C_GA, C_GB)):
            t[:, :, r, :] = kp(w_in[:, c0 + m * 128:c0 + (m + 1) * 128])
        J[13 + m, :, 0:2048] = t.reshape(128, -1)
        J[13 + m, :, 2048:2560] = kp(w_ba[:, m * 128:(m + 1) * 128]).reshape(128, -1)
        J[13 + m, :, 2560:3584] = kp(w_bb[:, m * 128:(m + 1) * 128]).reshape(128, -1)
    for hh in range(2):
        J[21 + hh, :, 0:4096] = kp(w_out[:, hh * 512:(hh + 1) * 512]).reshape(128, -1)
    for fs in range(8):
        J[23 + fs, :, 0:4096] = kp(w_up[:, fs * 512:(fs + 1) * 512]).reshape(128, -1)
    for hh in range(2):
        for fq in range(4):
            J[31 + hh * 4 + fq, :, 0:4096] = kp(w_down[fq * 1024:(fq + 1) * 1024, hh * 512:(hh + 1) * 512]).reshape(128, -1)
    return J


def _param_layout(norm1_g, w_in, q_norm_g, k_norm_g, cmp_pos_k, cmp_pos_v, w_cmp_k, w_cmp_v, conv_w,
                  w_branch_a, w_branch_b, w_out, norm2_g, w_up, w_down, l=0):
    f = lambda a: np.ascontiguousarray(np.asarray(a), dtype=np.float32)
    p = {
        "wjobs": _pack_jobs(f(w_in[l]), f(w_branch_a[l]), f(w_branch_b[l]), f(w_out[l]), f(w_up[l]), f(w_down[l]),
                            f(w_cmp_k[l]), f(w_cmp_v[l])),
        "w_cmp_k": f(w_cmp_k[l]), "w_cmp_v": f(w_cmp_v[l]),
        "g1c": f(np.asarray(norm1_g[l]).reshape(8, 128).T), "g2c": f(np.asarray(norm2_g[l]).reshape(8, 128).T),
        "gq": f(np.asarray(q_norm_g[l]).reshape(64, 1)), "gk": f(np.asarray(k_norm_g[l]).T),
        "posk": f(np.asarray(cmp_pos_k[l]).reshape(16, 128).T), "posv": f(np.asarray(cmp_pos_v[l]).reshape(16, 128).T),
        "convw": f(np.asarray(conv_w[l]).reshape(3, 8, 128).transpose(2, 1, 0)),
    }
    return p


def kernel(x, norm1_g, w_in, q_norm_g, k_norm_g, cmp_pos_k, cmp_pos_v, w_cmp_k, w_cmp_v, conv_w,
           w_branch_a, w_branch_b, w_out, norm2_g, w_up, w_down):
    x = np.asarray(x)
    B, S, _ = x.shape
    p = _param_layout(norm1_g, w_in, q_norm_g, k_norm_g, cmp_pos_k, cmp_pos_v, w_cmp_k, w_cmp_v, conv_w,
                      w_branch_a, w_branch_b, w_out, norm2_g, w_up, w_down)
    p.update(make_consts(S))
    nc = build(S)
    in_maps = [_host_inputs(S, x[b], p) for b in range(B)]
    res = run_bass_kernel_spmd(nc, in_maps, core_ids=list(range(B)))
    return np.stack([np.asarray(r["y"], dtype=np.float32) for r in res.results], axis=0)
```

```python
import numpy as np
import concourse.bass as bass
import concourse.mybir as mybir
from concourse.bass_utils import run_bass_kernel_spmd

F32 = mybir.dt.float32
BF16 = mybir.dt.bfloat16
ALU = mybir.AluOpType
ACTF = mybir.ActivationFunctionType
AX = mybir.AxisListType

D = 1024
KC = 8
PW = 6424
DFF = 4096
EPS = 1e-6
NEG = -30000.0
(C_Q, C_KC, C_VC, C_KS, C_VS, C_KW, C_VW, C_GN, C_CB, C_CC, C_CX, C_GA, C_GB) = (
    0, 512, 640, 768, 896, 1024, 1152, 1280, 1304, 2328, 3352, 4376, 5400)
SLOT = 4096
NSLOTS = 3
import os as _os
CASTE = int(_os.environ.get("CASTE", str(1 << 17)))


class Tracker:
    def __init__(self, nc):
        self.nc = nc
        self.eng = {"pe": nc.tensor, "act": nc.scalar, "dve": nc.vector,
                    "pool": nc.gpsimd, "sp": nc.sync}
        self.sem = {}
        self.cnt = {}
        for k in self.eng:
            self.sem[k] = nc.alloc_semaphore("s_" + k)
            self.cnt[k] = 0
        self.seen = {k: {} for k in self.eng}
        self.lastw = {}
        self.readers = {}
        self.ninstr = {k: 0 for k in self.eng}

    def _sem(self, key):
        if key not in self.sem:
            self.sem[key] = self.nc.alloc_semaphore("d%d" % len(self.sem))
            self.cnt[key] = 0
        return self.sem[key]

    def _deps(self, ek, reads, writes):
        deps = {}

        def add(h, same_ok):
            if h is None:
                return
            sk, v = h
            if sk == ek and ek == "pe":
                return
            if deps.get(sk, 0) < v:
                deps[sk] = v

        for r in reads:
            add(self.lastw.get(r), True)
            if isinstance(r, tuple) and r[0] in ("G", "S", "O", "T"):
                for sk, v in self.readers.get(r, {}).items():
                    if sk != ek:
                        add((sk, v), False)
        for w in writes:
            add(self.lastw.get(w), False)
            for sk, v in self.readers.get(w, {}).items():
                add((sk, v), False)
        return deps

    def _emit_waits(self, ek, deps):
        e = self.eng[ek]
        seen = self.seen[ek]
        for sk, v in deps.items():
            if seen.get(sk, 0) >= v:
                continue
            e.wait_ge(self.sem[sk], v)
            seen[sk] = v

    def _record(self, h, reads, writes):
        sk, v = h
        for r in reads:
            d = self.readers.setdefault(r, {})
            if d.get(sk, 0) < v:
                d[sk] = v
        for w in writes:
            self.lastw[w] = h
            self.readers[w] = {}

    def op(self, ek, fn, reads=(), writes=()):
        deps = self._deps(ek, reads, writes)
        self._emit_waits(ek, deps)
        ins = fn(self.eng[ek])
        self.cnt[ek] += 1
        self.ninstr[ek] += 1
        ins.then_inc(self.sem[ek], 1)
        h = (ek, self.cnt[ek])
        self._record(h, reads, writes)
        return h

    def mm(self, fn, reads=(), writes=(), inc=True):
        deps = self._deps("pe", reads, writes)
        self._emit_waits("pe", deps)
        ins = fn(self.eng["pe"])
        self.ninstr["pe"] += 1
        if inc:
            self.cnt["pe"] += 1
            ins.then_inc(self.sem["pe"], 1)
            h = ("pe", self.cnt["pe"])
        else:
            h = ("pe", self.cnt["pe"] + 1)
        self._record(h, reads, writes)
        return h

    def dma(self, qk, out, in_, reads=(), writes=(), semkey=None, **kw):
        deps = self._deps(qk, reads, writes)
        self._emit_waits(qk, deps)
        s = self._sem(semkey)
        ins = self.eng[qk].dma_start(out=out, in_=in_, **kw)
        self.cnt[semkey] += 16
        ins.then_inc(s, 16)
        h = (semkey, self.cnt[semkey])
        self._record(h, reads, writes)
        return h


def make_consts(S):
    NCB = S // 16 - 1
    NCT = (NCB + 127) // 128
    c = {}
    c["c_ident"] = np.eye(128, dtype=np.float32)
    k = np.arange(S)
    c["c_kaug"] = np.stack([(k % 128) - 64, k // 128, np.ones(S)]).astype(np.float32)
    cc = np.arange(NCT * 128)
    pos = 16 * cc + 31
    c["c_kcaug"] = np.stack([(pos % 128) - 64, pos // 128, np.ones_like(pos)]).astype(np.float32)
    NT_ = S // 128
    qa = np.zeros((3, 2, NT_, 4, 128), np.float32)
    for g in range(2):
        for h in range(4):
            sl = 2.0 ** (-(g * 4 + h + 1))
            qa[0, g, :, h, :] = sl
            qa[1, g, :, h, :] = 128.0 * sl
            qa[2, g, :, h, :] = -128.0 * sl * np.arange(NT_)[:, None]
    c["c_qaug"] = qa.reshape(3, 2, NT_ * 512)
    E = np.zeros((128, S), np.float32)
    for j in range(S // 64):
        E[j, 64 * j:64 * j + 64] = 1.0
    c["c_E"] = E
    u = np.arange(17 * 128)
    ccl = np.arange(128)
    c["c_wm"] = np.where(16 * ccl[:, None] + 31 > u[None, :], NEG, 0.0).astype(np.float32)
    kk = np.arange(128)[:, None]
    tt = np.arange(128)[None, :]
    caus = np.zeros((128, 2, 128), np.float32)
    caus[:, 0, :] = np.where(kk > tt, NEG, 0.0)
    caus[:, 1, :] = np.where(kk <= tt, NEG, 0.0)
    c["c_caus"] = caus
    TA = np.zeros((128, 128), np.float32)
    TB = np.zeros((128, 128), np.float32)
    for ttv in range(128):
        cr = 1 if ttv >= 64 else 0
        for mp in range(128):
            mr = mp - 64
            if mr <= cr - 2:
                TA[ttv, mp] = 1.0
            if mr == cr or mr == cr - 1:
                TB[ttv, mp] = 1e4
            elif mr > cr:
                TB[ttv, mp] = -1.0
    c["c_TA"] = TA
    c["c_TB"] = TB
    ov = np.zeros((128, NCT, 64), np.float32)
    for cb in range(NCB):
        for j in range(S // 64):
            o = min(16 * cb + 32, 64 * j + 64) - max(16 * cb, 64 * j)
            if o > 0:
                ov[cb % 128, cb // 128, j] = o / 32.0
    c["c_ov"] = ov
    return c


class _Stop(Exception):
    pass


def build(S, stop=None, dump=()):
    assert S % 512 == 0
    NT = S // 128
    NST = S // 512
    NSB = S // 64
    NSEL = min(16, NSB)
    assert NSEL in (8, 16)
    NCB = S // 16 - 1
    NCT = (NCB + 127) // 128
    NB16 = S // 16

    nc = bass.Bass("TRN2", target_bir_lowering=False)
    T = Tracker(nc)

    def din(name, shape):
        return nc.dram_tensor(name, list(shape), F32, kind="ExternalInput").ap()

    x_d = din("x", [S, D])
    y_d = nc.dram_tensor("y", [S, D], F32, kind="ExternalOutput").ap()
    wck_d = din("w_cmp_k", [2048, 64])
    wcv_d = din("w_cmp_v", [2048, 64])
    g1c_d = din("g1c", [128, 8])
    g2c_d = din("g2c", [128, 8])
    gq_d = din("gq", [64, 1])
    gk_d = din("gk", [64, 3])
    posk_d = din("posk", [128, 16])
    posv_d = din("posv", [128, 16])
    convw_d = din("convw", [128, 8, 3])
    cst = {}
    for name, shape in [("c_ident", [128, 128]), ("c_kaug", [3, S]), ("c_kcaug", [3, NCT * 128]),
                        ("c_qaug", [3, 2, S * 4]), ("c_E", [128, S]), ("c_wm", [128, 17 * 128]),
                        ("c_caus", [128, 2, 128]), ("c_TA", [128, 128]), ("c_TB", [128, 128]),
                        ("c_ov", [128, NCT, 64])]:
        cst[name] = din(name, shape)


    def sb(name, shape, dt):
        return nc.alloc_sbuf_tensor(name, list(shape), dt)

    ksT = sb("ksT", [67, 2, S], BF16)
    kwT = sb("kwT", [67, 2, S], BF16)
    vs = sb("vs", [128, NT, 2, 65], BF16)
    vw = sb("vw", [128, NT, 2, 65], BF16)
    kcr = sb("kcr", [128, 16, 33], BF16)
    vcr = sb("vcr", [128, 16, 33], BF16)
    kcT = sb("kcT", [67, 2, NCT * 128], BF16)
    vca = sb("vca", [128, NCT, 2, 128], BF16)
    cE = sb("cE", [128, S], BF16)
    wm = sb("wm", [128, 17 * 128], BF16)
    caus = sb("caus", [128, 2, 128], BF16)
    ident = sb("ident", [128, 128], BF16)
    TA = sb("TA", [128, 128], F32)
    TB = sb("TB", [128, 128], F32)
    posk = sb("posk_s", [128, 16], BF16)
    posv = sb("posv_s", [128, 16], BF16)
    biasrow = sb("biasrow", [1, 2, 2, 64], BF16)
    onesrow = sb("onesrow", [1, 128], BF16)
    g1c = sb("g1c_s", [128, 8], F32)
    g2c = sb("g2c_s", [128, 8], F32)
    gq = sb("gq_s", [64, 1], F32)
    gk = sb("gk_s", [64, 3], F32)
    convw = sb("convw_s", [128, 8, 3], F32)
    halo = sb("halo", [128, 8, 2], F32)
    neghalf = sb("neghalf", [128, 16], F32)
    selneg = sb("selneg", [128, 2, 128], BF16)
    xt = sb("xt", [128, 4, D], F32)
    xnT = sb("xnT", [128, KC, 512], BF16)
    xn_tok = sb("xn_tok", [128, 2, D], BF16)
    ss = sb("ss", [128, 8], F32)
    rstd = sb("rstd", [128, 8], F32)
    sq = sb("sq", [128, 768], F32)
    ssq = sb("ssq", [128, 16], F32)
    rq = sb("rq", [128, 16], F32)
    qk_tok2 = [sb("qk_tok%d" % i, [128, 768], BF16) for i in range(3)]
    tg = sb("tg", [128, 4, 24], F32)
    arena = sb("arena", [128, 32, 512], BF16)
    actT = arena
    zT = arena[:, 0:8, :]
    mixedT = arena[:, 8:16, :]
    qT = arena[0:67, 16:24, :].rearrange("p c t -> p (c t)").rearrange("p (g j h t) -> p g j h t", g=2, j=4, h=4)
    o_nsaT = arena[:, 24:28, :]
    vc_tok2 = [sb("vc_tok%d" % i, [128, 128], BF16) for i in range(3)]
    vtmp = sb("vtmp", [32, 2, 64], BF16)

    def alias(fc):
        if fc < 8:
            return [("zT", fc)]
        if fc < 16:
            return [("mixedT", fc - 8)]
        if fc < 20:
            return [("qT", 0)]
        if fc < 24:
            return [("qT", 1)]
        if fc < 28:
            return [("o_nsaT", jj) for jj in range(4)]
        return []
    Pt = [sb("P%d" % i, [128, 512], BF16) for i in range(4)]
    Oc2 = [sb("Oc%d" % i, [128, 8, 128], F32) for i in range(2)]
    Ow = sb("Ow", [128, 8, 65], F32)
    Os = sb("Os", [128, 8, 65], F32)
    sm2 = [sb("sm%d" % i, [128, 64], F32) for i in range(2)]
    imp2 = [sb("imp%d" % i, [128, 2, 64], F32) for i in range(2)]
    score2 = [sb("score%d" % i, [128, 2, 64], F32) for i in range(2)]
    work2 = [sb("work%d" % i, [128, 2, 64], F32) for i in range(2)]
    m82 = [sb("m8_%d" % i, [128, 2, 16], F32) for i in range(2)]
    selb2 = [sb("selb%d" % i, [128, 2, 64], BF16) for i in range(2)]
    o_tok = sb("o_tok", [128, 512], BF16)
    kc_tok2 = [sb("kc_tok%d" % i, [128, 128], BF16) for i in range(3)]
    kc_tok = kc_tok2[0]
    cxt = sb("cxt", [128, 512], F32)
    ubuf = sb("ubuf", [128, 514], F32)
    ybuf = sb("ybuf", [128, 512], F32)
    ta = sb("ta", [128, 2, 512], F32)
    otmp = ta
    mt = sb("mt", [128, 2, 512], F32)
    rl = sb("rl", [128, 2, 512], BF16)
    obuf = sb("obuf", [128, 4, 512], F32)
    wslots = [sb("wslot%d" % i, [128, SLOT], BF16) for i in range(NSLOTS)]

    NG = 6
    Gb = [nc.alloc_psum_tensor("G%d" % i, [128, 512], F32) for i in range(NG)]
    Sb = Gb[0:3]
    Ob = Gb[3:5]
    Tb = [nc.alloc_psum_tensor("T%d" % i, [128, 1024], BF16) for i in range(2)]
    rot = {"G": 0, "S": 0, "O": 0, "T": 0, "P": 0, "A": 0, "OB": 0}
    sel_done = {}
    selT_done = {}
    defer_pe = []

    def nxt(kind, n):
        v = rot[kind]
        rot[kind] = (v + 1) % n
        return v

    CK = "const"

    import os
    DBG = os.environ.get("KDBG", "").split(",")
    cl_n = [0]

    def cload(dst, src, key, after=()):
        cl_n[0] += 1
        T.dma("pool", dst, src, reads=list(after), writes=[("cl", cl_n[0])], semkey=CK)

    cload(ident[:, :], cst["c_ident"], "ident")
    cload(g1c[:, :], g1c_d, "g1c")
    cload(g2c[:, :], g2c_d, "g2c")
    cload(gq[:, :], gq_d, "gq")
    cload(gk[:, :], gk_d, "gk")
    cload(convw[:, :, :], convw_d, "convw")
    cload(TA[:, :], cst["c_TA"], "TA")
    cload(TB[:, :], cst["c_TB"], "TB")
    cload(cE[:, :], cst["c_E"], "cE")
    cload(wm[:, :], cst["c_wm"], "wm")
    cload(caus[:, :, :], cst["c_caus"], "caus")
    cload(posk[:, :], posk_d, "posk")
    cload(posv[:, :], posv_d, "posv")
    wflk = wslots[NSLOTS - 1][:, 0:1024].rearrange("p (c e) -> p c e", c=16)
    wflv = wslots[NSLOTS - 1][:, 1024:2048].rearrange("p (c e) -> p c e", c=16)
    WFK = ("ws", NSLOTS - 1)
    T.dma("pool", wflk, wck_d.rearrange("(c p) e -> p c e", p=128), writes=[WFK], semkey=("wsq", NSLOTS - 1))
    T.dma("pool", wflv, wcv_d.rearrange("(c p) e -> p c e", p=128), writes=[WFK], semkey=("wsq", NSLOTS - 1))
    T.op("dve", lambda e: e.memset(kcT[:, :, :], 0.0), writes=["kcT_ms"])
    T.op("dve", lambda e: e.memset(vca[:, :, :, :], 0.0), writes=["vca_ms"])
    T.op("dve", lambda e: e.memset(vs[:, :, :, 64:65], 1.0), writes=["vs1"])
    T.op("dve", lambda e: e.memset(vw[:, :, :, 64:65], 1.0), writes=["vw1"])
    T.op("dve", lambda e: e.memset(selneg[:, :, :], 0.0), writes=["selneg0", "selneg1"])
    T.op("dve", lambda e: e.memset(halo[:, :, :], 0.0), writes=["halo"])
    T.op("dve", lambda e: e.memset(neghalf[:, :], -0.5), writes=["neghalf"])
    T.op("dve", lambda e: e.memset(onesrow[:, :], 1.0), writes=["onesrow"])
    wck_v = wck_d.rearrange("(l d) e -> d l e", d=64)
    wcv_v = wcv_d.rearrange("(l d) e -> d l e", d=64)
    for g in range(2):
        cload(ksT[64:67, g, :], cst["c_kaug"], "ksTaug")
        cload(kwT[64:67, g, :], cst["c_kaug"], "kwTaug")
        cload(kcT[64:67, g, :], cst["c_kcaug"], "kcT", after=["kcT_ms"])
    cload(vca[:, :, 0, 64:128], cst["c_ov"], "vca", after=["vca_ms"])
    cload(vca[:, :, 1, 64:128], cst["c_ov"], "vca", after=["vca_ms"])
    ctot = ("const", T.cnt[CK])
    for key in ["ident", "g1c", "g2c", "gq", "gk", "convw", "TA", "TB", "cE", "wm", "caus", "posk", "posv",
                "ksTaug", "kwTaug", "kcT", "vca"]:
        T.lastw[key] = ctot

    def load_x(st, hh, q="pool"):
        for j in range(4):
            i = 4 * st + j
            T.dma(q, xt[:, j, hh * 512:(hh + 1) * 512], x_d[i * 128:(i + 1) * 128, hh * 512:(hh + 1) * 512],
                  writes=[("xt", j, hh)], semkey=("xt" if q == "pool" else "xts", j))

    load_x(0, 0)
    load_x(0, 1)

    JPS = 39
    NJOBS = JPS * NST
    wjobs_d = din("wjobs", [JPS, 128, SLOT])
    scr_t = nc.dram_tensor("scr_slots", [JPS, 128, SLOT], BF16).ap()
    job_used = [4096, 4096, 2240, 4096, 4096] + [3072] * 8 + [3584] * 8 + [4096] * 18
    assert len(job_used) == JPS
    wstate = {"issued": 0, "next": 0}

    def issue_job(m):
        sidx = m % NSLOTS
        s = wslots[sidx]
        wk_ = ("ws", sidx)
        jidx = m % JPS
        nu = job_used[jidx]
        if m >= JPS:
            T.dma("sp", s[0:64, 0:nu], scr_t[jidx][0:64, 0:nu], reads=[("scr", jidx)], writes=[wk_], semkey=wk_)
            T.dma("act", s[64:128, 0:nu], scr_t[jidx][64:128, 0:nu], reads=[("scr", jidx)], writes=[wk_], semkey=wk_)
            return
        for c0 in range(0, nu, 1024):
            c1 = min(nu, c0 + 1024)
            T.dma("pool", s[:, c0:c1], wjobs_d[jidx][:, c0:c1], writes=[wk_], semkey=("wsq", sidx))
        T.dma("sp", scr_t[jidx][:, 0:nu], s[:, 0:nu], reads=[wk_], writes=[("scr", jidx)], semkey=("scrs", jidx % 4))

    def wnext(hold=0):
        n = wstate["next"]
        wstate["next"] += 1
        while wstate["issued"] < min(NJOBS, n + NSLOTS - hold):
            issue_job(wstate["issued"])
            wstate["issued"] += 1
        return wslots[n % NSLOTS], ("ws", n % NSLOTS)

    T.op("dve", lambda e: e.tensor_scalar(out=gq[:, :], in0=gq[:, :], scalar1=0.125, scalar2=None, op0=ALU.mult),
         reads=["gq"], writes=["gq"])
    for kv, (pp, wf, pk, wk) in enumerate(((posk, wflk, "posk", WFK), (posv, wflv, "posv", WFK))):
        gb = nxt("G", NG)
        for c in range(16):
            T.mm(lambda e, c=c, pp=pp, wf=wf, gb=gb: e.matmul(Gb[gb][0:1, 0:64], lhsT=pp[:, c:c + 1], rhs=wf[:, c, :],
                                                              start=(c == 0), stop=(c == 15)),
                 reads=[pk, wk], writes=[("G", gb)], inc=(c == 15))
        for g in range(2):
            T.op("dve", lambda e, kv=kv, g=g, gb=gb: e.tensor_copy(out=biasrow[0:1, kv, g, :], in_=Gb[gb][0:1, 0:64]),
                 reads=[("G", gb)], writes=["biasrow"])

    def rms_chain(j, stat_col):
        a = nxt("A", 2)
        src_keys = [("xt", j, 0), ("xt", j, 1)]
        T.op("act", lambda e: e.activation(out=rl[:, :, :].rearrange("p a b -> p (a b)"), in_=xt[:, j, :], func=ACTF.Square,
                                           accum_out=ss[:, stat_col:stat_col + 1]),
             reads=src_keys, writes=[("rl", 0), ("rl", 1), ("ss", stat_col)])
        T.op("dve", lambda e: e.tensor_scalar(out=rstd[:, stat_col:stat_col + 1], in0=ss[:, stat_col:stat_col + 1],
                                              scalar1=1.0 / D, scalar2=EPS, op0=ALU.mult, op1=ALU.add),
             reads=[("ss", stat_col)], writes=[("rstd", stat_col)])
        T.op("pool", lambda e: e.tensor_tensor(out=rstd[:, stat_col:stat_col + 1], in0=rstd[:, stat_col:stat_col + 1],
                                               in1=neghalf[:, 0:1], op=ALU.pow),
             reads=[("rstd", stat_col), "neghalf"], writes=[("rstd", stat_col)])
        T.op("dve", lambda e: e.tensor_scalar(out=xn_tok[:, a, :], in0=xt[:, j, :], scalar1=rstd[:, stat_col:stat_col + 1],
                                              scalar2=None, op0=ALU.mult),
             reads=src_keys + [("rstd", stat_col)], writes=[("xn_tok", a)])
        return a

    def rms_tr(a, j, gcol):
        jsl = slice(j * 128, (j + 1) * 128)
        tb = nxt("T", 2)
        for k in range(KC):
            T.mm(lambda e, k=k: e.transpose(Tb[tb][:, k * 128:(k + 1) * 128], xn_tok[:, a, k * 128:(k + 1) * 128], ident[:, :]),
                 reads=[("xn_tok", a), "ident"], writes=[("T", tb)], inc=(k == KC - 1))
        T.op("dve", lambda e: e.tensor_tensor(out=xnT[:, :, jsl],
                                              in0=Tb[tb][:, :].rearrange("p (k t) -> p k t", k=KC),
                                              in1=gcol[:, :].unsqueeze(2).to_broadcast([128, KC, 128]), op=ALU.mult),
             reads=[("T", tb), "g1c", "g2c"], writes=[("xnT", j)])

    def rms_phase(gcol, col0):
        slots = {}
        for j in range(4):
            slots[j] = rms_chain(j, col0 + j)
            if j >= 1:
                rms_tr(slots[j - 1], j - 1, gcol)
        rms_tr(slots[3], 3, gcol)

    def qbc(g, jsl):
        return qT[0:67, g, jsl.start // 128, :, :]

    def ps3(bank):
        return bank[:, :].rearrange("p (h t) -> p h t", h=4)

    def bc4(ap2d):
        return ap2d.unsqueeze(1).to_broadcast([ap2d.shape[0], 4, ap2d.shape[1]])

    pend = []
    LAG = 2

    def emit_pv(blk):
        (p, v_ap, vkeys, ob, first, last, ncols, done) = blk
        for h in range(4):
            T.mm(lambda e, h=h: e.matmul(Ob[ob][:, h * ncols:(h + 1) * ncols], lhsT=Pt[p][:, h * 128:(h + 1) * 128], rhs=v_ap,
                                         start=(first and h == 0), stop=(last and h == 3), skip_group_check=True),
                 reads=[("P", p)] + vkeys, writes=[("G", 3 + ob)], inc=(h == 3))
        if done is not None:
            done()

    def flush_pv():
        while pend:
            emit_pv(pend.pop(0))

    def attn_block(g, jsl, kT_ap, kkeys, masks, v_ap, vkeys, ob, first, last, ncols, done=None):
        sbk = nxt("S", 3)
        nm = len(masks)
        T.mm(lambda e: e.matmul(ps3(Sb[sbk]), lhsT=kT_ap, rhs=qbc(g, jsl), start=True, stop=(nm == 0)),
             reads=kkeys + [("qT", g)], writes=[("G", sbk)], inc=(nm == 0))
        for mi, (ml, mr, mk) in enumerate(masks):
            T.mm(lambda e, ml=ml, mr=mr, mi=mi: e.matmul(ps3(Sb[sbk]), lhsT=ml, rhs=mr, start=False, stop=(mi == nm - 1)),
                 reads=mk, writes=[("G", sbk)], inc=(mi == nm - 1))
        p = nxt("P", 4)
        T.op("act", lambda e: e.activation(out=Pt[p][:, :], in_=Sb[sbk][:, :], func=ACTF.Exp),
             reads=[("G", sbk)], writes=[("P", p)])
        pend.append((p, v_ap, vkeys, ob, first, last, ncols, done))
        while len(pend) > LAG:
            emit_pv(pend.pop(0))

    build.marks = []

    def ckpt(name):
        build.marks.append((name, T.ninstr["pe"]))
        if stop == name:
            raise _Stop()

    def _main_loop():
        for st in range(NST):
            for g in range(2):
                T.dma("pool", arena[64:67, 16 + 4 * g:20 + 4 * g, :], cst["c_qaug"][:, g, st * 2048:(st + 1) * 2048].rearrange("r (c t) -> r c t", c=4),
                      writes=[("qT", g)], semkey=("qaug", g), )

            rms_phase(g1c, 0)
            xnT_keys = [("xnT", j) for j in range(4)]

            ckpt("A")
            def headnorm(bank_ap, bkey, nh, sqc, stc, dstc, par):
                w_ = nh * 64
                T.op("act", lambda e: e.activation(out=sq[:, sqc:sqc + w_], in_=bank_ap, func=ACTF.Square),
                     reads=[bkey], writes=[("sq", sqc)])
                T.op("dve", lambda e: e.tensor_reduce(out=ssq[:, stc:stc + nh], in_=sq[:, sqc:sqc + w_].rearrange("p (h d) -> p h d", d=64),
                                                      axis=AX.X, op=ALU.add),
                     reads=[("sq", sqc)], writes=[("ssq", stc)])
                T.op("dve", lambda e: e.tensor_scalar(out=rq[:, stc:stc + nh], in0=ssq[:, stc:stc + nh], scalar1=1.0 / 64, scalar2=EPS,
                                                      op0=ALU.mult, op1=ALU.add),
                     reads=[("ssq", stc)], writes=[("rq", stc)])
                T.op("pool", lambda e: e.tensor_tensor(out=rq[:, stc:stc + nh], in0=rq[:, stc:stc + nh], in1=neghalf[:, 0:nh], op=ALU.pow),
                     reads=[("rq", stc), "neghalf"], writes=[("rq", stc)])
                T.op("dve", lambda e: e.tensor_tensor(out=qk_tok2[par][:, dstc:dstc + w_].rearrange("p (h d) -> p h d", d=64),
                                                      in0=bank_ap.rearrange("p (h d) -> p h d", d=64),
                                                      in1=rq[:, stc:stc + nh].unsqueeze(2).to_broadcast([128, nh, 64]), op=ALU.mult),
                     reads=[bkey, ("rq", stc)], writes=[("qk_tok", par, dstc)])

            pipeB = []
            PLAG = 2

            def pipe2(p1, p2):
                for j in range(4):
                    k_ = pipeB_n[0]
                    pipeB_n[0] += 1
                    p1(j, k_ % 3)
                    pipeB.append((p2, j, k_ % 3))
                    while len(pipeB) > PLAG:
                        f_, j_, par_ = pipeB.pop(0)
                        f_(j_, par_)
            pipeB_n = [0]

            sA, kA = wnext()
            wA = sA[:, 0:4096].rearrange("p (k c) -> p k c", k=8)

            def a1(j, par):
                jsl = slice(j * 128, (j + 1) * 128)
                gb = nxt("G", NG)
                for k in range(KC):
                    T.mm(lambda e, k=k: e.matmul(Gb[gb][:, :], lhsT=xnT[:, k, jsl], rhs=wA[:, k, :], start=(k == 0), stop=(k == KC - 1)),
                         reads=[("xnT", j), kA], writes=[("G", gb)], inc=(k == KC - 1))
                headnorm(Gb[gb][:, :], ("G", gb), 8, 0, 0, 0, par)

            def a2(j, par):
                jsl = slice(j * 128, (j + 1) * 128)
                tb = nxt("T", 2)
                for h in range(8):
                    T.mm(lambda e, h=h: e.transpose(Tb[tb][0:64, h * 128:(h + 1) * 128], qk_tok2[par][:, h * 64:(h + 1) * 64], ident[:, :]),
                         reads=[("qk_tok", par, 0), "ident"], writes=[("T", tb)], inc=(h == 7))
                T.op("act", lambda e: e.activation(out=qT[0:64, :, j, :, :],
                                                   in_=Tb[tb][0:64, :].rearrange("p (g h t) -> p g h t", g=2, h=4),
                                                   func=ACTF.Copy, scale=gq[:, 0:1]),
                     reads=[("T", tb), "gq"], writes=[("qT", 0), ("qT", 1)])
            pipe2(a1, a2)

            sB, kB = wnext()
            wB = sB[:, 0:4096].rearrange("p (k c) -> p k c", k=8)

            def b1(j, par):
                i = 4 * st + j
                jsl = slice(j * 128, (j + 1) * 128)
                gb = nxt("G", NG)
                for k in range(KC):
                    T.mm(lambda e, k=k: e.matmul(Gb[gb][:, :], lhsT=xnT[:, k, jsl], rhs=wB[:, k, :], start=(k == 0), stop=(k == KC - 1)),
                         reads=[("xnT", j), kB], writes=[("G", gb)], inc=(k == KC - 1))
                T.op("act", lambda e: e.activation(out=kc_tok2[par][:, :], in_=Gb[gb][:, 0:128], func=ACTF.Copy),
                     reads=[("G", gb)], writes=[("kc_tok", par)])
                T.op("act", lambda e: e.activation(out=vc_tok2[par][:, :], in_=Gb[gb][:, 128:256], func=ACTF.Copy),
                     reads=[("G", gb)], writes=[("vc_tok", par)])
                T.op("act", lambda e: e.activation(out=vs[:, i, :, 0:64], in_=Gb[gb][:, 384:512].rearrange("p (g d) -> p g d", g=2),
                                                   func=ACTF.Copy),
                     reads=[("G", gb)], writes=[("vs", i)])
                headnorm(Gb[gb][:, 256:384], ("G", gb), 2, 512, 8, 512, par)

            def b2(j, par):
                i = 4 * st + j
                isl = slice(i * 128, (i + 1) * 128)
                tb2 = nxt("T", 2)
                for h in range(2):
                    T.mm(lambda e, h=h: e.transpose(Tb[tb2][0:64, h * 128:(h + 1) * 128], qk_tok2[par][:, 512 + h * 64:512 + (h + 1) * 64], ident[:, :]),
                         reads=[("qk_tok", par, 512), "ident"], writes=[("T", tb2)], inc=False)
                T.mm(lambda e: e.transpose(Tb[tb2][:, 512:640], kc_tok2[par][:, :], ident[:, :]),
                     reads=[("kc_tok", par), "ident"], writes=[("T", tb2)], inc=False)
                T.mm(lambda e: e.transpose(Tb[tb2][:, 640:768], vc_tok2[par][:, :], ident[:, :]),
                     reads=[("vc_tok", par), "ident"], writes=[("T", tb2)], inc=True)
                T.op("act", lambda e: e.activation(out=ksT[0:64, :, isl], in_=Tb[tb2][0:64, 0:256].rearrange("p (g t) -> p g t", g=2),
                                                   func=ACTF.Copy, scale=gk[:, 1:2]),
                     reads=[("T", tb2), "gk"], writes=[("ksT", i)])
                T.op("dve", lambda e: e.tensor_copy(out=kcr[:, :, 8 * j + 1:8 * j + 9].rearrange("p r b -> p b r"),
                                                    in_=Tb[tb2][:, 512:640].rearrange("p (b r) -> p b r", r=16)),
                     reads=[("T", tb2)], writes=["kcr"])
                T.op("dve", lambda e: e.tensor_copy(out=vcr[:, :, 8 * j + 1:8 * j + 9].rearrange("p r b -> p b r"),
                                                    in_=Tb[tb2][:, 640:768].rearrange("p (b r) -> p b r", r=16)),
                     reads=[("T", tb2)], writes=["vcr"])
            pipe2(b1, b2)

            sC, kC_ = wnext()
            wC = sC[:, 0:8 * 280].rearrange("p (k c) -> p k c", k=8)

            def c1(j, par):
                i = 4 * st + j
                jsl = slice(j * 128, (j + 1) * 128)
                gb = nxt("G", NG)
                for k in range(KC):
                    T.mm(lambda e, k=k: e.matmul(Gb[gb][:, 0:280], lhsT=xnT[:, k, jsl], rhs=wC[:, k, :], start=(k == 0), stop=(k == KC - 1)),
                         reads=[("xnT", j), kC_], writes=[("G", gb)], inc=(k == KC - 1))
                T.op("act", lambda e: e.activation(out=vw[:, i, :, 0:64], in_=Gb[gb][:, 128:256].rearrange("p (g d) -> p g d", g=2),
                                                   func=ACTF.Copy),
                     reads=[("G", gb)], writes=[("vw", i)])
                T.op("act", lambda e: e.activation(out=tg[:, j, :], in_=Gb[gb][:, 256:280], func=ACTF.Tanh, scale=0.5),
                     reads=[("G", gb)], writes=[("tg", j)])
                headnorm(Gb[gb][:, 0:128], ("G", gb), 2, 640, 10, 640, par)

            def c2(j, par):
                i = 4 * st + j
                isl = slice(i * 128, (i + 1) * 128)
                tb3 = nxt("T", 2)
                for h in range(2):
                    T.mm(lambda e, h=h: e.transpose(Tb[tb3][0:64, h * 128:(h + 1) * 128], qk_tok2[par][:, 640 + h * 64:640 + (h + 1) * 64], ident[:, :]),
                         reads=[("qk_tok", par, 640), "ident"], writes=[("T", tb3)], inc=(h == 1))
                T.op("act", lambda e: e.activation(out=kwT[0:64, :, isl], in_=Tb[tb3][0:64, 0:256].rearrange("p (g t) -> p g t", g=2),
                                                   func=ACTF.Copy, scale=gk[:, 2:3]),
                     reads=[("T", tb3), "gk"], writes=[("kwT", i)])
            pipe2(c1, c2)
            while pipeB:
                f_, j_, par_ = pipeB.pop(0)
                f_(j_, par_)

            ckpt("B")
            c0 = max(0, 32 * st - 1)
            c1 = 32 * st + 30
            n = c1 - c0 + 1
            bl0 = c0 - 32 * st + 1
            for kv, (src, srck) in ((1, (vcr, "vcr")), (0, (kcr, "kcr"))):
                s_, sk = wnext()
                wbd = s_[:, 0:4096].rearrange("p (l e) -> p l e", l=32)
                gb = nxt("G", NG)
                for l in range(32):
                    T.mm(lambda e, l=l: e.matmul(Gb[gb][0:n, 0:128], lhsT=src[:, l % 16, bl0 + l // 16:bl0 + l // 16 + n],
                                                 rhs=wbd[:, l, :], start=(l == 0), stop=False),
                         reads=[srck, sk], writes=[("G", gb)], inc=False)
                T.mm(lambda e: e.matmul(Gb[gb][0:n, 0:128], lhsT=onesrow[0:1, 0:n],
                                        rhs=biasrow[0:1, kv, :, :].rearrange("p g e -> p (g e)"), start=False, stop=True),
                     reads=["onesrow", "biasrow"], writes=[("G", gb)], inc=True)
                if kv == 1:
                    T.op("act", lambda e: e.activation(out=vtmp[0:n, :, :], in_=Gb[gb][0:n, 0:128].rearrange("p (g d) -> p g d", g=2),
                                                       func=ACTF.Copy),
                         reads=[("G", gb)], writes=["vtmp"])
                    for ct in range(NCT):
                        lo = max(c0, 128 * ct)
                        hi = min(c1, 128 * ct + 127)
                        if lo > hi:
                            continue
                        T.dma("pool", vca[lo - 128 * ct:hi - 128 * ct + 1, ct, :, 0:64], vtmp[lo - c0:hi - c0 + 1, :, :],
                              reads=["vtmp"], writes=["vca"], semkey="vcadma")
                else:
                    T.op("act", lambda e: e.activation(out=sq[0:n, 0:128], in_=Gb[gb][0:n, 0:128], func=ACTF.Square),
                         reads=[("G", gb)], writes=["sq_q"])
                    T.op("dve", lambda e: e.tensor_reduce(out=ssq[0:n, 12:14], in_=sq[0:n, 0:128].rearrange("p (h d) -> p h d", d=64),
                                                          axis=AX.X, op=ALU.add),
                         reads=["sq_q"], writes=["ssqc"])
                    T.op("dve", lambda e: e.tensor_scalar(out=rq[0:n, 12:14], in0=ssq[0:n, 12:14], scalar1=1.0 / 64, scalar2=EPS,
                                                          op0=ALU.mult, op1=ALU.add),
                         reads=["ssqc"], writes=["rqc"])
                    T.op("pool", lambda e: e.tensor_tensor(out=rq[0:n, 12:14], in0=rq[0:n, 12:14], in1=neghalf[0:n, 0:2], op=ALU.pow),
                         reads=["rqc", "neghalf"], writes=["rqc"])
                    T.op("dve", lambda e: e.tensor_tensor(out=kc_tok[0:n, :].rearrange("p (h d) -> p h d", d=64),
                                                          in0=Gb[gb][0:n, 0:128].rearrange("p (h d) -> p h d", d=64),
                                                          in1=rq[0:n, 12:14].unsqueeze(2).to_broadcast([n, 2, 64]), op=ALU.mult),
                         reads=[("G", gb), "rqc"], writes=[("kc_tok", 0)])
                    tb = nxt("T", 2)
                    for g in range(2):
                        T.mm(lambda e, g=g: e.transpose(Tb[tb][0:64, g * 128:g * 128 + n], kc_tok[0:n, g * 64:(g + 1) * 64], ident[0:n, 0:n]),
                             reads=[("kc_tok", 0), "ident"], writes=[("T", tb)], inc=(g == 1))
                    T.op("act", lambda e: e.activation(out=kcT[0:64, :, c0:c0 + n],
                                                       in_=Tb[tb][0:64, 0:256].rearrange("p (g t) -> p g t", g=2)[:, :, 0:n],
                                                       func=ACTF.Copy, scale=gk[:, 0:1]),
                         reads=[("T", tb), "gk"], writes=["kcT"])
            T.op("dve", lambda e: e.tensor_copy(out=kcr[:, :, 0:1], in_=kcr[:, :, 32:33]), reads=["kcr"], writes=["kcr"])
            T.op("dve", lambda e: e.tensor_copy(out=vcr[:, :, 0:1], in_=vcr[:, :, 32:33]), reads=["vcr"], writes=["vcr"])

            ckpt("C")
            def cmp_done(j, g, ob):
                i = 4 * st + j
                pr = j % 2
                Oc, sm, imp, score, work, m8, selb = Oc2[pr], sm2[pr], imp2[pr], score2[pr], work2[pr], m82[pr], selb2[pr]
                tsl = slice(64 - 2 * i, 128 - 2 * i)
                T.op("act", lambda e: e.activation(out=Oc[:, g * 4:(g + 1) * 4, :], in_=Ob[ob][:, :].rearrange("p (h c) -> p h c", h=4),
                                                   func=ACTF.Copy),
                     reads=[("G", 3 + ob)], writes=[("Oc", pr, g)])
                T.op("dve", lambda e: e.tensor_reduce(out=sm[:, g * 4:(g + 1) * 4], in_=Oc[:, g * 4:(g + 1) * 4, 64:128],
                                                      axis=AX.X, op=ALU.add),
                     reads=[("Oc", pr, g)], writes=[("denc", pr, g)])
                T.op("dve", lambda e: e.tensor_scalar(out=sm[:, g * 4:(g + 1) * 4], in0=sm[:, g * 4:(g + 1) * 4], scalar1=1e-30,
                                                      scalar2=None, op0=ALU.max),
                     reads=[("denc", pr, g)], writes=[("denc", pr, g)])
                T.op("dve", lambda e: e.reciprocal(out=sm[:, 8 + g * 4:8 + (g + 1) * 4], in_=sm[:, g * 4:(g + 1) * 4]),
                     reads=[("denc", pr, g)], writes=[("rc", pr, g)])
                for h in range(4):
                    hh = g * 4 + h
                    if h == 0:
                        T.op("dve", lambda e: e.tensor_scalar(out=imp[:, g, :], in0=Oc[:, hh, 64:128], scalar1=sm[:, 8 + hh:9 + hh],
                                                              scalar2=None, op0=ALU.mult),
                             reads=[("Oc", pr, g), ("rc", pr, g)], writes=[("imp", pr, g)])
                    else:
                        T.op("dve", lambda e: e.scalar_tensor_tensor(out=imp[:, g, :], in0=Oc[:, hh, 64:128], scalar=sm[:, 8 + hh:9 + hh],
                                                                     in1=imp[:, g, :], op0=ALU.mult, op1=ALU.add),
                             reads=[("Oc", pr, g), ("rc", pr, g), ("imp", pr, g)], writes=[("imp", pr, g)])
                T.op("dve", lambda e: e.tensor_tensor(out=score[:, g, :], in0=imp[:, g, :], in1=TA[:, tsl], op=ALU.mult),
                     reads=[("imp", pr, g), "TA"], writes=[("score", pr, g)])
                T.op("dve", lambda e: e.tensor_tensor(out=score[:, g, :], in0=score[:, g, :], in1=TB[:, tsl], op=ALU.add),
                     reads=[("score", pr, g), "TB"], writes=[("score", pr, g)])
                T.op("dve", lambda e: e.memset(score[:, g, 0:1], 1e4), reads=[("score", pr, g)], writes=[("score", pr, g)])
                T.op("dve", lambda e: e.max(out=m8[:, g, 0:8], in_=score[:, g, :]), reads=[("score", pr, g)], writes=[("m8", pr, g)])
                if NSEL == 16:
                    T.op("dve", lambda e: e.match_replace(out=work[:, g, :], in_to_replace=m8[:, g, 0:8], in_values=score[:, g, :],
                                                          imm_value=-2.0),
                         reads=[("score", pr, g), ("m8", pr, g)], writes=[("work", pr, g)])
                    T.op("dve", lambda e: e.max(out=m8[:, g, 8:16], in_=work[:, g, :]), reads=[("work", pr, g)], writes=[("m8b", pr, g)])
                    thr_ap, thrk = m8[:, g, 15:16], "m8b"
                else:
                    thr_ap, thrk = m8[:, g, 7:8], "m8"
                T.op("dve", lambda e: e.tensor_scalar(out=selb[:, g, :], in0=score[:, g, :], scalar1=thr_ap, scalar2=None, op0=ALU.is_ge),
                     reads=[("score", pr, g), (thrk, pr, g)], writes=[("selb", pr, g)])
                sel_done[(i, g)] = True

            def sel_transpose(j, g):
                pr = j % 2
                tb = nxt("T", 2)
                T.mm(lambda e: e.transpose(Tb[tb][0:64, 0:128], selb2[pr][:, g, :], ident[:, :]),
                     reads=[("selb", pr, g), "ident"], writes=[("T", tb)])
                T.op("dve", lambda e: e.tensor_scalar(out=selneg[0:64, g, :], in0=Tb[tb][0:64, 0:128], scalar1=-NEG, scalar2=NEG,
                                                      op0=ALU.mult, op1=ALU.add),
                     reads=[("T", tb)], writes=["selneg%d" % g])

            def win_done(j, g, ob):
                T.op("act", lambda e: e.activation(out=Ow[:, g * 4:(g + 1) * 4, :],
                                                   in_=Ob[ob][:, 0:260].rearrange("p (h c) -> p h c", h=4), func=ACTF.Copy),
                     reads=[("G", 3 + ob)], writes=[("Ow", g)])

            def slc_done(j, g, ob):
                pr = j % 2
                Oc, sm = Oc2[pr], sm2[pr]
                jsl = slice(j * 128, (j + 1) * 128)
                T.op("act", lambda e: e.activation(out=Os[:, g * 4:(g + 1) * 4, :],
                                                   in_=Ob[ob][:, 0:260].rearrange("p (h c) -> p h c", h=4), func=ACTF.Copy),
                     reads=[("G", 3 + ob)], writes=[("Os", g)])
                if g == 0:
                    return
                T.op("dve", lambda e: e.tensor_scalar(out=sm[:, 16:40], in0=tg[:, j, :], scalar1=1.0, scalar2=0.5, op0=ALU.add, op1=ALU.mult),
                     reads=[("tg", j)], writes=["sg"])
                T.op("dve", lambda e: e.reciprocal(out=sm[:, 40:48], in_=Os[:, :, 64:65].rearrange("p h o -> p (h o)")),
                     reads=[("Os", 0), ("Os", 1)], writes=["rs"])
                T.op("dve", lambda e: e.reciprocal(out=sm[:, 48:56], in_=Ow[:, :, 64:65].rearrange("p h o -> p (h o)")),
                     reads=[("Ow", 0), ("Ow", 1)], writes=["rw"])
                T.op("dve", lambda e: e.tensor_tensor(out=sm[:, 16:24], in0=sm[:, 16:24], in1=sm[:, 8:16], op=ALU.mult),
                     reads=["sg", ("rc", pr, 0), ("rc", pr, 1)], writes=["fc"])
                T.op("dve", lambda e: e.tensor_tensor(out=sm[:, 24:40], in0=sm[:, 24:40], in1=sm[:, 40:56], op=ALU.mult),
                     reads=["sg", "rs", "rw"], writes=["fsw"])
                T.op("dve", lambda e: e.tensor_tensor(out=otmp[:, 0, :].rearrange("p (h d) -> p h d", d=64), in0=Oc[:, :, 0:64],
                                                      in1=sm[:, 16:24].unsqueeze(2).to_broadcast([128, 8, 64]), op=ALU.mult),
                     reads=[("Oc", pr, 0), ("Oc", pr, 1), "fc"], writes=[("ta", 0)])
                T.op("dve", lambda e: e.tensor_tensor(out=otmp[:, 1, :].rearrange("p (h d) -> p h d", d=64), in0=Os[:, :, 0:64],
                                                      in1=sm[:, 24:32].unsqueeze(2).to_broadcast([128, 8, 64]), op=ALU.mult),
                     reads=[("Os", 0), ("Os", 1), "fsw"], writes=[("ta", 1)])
                T.op("dve", lambda e: e.tensor_tensor(out=otmp[:, 0, :], in0=otmp[:, 0, :], in1=otmp[:, 1, :], op=ALU.add),
                     reads=[("ta", 0), ("ta", 1)], writes=[("ta", 0)])
                T.op("dve", lambda e: e.tensor_tensor(out=otmp[:, 1, :].rearrange("p (h d) -> p h d", d=64), in0=Ow[:, :, 0:64],
                                                      in1=sm[:, 32:40].unsqueeze(2).to_broadcast([128, 8, 64]), op=ALU.mult),
                     reads=[("Ow", 0), ("Ow", 1), "fsw"], writes=[("ta", 1)])
                T.op("dve", lambda e: e.tensor_tensor(out=o_tok[:, :], in0=otmp[:, 0, :], in1=otmp[:, 1, :], op=ALU.add),
                     reads=[("ta", 0), ("ta", 1)], writes=["o_tok"])

                def fin():
                    tb = nxt("T", 2)
                    for k in range(4):
                        T.mm(lambda e, k=k: e.transpose(Tb[tb][:, k * 128:(k + 1) * 128], o_tok[:, k * 128:(k + 1) * 128], ident[:, :]),
                             reads=["o_tok", "ident"], writes=[("T", tb)], inc=(k == 3))
                    T.op("act", lambda e: e.activation(out=o_nsaT[:, :, jsl], in_=Tb[tb][:, 0:512].rearrange("p (k t) -> p k t", k=4),
                                                       func=ACTF.Copy),
                         reads=[("T", tb)], writes=[("o_nsaT", j)])
                defer_pe.append(fin)

            def C_blocks(j):
                i = 4 * st + j
                jsl = slice(j * 128, (j + 1) * 128)
                nct_vis = min(NCT, (8 * i + 6) // 128 + 1)
                for g in range(2):
                    ob = nxt("O", 2)
                    for ct in range(nct_vis):
                        masks = []
                        ut = i - 16 * ct
                        if ut < 17:
                            masks.append((ident[:, :], bc4(wm[:, ut * 128:(ut + 1) * 128]), ["ident", "wm"]))
                        lastb = (ct == nct_vis - 1)
                        attn_block(g, jsl, kcT[0:67, g, ct * 128:(ct + 1) * 128], ["kcT"], masks,
                                   vca[:, ct, g, :], ["vca"], ob, ct == 0, lastb, 128,
                                   done=(lambda j=j, g=g, ob=ob: cmp_done(j, g, ob)) if lastb else None)

            def W_blocks(j):
                i = 4 * st + j
                jsl = slice(j * 128, (j + 1) * 128)
                for g in range(2):
                    ob = nxt("O", 2)
                    kts = list(range(max(0, i - 4), i + 1))
                    for kt in kts:
                        masks = []
                        if kt == i:
                            masks.append((ident[:, :], bc4(caus[:, 0, :]), ["ident", "caus"]))
                        if kt == i - 4:
                            masks.append((ident[:, :], bc4(caus[:, 1, :]), ["ident", "caus"]))
                        lastb = (kt == kts[-1])
                        attn_block(g, jsl, kwT[0:67, g, kt * 128:(kt + 1) * 128], [("kwT", kt), "kwTaug"], masks,
                                   vw[:, kt, g, :], [("vw", kt), "vw1"], ob, kt == kts[0], lastb, 65,
                                   done=(lambda j=j, g=g, ob=ob: win_done(j, g, ob)) if lastb else None)
                    if g == 0:
                        for g2 in range(2):
                            if (i, g2) in sel_done:
                                sel_transpose(j, g2)
                                selT_done[(i, g2)] = True

            def S_blocks(j):
                i = 4 * st + j
                jsl = slice(j * 128, (j + 1) * 128)
                for g in range(2):
                    if (i, g) not in selT_done:
                        if (i, g) not in sel_done:
                            flush_pv()
                        sel_transpose(j, g)
                    if g == 1:
                        while defer_pe:
                            defer_pe.pop(0)()
                    ob = nxt("O", 2)
                    for kt in range(i + 1):
                        if kt == i:
                            masks = [(ident[:, :], bc4(caus[:, 0, :]), ["ident", "caus"])]
                        else:
                            masks = [(cE[:, kt * 128:(kt + 1) * 128], bc4(selneg[:, g, :]), ["cE", "selneg%d" % g])]
                        lastb = (kt == i)
                        attn_block(g, jsl, ksT[0:67, g, kt * 128:(kt + 1) * 128], [("ksT", kt), "ksTaug"], masks,
                                   vs[:, kt, g, :], [("vs", kt), "vs1"], ob, kt == 0, lastb, 65,
                                   done=(lambda j=j, g=g, ob=ob: slc_done(j, g, ob)) if lastb else None)

            C_blocks(0)
            for j in range(4):
                W_blocks(j)
                if j < 3:
                    C_blocks(j + 1)
                S_blocks(j)
            flush_pv()
            onsa_keys = [("o_nsaT", j) for j in range(4)]

            ckpt("D")
            for cc in range(8):
                s, sk = wnext()
                wv = s[:, 0:3072].rearrange("p (k r c) -> p k r c", k=8, r=3)
                gbs = []
                for r in (1, 2, 0):
                    gb = nxt("G", NG)
                    for k in range(KC):
                        T.mm(lambda e, k=k, r=r, gb=gb: e.matmul(Gb[gb][:, :], lhsT=wv[:, k, r, :], rhs=xnT[:, k, :], start=(k == 0), stop=(k == KC - 1)),
                             reads=xnT_keys + [sk], writes=[("G", gb)], inc=(k == KC - 1))
                    if r == 1:
                        T.op("act", lambda e, gb=gb: e.activation(out=cxt[:, :], in_=Gb[gb][:, :], func=ACTF.Copy),
                             reads=[("G", gb)], writes=["cxt"])
                    elif r == 2:
                        T.op("dve", lambda e: e.tensor_copy(out=ubuf[:, 0:2], in_=halo[:, cc, :]), reads=["halo"], writes=["ubufh"])
                        T.op("dve", lambda e, gb=gb: e.tensor_tensor(out=ubuf[:, 2:514], in0=Gb[gb][:, :], in1=cxt[:, :], op=ALU.mult),
                             reads=[("G", gb), "cxt"], writes=["ubuf"])
                        T.op("dve", lambda e: e.tensor_copy(out=halo[:, cc, :], in_=ubuf[:, 512:514]), reads=["ubuf"], writes=["halo"])
                        T.op("dve", lambda e: e.tensor_scalar(out=ybuf[:, :], in0=ubuf[:, 2:514], scalar1=convw[:, cc, 2:3], scalar2=None, op0=ALU.mult),
                             reads=["ubuf", "convw"], writes=["ybuf"])
                        T.op("dve", lambda e: e.scalar_tensor_tensor(out=ybuf[:, :], in0=ubuf[:, 1:513], scalar=convw[:, cc, 1:2], in1=ybuf[:, :],
                                                                     op0=ALU.mult, op1=ALU.add),
                             reads=["ubuf", "ubufh", "convw", "ybuf"], writes=["ybuf"])
                        T.op("dve", lambda e: e.scalar_tensor_tensor(out=ybuf[:, :], in0=ubuf[:, 0:512], scalar=convw[:, cc, 0:1], in1=ybuf[:, :],
                                                                     op0=ALU.mult, op1=ALU.add),
                             reads=["ubuf", "ubufh", "convw", "ybuf"], writes=["ybuf"])
                    else:
                        T.op("dve", lambda e, gb=gb: e.tensor_tensor(out=zT[:, cc, :], in0=Gb[gb][:, :], in1=ybuf[:, :], op=ALU.mult),
                             reads=[("G", gb), "ybuf"], writes=[("zT", cc)])
                if cc == 1:
                    while defer_pe:
                        defer_pe.pop(0)()
            zT_keys = [("zT", cc) for cc in range(8)]

            ckpt("E")
            for m in range(8):
                s, sk = wnext()
                wg = s[:, 0:2048].rearrange("p (k r c) -> p k r c", k=8, r=2)
                wa = s[:, 2048:2560].rearrange("p (k c) -> p k c", k=4)
                wb = s[:, 2560:3584].rearrange("p (k c) -> p k c", k=8)
                for r in range(2):
                    gb = nxt("G", NG)
                    for k in range(KC):
                        T.mm(lambda e, k=k, r=r, gb=gb: e.matmul(Gb[gb][:, :], lhsT=wg[:, k, r, :], rhs=xnT[:, k, :], start=(k == 0), stop=(k == KC - 1)),
                             reads=xnT_keys + [sk], writes=[("G", gb)], inc=(k == KC - 1))
                    T.op("act", lambda e, r=r, gb=gb: e.activation(out=ta[:, r, :], in_=Gb[gb][:, :], func=ACTF.Tanh, scale=0.5),
                         reads=[("G", gb)], writes=[("ta", r)])
                    gb2 = nxt("G", NG)
                    if r == 0:
                        for k in range(4):
                            T.mm(lambda e, k=k, gb2=gb2: e.matmul(Gb[gb2][:, :], lhsT=wa[:, k, :], rhs=o_nsaT[:, k, :], start=(k == 0), stop=(k == 3)),
                                 reads=onsa_keys + [sk], writes=[("G", gb2)], inc=(k == 3))
                    else:
                        for k in range(KC):
                            T.mm(lambda e, k=k, gb2=gb2: e.matmul(Gb[gb2][:, :], lhsT=wb[:, k, :], rhs=zT[:, k, :], start=(k == 0), stop=(k == KC - 1)),
                                 reads=zT_keys + [sk], writes=[("G", gb2)], inc=(k == KC - 1))
                    T.op("dve", lambda e, r=r, gb2=gb2: e.scalar_tensor_tensor(out=mt[:, r, :], in0=ta[:, r, :], scalar=1.0, in1=Gb[gb2][:, :],
                                                                               op0=ALU.add, op1=ALU.mult),
                         reads=[("ta", r), ("G", gb2)], writes=[("mt", r)])
                T.op("dve", lambda e: e.tensor_tensor(out=mixedT[:, m, :], in0=mt[:, 0, :], in1=mt[:, 1, :], op=ALU.add),
                     reads=[("mt", 0), ("mt", 1)], writes=[("mixedT", m)])
            mixed_keys = [("mixedT", m) for m in range(8)]

            ckpt("F")
            s0, sk0 = wnext()
            s1, sk1 = wnext(hold=1)
            wos = [(s0[:, 0:4096].rearrange("p (k c) -> p k c", k=8), sk0), (s1[:, 0:4096].rearrange("p (k c) -> p k c", k=8), sk1)]
            slots_ = {}
            for j in range(4):
                jsl = slice(j * 128, (j + 1) * 128)
                for hh in range(2):
                    wo, sk = wos[hh]
                    gb = nxt("G", NG)
                    for k in range(KC):
                        T.mm(lambda e, k=k, gb=gb: e.matmul(Gb[gb][:, :], lhsT=mixedT[:, k, jsl], rhs=wo[:, k, :], start=(k == 0), stop=(k == KC - 1)),
                             reads=mixed_keys + [sk], writes=[("G", gb)], inc=(k == KC - 1))
                    T.op("dve", lambda e, gb=gb, j=j: e.scalar_tensor_tensor(out=xt[:, j, hh * 512:(hh + 1) * 512], in0=Gb[gb][:, :], scalar=0.5,
                                                                             in1=xt[:, j, hh * 512:(hh + 1) * 512], op0=ALU.mult, op1=ALU.add),
                         reads=[("G", gb), ("xt", j, hh)], writes=[("xt", j, hh)])
                slots_[j] = rms_chain(j, 4 + j)
                if j >= 1:
                    rms_tr(slots_[j - 1], j - 1, g2c)
            rms_tr(slots_[3], 3, g2c)

            ckpt("G")
            ckpt("H")
            for fs in range(8):
                s, sk = wnext()
                wu = s[:, 0:4096].rearrange("p (k c) -> p k c", k=8)
                for fl in range(4):
                    fc = fs * 4 + fl
                    gb = nxt("G", NG)
                    for k in range(KC):
                        T.mm(lambda e, k=k, gb=gb: e.matmul(Gb[gb][:, :], lhsT=wu[:, k, fl * 128:(fl + 1) * 128], rhs=xnT[:, k, :],
                                                            start=(k == 0), stop=(k == KC - 1)),
                             reads=xnT_keys + [sk], writes=[("G", gb)], inc=(k == KC - 1))
                    r = fc % 2
                    T.op("act", lambda e, gb=gb, r=r: e.activation(out=rl[:, r, :], in_=Gb[gb][:, :], func=ACTF.Relu),
                         reads=[("G", gb)], writes=[("rl", r)])
                    T.op("dve", lambda e, r=r, fc=fc: e.tensor_tensor(out=actT[:, fc, :], in0=rl[:, r, :], in1=rl[:, r, :], op=ALU.mult),
                         reads=[("rl", r)], writes=[("actT", fc)] + alias(fc))
            act_keys = [("actT", fc) for fc in range(32)]

            ckpt("I")
            accs = [(Gb[q_], ("G", q_)) for q_ in range(4)]
            for hh in range(2):
                for fq in range(4):
                    s, sk = wnext()
                    wd = s[:, 0:4096].rearrange("p (k c) -> p k c", k=8)
                    for fl in range(8):
                        fc = fq * 8 + fl
                        for j in range(4):
                            jsl = slice(j * 128, (j + 1) * 128)
                            T.mm(lambda e, fl=fl, fc=fc, j=j: e.matmul(accs[j][0][:, :], lhsT=actT[:, fc, jsl], rhs=wd[:, fl, :],
                                                                       start=(fc == 0), stop=(fc == 31)),
                                 reads=[("actT", fc), sk] + alias(fc), writes=[accs[j][1]], inc=(fl == 7 and j == 3) or fc == 31)
                for j in range(4):
                    i = 4 * st + j
                    o_ = nxt("OB", 4)
                    T.op("dve", lambda e, j=j: e.tensor_tensor(out=obuf[:, o_, :], in0=accs[j][0][:, :],
                                                               in1=xt[:, j, hh * 512:(hh + 1) * 512], op=ALU.add),
                         reads=[accs[j][1], ("xt", j, hh)], writes=[("ob", o_)])
                    T.dma("pool", y_d[i * 128:(i + 1) * 128, hh * 512:(hh + 1) * 512], obuf[:, o_, :], reads=[("ob", o_)],
                          semkey=("ob", o_))
                if st + 1 < NST:
                    load_x(st + 1, hh, "sp")

            ckpt("J")

    try:
        ckpt("setup")
        _main_loop()
    except _Stop:
        for sk_, v_ in T.cnt.items():
            if v_ > 0:
                nc.gpsimd.wait_ge(T.sem[sk_], v_)
        dumpable = {"xnT": xnT[:, :, :], "qT": qT, "ksT": ksT[:, :, :], "kwT": kwT[:, :, :], "vs": vs[:, :, :, :],
                    "vw": vw[:, :, :, :], "kcT": kcT[:, :, :], "vca": vca[:, :, :, :], "o_nsaT": o_nsaT, "tg": tg[:, :, :],
                    "zT": zT, "mixedT": mixedT, "Oc": Oc2[0][:, :, :], "Ow": Ow[:, :, :], "Os": Os[:, :, :], "imp": imp2[0][:, :, :],
                    "selb": selb2[0][:, :, :], "score": score2[0][:, :, :], "selneg": selneg[:, :, :], "o_tok": o_tok[:, :],
                    "actT": arena[:, :, :], "kcr": kcr[:, :, :], "sm": sm2[0][:, :], "m8": m82[0][:, :, :]}
        for name in dump:
            ap_ = dumpable[name]
            od = nc.dram_tensor("dbg_" + name, list(ap_.shape), F32, kind="ExternalOutput").ap()
            T.dma("pool", od, ap_, semkey="dbg")
        if dump:
            nc.gpsimd.wait_ge(T.sem["dbg"], T.cnt["dbg"])
        for j in range(4):
            T.dma("pool", y_d[j * 128:(j + 1) * 128, :], xt[:, j, :], reads=[("xt", j, 0), ("xt", j, 1)], semkey=("ob", j % 3))

    for o_ in range(4):
        if ("ob", o_) in T.sem:
            nc.gpsimd.wait_ge(T.sem[("ob", o_)], T.cnt[("ob", o_)])
    build.stats = dict(T.ninstr)
    return nc


def _host_inputs(S, x_b, p):
    d = {"x": np.ascontiguousarray(x_b, dtype=np.float32)}
    d.update(p)
    return d


def _pack_jobs(w_in, w_ba, w_bb, w_out, w_up, w_down, wck, wcv):
    J = np.zeros((39, 128, SLOT), np.float32)

    def kp(a):
        k = a.shape[0] // 128
        return a.reshape(k, 128, a.shape[1]).transpose(1, 0, 2)

    J[0, :, 0:4096] = kp(w_in[:, 0:512]).reshape(128, -1)
    J[1, :, 0:4096] = kp(w_in[:, 512:1024]).reshape(128, -1)
    J[2, :, 0:2240] = kp(w_in[:, 1024:1304]).reshape(128, -1)
    for kv, w in enumerate((wck, wcv)):
        w3 = w.reshape(32, 64, 64)
        blk = np.zeros((128, 32, 128), np.float32)
        for g in range(2):
            blk[g * 64:(g + 1) * 64, :, g * 64:(g + 1) * 64] = w3.transpose(1, 0, 2)
        J[4 - kv, :, 0:4096] = blk.reshape(128, -1)
    for cc in range(8):
        t = np.zeros((128, 8, 3, 128), np.float32)
        for r, c0 in enumerate((C_CB, C_CC, C_CX)):
            t[:, :, r, :] = kp(w_in[:, c0 + cc * 128:c0 + (cc + 1) * 128])
        J[5 + cc, :, 0:3072] = t.reshape(128, -1)
    for m in range(8):
        t = np.zeros((128, 8, 2, 128), np.float32)
        for r, c0 in enumerate((C_GA, C_GB)):
            t[:, :, r, :] = kp(w_in[:, c0 + m * 128:c0 + (m + 1) * 128])
        J[13 + m, :, 0:2048] = t.reshape(128, -1)
        J[13 + m, :, 2048:2560] = kp(w_ba[:, m * 128:(m + 1) * 128]).reshape(128, -1)
        J[13 + m, :, 2560:3584] = kp(w_bb[:, m * 128:(m + 1) * 128]).reshape(128, -1)
    for hh in range(2):
        J[21 + hh, :, 0:4096] = kp(w_out[:, hh * 512:(hh + 1) * 512]).reshape(128, -1)
    for fs in range(8):
        J[23 + fs, :, 0:4096] = kp(w_up[:, fs * 512:(fs + 1) * 512]).reshape(128, -1)
    for hh in range(2):
        for fq in range(4):
            J[31 + hh * 4 + fq, :, 0:4096] = kp(w_down[fq * 1024:(fq + 1) * 1024, hh * 512:(hh + 1) * 512]).reshape(128, -1)
    return J


def _param_layout(norm1_g, w_in, q_norm_g, k_norm_g, cmp_pos_k, cmp_pos_v, w_cmp_k, w_cmp_v, conv_w,
                  w_branch_a, w_branch_b, w_out, norm2_g, w_up, w_down, l=0):
    f = lambda a: np.ascontiguousarray(np.asarray(a), dtype=np.float32)
    p = {
        "wjobs": _pack_jobs(f(w_in[l]), f(w_branch_a[l]), f(w_branch_b[l]), f(w_out[l]), f(w_up[l]), f(w_down[l]),
                            f(w_cmp_k[l]), f(w_cmp_v[l])),
        "w_cmp_k": f(w_cmp_k[l]), "w_cmp_v": f(w_cmp_v[l]),
        "g1c": f(np.asarray(norm1_g[l]).reshape(8, 128).T), "g2c": f(np.asarray(norm2_g[l]).reshape(8, 128).T),
        "gq": f(np.asarray(q_norm_g[l]).reshape(64, 1)), "gk": f(np.asarray(k_norm_g[l]).T),
        "posk": f(np.asarray(cmp_pos_k[l]).reshape(16, 128).T), "posv": f(np.asarray(cmp_pos_v[l]).reshape(16, 128).T),
        "convw": f(np.asarray(conv_w[l]).reshape(3, 8, 128).transpose(2, 1, 0)),
    }
    return p


def kernel(x, norm1_g, w_in, q_norm_g, k_norm_g, cmp_pos_k, cmp_pos_v, w_cmp_k, w_cmp_v, conv_w,
           w_branch_a, w_branch_b, w_out, norm2_g, w_up, w_down):
    x = np.asarray(x)
    B, S, _ = x.shape
    p = _param_layout(norm1_g, w_in, q_norm_g, k_norm_g, cmp_pos_k, cmp_pos_v, w_cmp_k, w_cmp_v, conv_w,
                      w_branch_a, w_branch_b, w_out, norm2_g, w_up, w_down)
    p.update(make_consts(S))
    nc = build(S)
    in_maps = [_host_inputs(S, x[b], p) for b in range(B)]
    res = run_bass_kernel_spmd(nc, in_maps, core_ids=list(range(B)))
    return np.stack([np.asarray(r["y"], dtype=np.float32) for r in res.results], axis=0)
```
